# Optimizing a Trainium2 kernel written in Bass

```python
import jax, jax.numpy as jnp
from jax import lax
import numpy as np

D_MODEL = 1024
BATCH = 32
SEQ = 256
DEPTH = 4
DEC_BATCH = 4
DEC_SEQ = 2048
PAST_LEN = 256

GRID_W = 64
N_MIXERS = 3
D_FF = 4 * D_MODEL
D_RNN = D_MODEL
LRU_BLOCK = 256
N_LRU_BLOCKS = D_RNN // LRU_BLOCK
LRU_CONV_W = 4
LRU_C = 8.0
CONF_CONV_W = 31
NA_HEAD_DIM = 64
NA_HEADS = D_MODEL // NA_HEAD_DIM
NA_WIN_ROWS = 8
NA_WIN_COLS = 16
Q_BLOCK = 128
ATT_SCALE = NA_HEAD_DIM ** -0.5
EPS = 1e-6
N_LRU_LAYERS = (DEPTH + 2) // 3
N_CONV_LAYERS = (DEPTH + 1) // 3
N_NA_LAYERS = DEPTH // 3

kernel_name = 'hybrid_lru_conformer_natten_diffusion_step'

F32 = jnp.float32


def _rmsnorm(x, g=None):
    xf = x.astype(F32)
    y = xf * lax.rsqrt(jnp.mean(xf * xf, axis=-1, keepdims=True) + EPS)
    if g is not None:
        y = y * g.astype(F32)
    return y.astype(x.dtype)


def _layernorm(x, g, b):
    xf = x.astype(F32)
    mu = jnp.mean(xf, axis=-1, keepdims=True)
    xc = xf - mu
    var = jnp.mean(xc * xc, axis=-1, keepdims=True)
    return (xc * lax.rsqrt(var + EPS) * g.astype(F32) + b.astype(F32)).astype(x.dtype)


def _adaln(cond, w, b):
    m = jax.nn.silu(cond) @ w + b
    return [t[:, None, :] for t in jnp.split(m, 6, axis=-1)]


def _modulate(x, shift, scale):
    return _rmsnorm(x) * (1 + scale) + shift


def _dwconv_centred(x, w, b):
    k = w.shape[0]
    lo = (k - 1) // 2
    hi = k - 1 - lo
    y = lax.conv_general_dilated(x, w[:, None, :], window_strides=(1,), padding=[(lo, hi)],
                                 dimension_numbers=('NWC', 'WIO', 'NWC'),
                                 feature_group_count=x.shape[-1])
    return y + b


def _sq_relu_mlp(h, w1, w2):
    return jnp.square(jax.nn.relu(h @ w1)) @ w2


def _block_diag(x, w, b):
    bsz, t, c = x.shape
    xb = x.reshape(bsz, t, N_LRU_BLOCKS, LRU_BLOCK)
    y = jnp.einsum('btni,nij->btnj', xb, w.astype(F32)).reshape(bsz, t, c)
    return y + b.astype(F32)


def _linear_scan(a, bx, h0):
    bx = bx.at[:, 0].add(a[:, 0] * h0)

    def comb(l, r):
        return l[0] * r[0], r[0] * l[1] + r[1]

    _, h = lax.associative_scan(comb, (a, bx), axis=1)
    return h


def _lru_direction(xf, w_a, b_a, w_x, b_x, lam, h0):
    r = jax.nn.sigmoid(_block_diag(xf, w_a, b_a))
    ig = jax.nn.sigmoid(_block_diag(xf, w_x, b_x))
    log_a = -LRU_C * r * jax.nn.softplus(-lam.astype(F32))
    a = jnp.exp(log_a)
    bx = jnp.sqrt(-jnp.expm1(2.0 * log_a)) * (ig * xf)
    return _linear_scan(a, bx, h0.astype(F32))


def _lru_mixer(h, w_in, conv_w, conv_b, w_a, b_a, w_x, b_x, lam, w_out, h0):
    gate_br, rec = jnp.split(h @ w_in, 2, axis=-1)
    xf = _dwconv_centred(rec, conv_w, conv_b).astype(F32)
    hs_f = _lru_direction(xf, w_a[0], b_a[0], w_x[0], b_x[0], lam[0], h0[:, 0])
    hs_b = jnp.flip(_lru_direction(jnp.flip(xf, 1), w_a[1], b_a[1], w_x[1], b_x[1], lam[1], h0[:, 1]), 1)
    final = jnp.stack([hs_f[:, -1], hs_b[:, 0]], axis=1)
    y = (hs_f + hs_b).astype(h.dtype) * jax.nn.gelu(gate_br, approximate=True)
    return y @ w_out, final


def _conformer_conv(h, w_pw1, b_pw1, dw_w, dw_b, ln_g, ln_b, w_pw2, b_pw2):
    val, gate = jnp.split(h @ w_pw1 + b_pw1, 2, axis=-1)
    z = val * jax.nn.sigmoid(gate)
    z = _dwconv_centred(z, dw_w, dw_b)
    z = jax.nn.silu(_layernorm(z, ln_g, ln_b))
    return z @ w_pw2 + b_pw2


def _split_qkv(h, w_qkv):
    bsz, t, _ = h.shape
    qkv = (h @ w_qkv).reshape(bsz, t, 3, NA_HEADS, NA_HEAD_DIM)
    return qkv[:, :, 0], qkv[:, :, 1], qkv[:, :, 2]


def _na_context(h, w_qkv, w_o):
    bsz, L, d = h.shape
    q, k, v = _split_qkv(h, w_qkv)
    nblk = L // Q_BLOCK
    qb = jnp.moveaxis(q.reshape(bsz, nblk, Q_BLOCK, NA_HEADS, NA_HEAD_DIM), 1, 0)

    def one(q_blk):
        s = jnp.einsum('bqhd,bkhd->bhqk', q_blk, k).astype(F32) * ATT_SCALE
        p = jax.nn.softmax(s, axis=-1).astype(v.dtype)
        return jnp.einsum('bhqk,bkhd->bqhd', p, v)

    o = jnp.moveaxis(lax.map(one, qb), 0, 1).reshape(bsz, L, d)
    return o @ w_o, k, v


def _na_latent(h, w_qkv, w_o, rpb, k_ctx, v_ctx):
    bsz, t, d = h.shape
    rows = t // GRID_W
    kr = min(NA_WIN_ROWS, rows)
    kc = NA_WIN_COLS
    q, k, v = _split_qkv(h, w_qkv)
    grid = (bsz, rows, GRID_W, NA_HEADS, NA_HEAD_DIM)
    q, k, v = q.reshape(grid), k.reshape(grid), v.reshape(grid)
    r_start = jnp.clip(jnp.arange(rows) - kr // 2, 0, rows - kr)
    cols = jnp.arange(GRID_W)
    col_win = jnp.clip(cols - kc // 2, 0, GRID_W - kc)[:, None] + jnp.arange(kc)[None, :]
    dc = col_win - cols[:, None] + (NA_WIN_COLS - 1)
    rpb_f = rpb.astype(F32)
    n_loc = kr * kc

    def one_row(args):
        q_r, r = args
        rs = r_start[r]
        k_win = lax.dynamic_slice_in_dim(k, rs, kr, axis=1)[:, :, col_win]
        v_win = lax.dynamic_slice_in_dim(v, rs, kr, axis=1)[:, :, col_win]
        dr = rs + jnp.arange(kr) - r + (NA_WIN_ROWS - 1)
        bias = rpb_f[:, dr[None, :, None], dc[:, None, :]]
        s_loc = jnp.einsum('bqhd,brqchd->bhqrc', q_r, k_win).astype(F32) * ATT_SCALE + bias[None]
        s_ctx = jnp.einsum('bqhd,bkhd->bhqk', q_r, k_ctx).astype(F32) * ATT_SCALE
        s = jnp.concatenate([s_loc.reshape(bsz, NA_HEADS, GRID_W, n_loc), s_ctx], axis=-1)
        p = jax.nn.softmax(s, axis=-1).astype(v.dtype)
        p_loc = p[..., :n_loc].reshape(bsz, NA_HEADS, GRID_W, kr, kc)
        p_ctx = p[..., n_loc:]
        return (jnp.einsum('bhqrc,brqchd->bqhd', p_loc, v_win)
                + jnp.einsum('bhqk,bkhd->bqhd', p_ctx, v_ctx))

    o = lax.map(one_row, (jnp.moveaxis(q, 1, 0), jnp.arange(rows)))
    o = jnp.moveaxis(o, 0, 1).reshape(bsz, t, d)
    return o @ w_o


def _trunk(x, cond, is_ctx, lru_h0, ctx_k, ctx_v, P):
    lru_states, ks, vs = [], [], []
    for i in range(DEPTH):
        sh1, sc1, g1, sh2, sc2, g2 = _adaln(cond, P['w_mod'][i], P['b_mod'][i])
        h = _modulate(x, sh1, sc1)
        kind, j = i % N_MIXERS, i // N_MIXERS
        if kind == 0:
            h0 = jnp.zeros((x.shape[0], 2, D_RNN), F32) if is_ctx else lru_h0[:, j]
            y, h_last = _lru_mixer(h, P['lru_w_in'][j], P['lru_conv_w'][j], P['lru_conv_b'][j],
                                   P['lru_w_a'][j], P['lru_b_a'][j], P['lru_w_x'][j], P['lru_b_x'][j],
                                   P['lru_lambda'][j], P['lru_w_out'][j], h0)
            if is_ctx:
                lru_states.append(h_last.astype(x.dtype))
        elif kind == 1:
            y = _conformer_conv(h, P['conf_w_pw1'][j], P['conf_b_pw1'][j], P['conf_dw_w'][j],
                                P['conf_dw_b'][j], P['conf_ln_g'][j], P['conf_ln_b'][j],
                                P['conf_w_pw2'][j], P['conf_b_pw2'][j])
        else:
            if is_ctx:
                y, k, v = _na_context(h, P['na_w_qkv'][j], P['na_w_o'][j])
                ks.append(k)
                vs.append(v)
            else:
                y = _na_latent(h, P['na_w_qkv'][j], P['na_w_o'][j], P['na_rpb'][j],
                               ctx_k[:, j], ctx_v[:, j])
        x = x + g1 * y
        h = _modulate(x, sh2, sc2)
        x = x + g2 * _sq_relu_mlp(h, P['w_ff1'][i], P['w_ff2'][i])
    return _rmsnorm(x, P['final_g']), lru_states, ks, vs


def _stack_layers(lst, empty_shape, dtype):
    return jnp.stack(lst, axis=1) if lst else jnp.zeros(empty_shape, dtype)


def setup_inputs(seed: int = 0) -> dict:
    key = jax.random.key(seed)
    ks = iter(jax.random.split(key, 48))

    def nrm(shape, s):
        return jax.random.normal(next(ks), shape, F32) * s

    d = D_MODEL
    u = jax.random.uniform(next(ks), (N_LRU_LAYERS, 2, D_RNN), F32, 0.9, 0.999)
    a_base = u ** (1.0 / LRU_C)
    lru_lambda = jnp.log(a_base) - jnp.log1p(-a_base)
    return {
        'x_prompt': nrm((BATCH, SEQ, d), 1.0),
        'x_sample': nrm((DEC_BATCH, DEC_SEQ, d), 1.0),
        'state_lru': nrm((DEC_BATCH, N_LRU_LAYERS, 2, D_RNN), 0.5),
        'cache_k': nrm((DEC_BATCH, N_NA_LAYERS, PAST_LEN, NA_HEADS, NA_HEAD_DIM), 1.0),
        'cache_v': nrm((DEC_BATCH, N_NA_LAYERS, PAST_LEN, NA_HEADS, NA_HEAD_DIM), 1.0),
        'c': nrm((DEC_BATCH, d), 1.0),
        'c_ctx': nrm((d,), 1.0),
        'w_mod': nrm((DEPTH, d, 6 * d), 0.5 * d ** -0.5),
        'b_mod': nrm((DEPTH, 6 * d), 0.02),
        'w_ff1': nrm((DEPTH, d, D_FF), d ** -0.5),
        'w_ff2': nrm((DEPTH, D_FF, d), D_FF ** -0.5),
        'lru_w_in': nrm((N_LRU_LAYERS, d, 2 * D_RNN), d ** -0.5),
        'lru_conv_w': nrm((N_LRU_LAYERS, LRU_CONV_W, D_RNN), LRU_CONV_W ** -0.5),
        'lru_conv_b': nrm((N_LRU_LAYERS, D_RNN), 0.02),
        'lru_w_a': nrm((N_LRU_LAYERS, 2, N_LRU_BLOCKS, LRU_BLOCK, LRU_BLOCK), LRU_BLOCK ** -0.5),
        'lru_b_a': nrm((N_LRU_LAYERS, 2, D_RNN), 0.02),
        'lru_w_x': nrm((N_LRU_LAYERS, 2, N_LRU_BLOCKS, LRU_BLOCK, LRU_BLOCK), LRU_BLOCK ** -0.5),
        'lru_b_x': nrm((N_LRU_LAYERS, 2, D_RNN), 0.02),
        'lru_lambda': lru_lambda,
        'lru_w_out': nrm((N_LRU_LAYERS, D_RNN, d), D_RNN ** -0.5),
        'conf_w_pw1': nrm((N_CONV_LAYERS, d, 2 * d), d ** -0.5),
        'conf_b_pw1': nrm((N_CONV_LAYERS, 2 * d), 0.02),
        'conf_dw_w': nrm((N_CONV_LAYERS, CONF_CONV_W, d), CONF_CONV_W ** -0.5),
        'conf_dw_b': nrm((N_CONV_LAYERS, d), 0.02),
        'conf_ln_g': 1.0 + nrm((N_CONV_LAYERS, d), 0.02),
        'conf_ln_b': nrm((N_CONV_LAYERS, d), 0.02),
        'conf_w_pw2': nrm((N_CONV_LAYERS, d, d), d ** -0.5),
        'conf_b_pw2': nrm((N_CONV_LAYERS, d), 0.02),
        'na_w_qkv': nrm((N_NA_LAYERS, d, 3 * d), d ** -0.5),
        'na_w_o': nrm((N_NA_LAYERS, d, d), d ** -0.5),
        'na_rpb': nrm((N_NA_LAYERS, NA_HEADS, 2 * NA_WIN_ROWS - 1, 2 * NA_WIN_COLS - 1), 0.1),
        'final_g': 1.0 + nrm((d,), 0.02),
    }


def reference(x_prompt, x_sample, state_lru, cache_k, cache_v, c, c_ctx,
              w_mod, b_mod, w_ff1, w_ff2,
              lru_w_in, lru_conv_w, lru_conv_b, lru_w_a, lru_b_a, lru_w_x, lru_b_x, lru_lambda, lru_w_out,
              conf_w_pw1, conf_b_pw1, conf_dw_w, conf_dw_b, conf_ln_g, conf_ln_b, conf_w_pw2, conf_b_pw2,
              na_w_qkv, na_w_o, na_rpb, final_g):
    P = {
        'w_mod': w_mod, 'b_mod': b_mod, 'w_ff1': w_ff1, 'w_ff2': w_ff2,
        'lru_w_in': lru_w_in, 'lru_conv_w': lru_conv_w, 'lru_conv_b': lru_conv_b,
        'lru_w_a': lru_w_a, 'lru_b_a': lru_b_a, 'lru_w_x': lru_w_x, 'lru_b_x': lru_b_x,
        'lru_lambda': lru_lambda, 'lru_w_out': lru_w_out,
        'conf_w_pw1': conf_w_pw1, 'conf_b_pw1': conf_b_pw1, 'conf_dw_w': conf_dw_w,
        'conf_dw_b': conf_dw_b, 'conf_ln_g': conf_ln_g, 'conf_ln_b': conf_ln_b,
        'conf_w_pw2': conf_w_pw2, 'conf_b_pw2': conf_b_pw2,
        'na_w_qkv': na_w_qkv, 'na_w_o': na_w_o, 'na_rpb': na_rpb, 'final_g': final_g,
    }
    y_prompt, lru_list, k_list, v_list = _trunk(x_prompt, c_ctx[None, :], True, None, None, None, P)
    y_sample, _, _, _ = _trunk(x_sample, c, False, state_lru, cache_k, cache_v, P)
    bp, lp = x_prompt.shape[0], x_prompt.shape[1]
    new_state_lru = _stack_layers(lru_list, (bp, 0, 2, D_RNN), x_prompt.dtype)
    new_cache_k = _stack_layers(k_list, (bp, 0, lp, NA_HEADS, NA_HEAD_DIM), x_prompt.dtype)
    new_cache_v = _stack_layers(v_list, (bp, 0, lp, NA_HEADS, NA_HEAD_DIM), x_prompt.dtype)
    return (y_prompt, y_sample, new_state_lru, new_cache_k, new_cache_v)
```

```python
import numpy as np
from contextlib import ExitStack
import concourse.bass as bass
import concourse.mybir as mybir
from concourse.bass_utils import run_bass_kernel_spmd

F32 = mybir.dt.float32
BF16 = mybir.dt.bfloat16
ALU = mybir.AluOpType
AF = mybir.ActivationFunctionType

T = 2048
D = 1024
NCH = 8
TB = 512
NTB = 4
DFF = 4096
EPS = 1e-6
NSEG = 8
SEG = 256
NEG = -30000.0

STOP_AFTER = None
ONLY_LAYER = None
NA_DEBUG = 3
NA_SKIP = set()


class Planner:
    ENG = ('pe', 'act', 'dve', 'pool', 'sp')

    def __init__(self):
        self.dry = False
        self.reset()

    def reset(self):
        self.streams = {e: [] for e in self.ENG}
        self.cnt = {}
        self.seen = {e: {} for e in self.ENG}
        self.lastw = {}
        self.readers = {}
        self.bar = []

    def _deps(self, reads, writes):
        d = []
        for k in reads:
            if k in self.lastw:
                d.append(self.lastw[k])
        for k in writes:
            if k in self.lastw:
                d.append(self.lastw[k])
            r = self.readers.get(k)
            if r:
                d.extend(r.items())
        return d

    def _waits(self, eng, deps):
        need = {}
        seen = self.seen[eng]
        own = 'E_' + eng
        for (sem, v) in deps:
            if sem == own and (eng == 'pe' or v > self.cnt.get(sem, 0)):
                continue
            if seen.get(sem, 0) < v and need.get(sem, 0) < v:
                need[sem] = v
        for sem, v in need.items():
            seen[sem] = v
            self.streams[eng].append(('wait', sem, v))

    def _mark(self, reads, writes, sem, v):
        for k in writes:
            self.lastw[k] = (sem, v)
            self.readers[k] = {}
        for k in reads:
            r = self.readers.setdefault(k, {})
            if r.get(sem, 0) < v:
                r[sem] = v

    def op(self, eng, fn, reads=(), writes=(), inc=True):
        if self.dry:
            return
        self._waits(eng, self._deps(reads, writes))
        sem = 'E_' + eng
        v = self.cnt.get(sem, 0) + 1
        if inc:
            self.cnt[sem] = v
        self.streams[eng].append(('op', fn, sem if inc else None, 1))
        self._mark(reads, writes, sem, v)

    def dma(self, eng, fns, sem, reads=(), writes=(), extra=()):
        if self.dry:
            return
        self._waits(eng, self._deps(reads, writes) + list(extra))
        v = self.cnt.get(sem, 0) + 16 * len(fns)
        self.cnt[sem] = v
        for fn in fns:
            self.streams[eng].append(('op', fn, sem, 16))
        self._mark(reads, writes, sem, v)

    def barrier(self):
        if self.dry:
            return
        cur = [(s, v) for s, v in self.cnt.items() if s.startswith('E_') or s.startswith('ST')]
        for e in ('pe', 'act', 'dve'):
            self._waits(e, cur)
        self.bar = cur


class WRing:
    def __init__(self, p, slots):
        self.p = p
        self.slots = slots
        self.R = len(slots)
        self.reqs = []
        self.n_acq = 0
        self.n_dma = 0

    def start_real(self):
        self.n_acq = 0
        self.n_dma = 0

    def _emit(self):
        n = self.n_dma
        if n >= len(self.reqs):
            return
        self.n_dma += 1
        slot = n % self.R
        pairs = self.reqs[n](self.slots[slot])
        if not pairs:
            return
        fns = []
        for (o, i) in pairs:
            fns.append(lambda e, o=o, i=i: e.dma_start(out=o, in_=i))
        self.p.dma('pool', fns, 'W%d' % slot, reads=(), writes=[('W', slot)])

    def acquire(self, desc):
        if self.p.dry:
            self.reqs.append(desc)
            return len(self.reqs) - 1
        n = self.n_acq
        self.n_acq += 1
        if n == 0:
            for _ in range(self.R):
                self._emit()
        assert self.n_dma > n
        return n

    def key(self, n):
        return ('W', n % self.R)

    def ap(self, n):
        return self.slots[n % self.R]

    def release(self, n):
        if self.p.dry:
            return
        if n + self.R == self.n_dma:
            self._emit()


def _smalls_layout():
    off = {}
    cur = 0

    def add(name, n):
        nonlocal cur
        off[name] = (cur, n)
        cur += n
    add('cond', 8)
    add('bmod', 4 * 48)
    add('lru_cw', 2 * 8 * 4)
    add('lru_cb', 2 * 8)
    add('lru_ba', 2 * 2 * 8)
    add('lru_bx', 2 * 2 * 8)
    add('lru_lam', 2 * 2 * 8)
    add('h0', 2 * 2 * 8)
    add('cf_b1', 16)
    add('cf_dw', 8 * 31)
    add('cf_db', 8)
    add('cf_lg', 8)
    add('cf_lb', 8)
    add('cf_b2', 8)
    add('fin_g', 8)
    add('m', 8)
    add('colb', 16 * 8)
    add('ident', 128)
    add('eps', 8)
    add('one', 8)
    add('q25', 8)
    return off, cur


SM_OFF, SM_N = _smalls_layout()


def _fm(v):
    v = np.asarray(v, np.float32)
    lead = v.shape[:-1]
    a = v.reshape(lead + (8, 128))
    a = np.moveaxis(a, -1, 0)
    return np.ascontiguousarray(a).reshape(128, -1)


def build_program():
    nc = bass.Bass("TRN2", target_bir_lowering=False)
    dt_in = {}

    def din(name, shape):
        dt_in[name] = nc.dram_tensor(name, list(shape), F32, kind="ExternalInput").ap()
        return dt_in[name]

    def dout(name, shape):
        return nc.dram_tensor(name, list(shape), F32, kind="ExternalOutput").ap()

    xin = din('xin', [D, T])
    smalls_d = din('smalls', [128, SM_N])
    w_mod = din('w_mod', [4, D, 6 * D])
    w_ff1 = din('w_ff1', [4, D, DFF])
    w_ff2 = din('w_ff2', [4, DFF, D])
    lru_w_in = din('lru_w_in', [2, D, 2 * D])
    lru_w_a = din('lru_w_a', [2, 2, 4, 256, 256])
    lru_w_x = din('lru_w_x', [2, 2, 4, 256, 256])
    lru_w_out = din('lru_w_out', [2, D, D])
    st_out = dout('st_out', [128, 256])
    conf_w_pw1 = din('conf_w_pw1', [1, D, 2 * D])
    conf_w_pw2 = din('conf_w_pw2', [1, D, D])
    na_w_qkv = din('na_w_qkv', [1, D, 3 * D])
    na_w_o = din('na_w_o', [1, D, D])
    fbias = din('fbias', [8, 128, 2304])
    ctxk_d = din('ctxk', [D, 256])
    ctxv_d = din('ctxv', [256, D])
    k_out = dout('k_out', [D, T])
    v_out = dout('v_out', [T, D])
    y_out = dout('y_out', [D, T])

    p = Planner()
    es = ExitStack()
    with es:
        def sb(name, shape, dt):
            return es.enter_context(nc.sbuf_tensor(name, list(shape), dt))

        xT = sb('xT', [128, NCH, T], F32)
        h = sb('h', [128, NCH, T], BF16)
        wr = [sb('wr%d' % i, [128, 4096], BF16) for i in range(4)]
        sm = sb('smalls_sb', [128, SM_N], F32)
        mod = sb('mod', [128, 4, 48], F32)
        ones = sb('ones', [128, 128], BF16)
        scb = sb('scb', [128, 8], BF16)
        SCR = 18752
        stt = sb('stt', [128, 256], F32)
        cch = sb('cch', [128, 2, 32], F32)
        hbias = sb('hbias', [128, 2, 32], F32)
        scr = sb('scr', [128, SCR], F32)
        ps = es.enter_context(nc.psum_tensor('ps', [128, 8, 512], F32))

        ring = WRing(p, [w[:] for w in wr])

        def S(name, idx=None):
            o, n = SM_OFF[name]
            if idx is None:
                return sm[:, o:o + n]
            return sm[:, o + idx:o + idx + 1]

        bank_ctr = [0]

        def nbank():
            b = bank_ctr[0] % 8
            bank_ctr[0] += 1
            return b

        def XK(c, tb):
            return ('x', c, tb)

        def HK(c, tb):
            return ('h', c, tb)

        def setup():
            xv = xin.rearrange("(c p) t -> p c t", p=128)
            fns = []
            for c in range(NCH):
                fns.append(lambda e, c=c: e.dma_start(out=xT[:, c, :], in_=xv[:, c, :]))
            p.dma('sp', fns, 'LDX', writes=[XK(c, tb) for c in range(NCH) for tb in range(NTB)])
            p.dma('sp', [lambda e: e.dma_start(out=sm[:], in_=smalls_d[:, :])], 'LD0', writes=['sm'])
            p.op('dve', lambda e: e.memset(ones[:], 1.0), writes=['ones'])
            p.op('act', lambda e: e.activation(out=scb[:], in_=S('cond'), func=AF.Silu),
                 reads=['sm'], writes=['scb'])

        def mod_tile(l, cg, bank):
            n = ring.acquire(lambda slot, cg=cg: [(
                slot[:, 0:4096].rearrange("p (k n) -> p k n", k=8),
                w_mod[l, :, cg * 512:(cg + 1) * 512].rearrange("(k p) n -> p k n", p=128))])
            if not p.dry:
                wt = ring.ap(n).rearrange("p (k n) -> p k n", k=8)
                for j in range(4):
                    col = cg * 4 + j
                    for k in range(8):
                        p.op('pe', lambda e, wt=wt, j=j, k=k, col=col: e.matmul(
                            ps[:, bank, col:col + 1], wt[:, k, j * 128:(j + 1) * 128], scb[:, k:k + 1],
                            start=(k == 0), stop=(k == 7)),
                            reads=[ring.key(n), 'scb'], writes=[('ps', bank)], inc=(k == 7))
            ring.release(n)

        def mod_finish(l, bank):
            o, _ = SM_OFF['bmod']
            p.op('dve', lambda e: e.tensor_tensor(out=mod[:, l, :], in0=ps[:, bank, 0:48],
                                                   in1=sm[:, o + l * 48:o + (l + 1) * 48], op=ALU.add),
                 reads=[('ps', bank), 'sm'], writes=[('mod', l)])
            for a in (8, 32):
                p.op('dve', lambda e, a=a: e.tensor_scalar(out=mod[:, l, a:a + 8], in0=mod[:, l, a:a + 8],
                                                           scalar1=1.0, scalar2=None, op0=ALU.add),
                     reads=[('mod', l)], writes=[('mod', l)])

        def modulation(l):
            bank = nbank()
            for cg in range(12):
                mod_tile(l, cg, bank)
            mod_finish(l, bank)

        def scr_f32(off, n):
            return scr[:, off:off + n]

        def scr_bf16(off, n):
            return scr[:, off:off + n // 2].bitcast(BF16)

        def rms_stats(tb, sq, rstd):
            bank = nbank()
            for c in range(NCH):
                p.op('act', lambda e, c=c: e.activation(out=sq[:, c, :], in_=xT[:, c, tb * TB:(tb + 1) * TB],
                                                       func=AF.Square),
                     reads=[XK(c, tb)], writes=[('sq', c)])
            for c in range(NCH):
                p.op('pe', lambda e, c=c: e.matmul(ps[:, bank, :], ones[:], sq[:, c, :],
                                                  start=(c == 0), stop=(c == NCH - 1)),
                     reads=['ones', ('sq', c)], writes=[('ps', bank)], inc=(c == NCH - 1))
            p.op('act', lambda e: e.activation(out=rstd, in_=ps[:, bank, :], func=AF.Sqrt,
                                               bias=S('eps', 0), scale=1.0 / D),
                 reads=[('ps', bank), 'sm'], writes=['rstd'])
            p.op('dve', lambda e: e.reciprocal(out=rstd, in_=rstd), reads=['rstd'], writes=['rstd'])

        def norm_mod(l, a):
            sq = scr_bf16(0, 8 * 512).rearrange("p (c n) -> p c n", c=8)
            rstd = scr_f32(2048, 512)
            tmps = [scr_f32(2560, 512), scr_f32(3072, 512)]
            for tb in range(NTB):
                rms_stats(tb, sq, rstd)
                for c in range(NCH):
                    tmp = tmps[c % 2]
                    tk = ('tmp', c % 2)
                    p.op('dve', lambda e, c=c, tmp=tmp, tb=tb: e.tensor_tensor(
                        out=tmp, in0=xT[:, c, tb * TB:(tb + 1) * TB], in1=rstd, op=ALU.mult),
                        reads=[XK(c, tb), 'rstd'], writes=[tk])
                    p.op('act', lambda e, c=c, tmp=tmp, tb=tb: e.activation(
                        out=h[:, c, tb * TB:(tb + 1) * TB], in_=tmp, func=AF.Identity,
                        bias=mod[:, l, a + c:a + c + 1], scale=mod[:, l, a + 8 + c:a + 9 + c]),
                        reads=[tk, ('mod', l)], writes=[HK(c, tb)])

        def ffn(l):
            G = 4
            fctr = [0]

            def fbank():
                b = fctr[0] % 7
                fctr[0] += 1
                return b
            mod_todo = list(range(12)) if l + 1 < 4 else []
            hid = scr_bf16(0, G * T).rearrange("p (c n) -> p c n", c=G)
            rl = [scr_f32(4096, 512), scr_f32(4608, 512)]
            for g in range(DFF // (G * 128)):
                n1 = ring.acquire(lambda slot, g=g: [(
                    slot[:, 0:4096].rearrange("p (k n) -> p k n", k=8),
                    w_ff1[l, :, g * 512:(g + 1) * 512].rearrange("(k p) n -> p k n", p=128))])
                n2 = ring.acquire(lambda slot, g=g: [(
                    slot[:, 0:4096].rearrange("p (k n) -> p k n", k=4),
                    w_ff2[l, g * 512:(g + 1) * 512, :].rearrange("(k p) n -> p k n", p=128))])
                if not p.dry:
                    w1 = ring.ap(n1).rearrange("p (k n) -> p k n", k=8)
                    w2 = ring.ap(n2).rearrange("p (k n) -> p k n", k=4)
                    it = 0
                    for tb in range(NTB):
                        for c in range(G):
                            bank = fbank()
                            for k in range(NCH):
                                p.op('pe', lambda e, c=c, k=k, bank=bank, tb=tb, w1=w1: e.matmul(
                                    ps[:, bank, :], w1[:, k, c * 128:(c + 1) * 128], h[:, k, tb * TB:(tb + 1) * TB],
                                    start=(k == 0), stop=(k == NCH - 1)),
                                    reads=[ring.key(n1), HK(k, tb)], writes=[('ps', bank)], inc=(k == NCH - 1))
                            r = rl[it % 2]
                            rk = ('rl', it % 2)
                            it += 1
                            p.op('act', lambda e, r=r, bank=bank: e.activation(out=r, in_=ps[:, bank, :], func=AF.Relu),
                                 reads=[('ps', bank)], writes=[rk])
                            p.op('dve', lambda e, r=r, c=c, tb=tb: e.tensor_tensor(
                                out=hid[:, c, tb * TB:(tb + 1) * TB], in0=r, in1=r, op=ALU.mult),
                                reads=[rk], writes=[('hid', c, tb)])
                    ring.release(n1)
                    for tb in range(NTB):
                        for oc in range(NCH):
                            bank = fbank()
                            for k in range(G):
                                p.op('pe', lambda e, oc=oc, k=k, bank=bank, tb=tb, w2=w2: e.matmul(
                                    ps[:, bank, :], w2[:, k, oc * 128:(oc + 1) * 128], hid[:, k, tb * TB:(tb + 1) * TB],
                                    start=(k == 0), stop=(k == G - 1)),
                                    reads=[ring.key(n2), ('hid', k, tb)], writes=[('ps', bank)], inc=(k == G - 1))
                            p.op('dve', lambda e, oc=oc, bank=bank, tb=tb: e.scalar_tensor_tensor(
                                out=xT[:, oc, tb * TB:(tb + 1) * TB], in0=ps[:, bank, :],
                                scalar=mod[:, l, 40 + oc:41 + oc], in1=xT[:, oc, tb * TB:(tb + 1) * TB],
                                op0=ALU.mult, op1=ALU.add),
                                reads=[('ps', bank), XK(oc, tb), ('mod', l)], writes=[XK(oc, tb)])
                    ring.release(n2)
                else:
                    ring.release(n1)
                    ring.release(n2)
                for _ in range(2 if g % 2 == 0 else 1):
                    if mod_todo:
                        mod_tile(l + 1, mod_todo.pop(0), 7)
            if l + 1 < 4:
                mod_finish(l + 1, 7)


        def lru_consts():
            o, n = SM_OFF['lru_lam']
            p.op('act', lambda e: e.activation(out=cch[:, 0, :], in_=sm[:, o:o + n], func=AF.Exp, scale=-1.0),
                 reads=['sm'], writes=['cch'])
            p.op('act', lambda e: e.activation(out=cch[:, 0, :], in_=cch[:, 0, :], func=AF.Ln, bias=S('one', 0)),
                 reads=['cch', 'sm'], writes=['cch'])
            p.op('dve', lambda e: e.tensor_scalar(out=cch[:, 1, :], in0=cch[:, 0, :], scalar1=-8.0, scalar2=None,
                                                  op0=ALU.mult), reads=['cch'], writes=['cch'])
            p.op('dve', lambda e: e.tensor_scalar(out=cch[:, 0, :], in0=cch[:, 0, :], scalar1=-4.0, scalar2=None,
                                                  op0=ALU.mult), reads=['cch'], writes=['cch'])
            oa, na = SM_OFF['lru_ba']
            ox, nx = SM_OFF['lru_bx']
            p.op('dve', lambda e: e.tensor_scalar(out=hbias[:, 0, :], in0=sm[:, oa:oa + na], scalar1=0.5, scalar2=None,
                                                  op0=ALU.mult), reads=['sm'], writes=['hbias'])
            p.op('dve', lambda e: e.tensor_scalar(out=hbias[:, 1, :], in0=sm[:, ox:ox + nx], scalar1=0.5, scalar2=None,
                                                  op0=ALU.mult), reads=['sm'], writes=['hbias'])

        def lru_mixer(l, j):
            PADW = 259
            recp = scr[:, 0:2 * 8 * PADW].rearrange("p (c s w) -> p c s w", c=2, s=8)
            A = scr[:, 0:2048]
            B = scr[:, 2072:2072 + 2048]
            xf = scr[:, 4144:4144 + 4096].rearrange("p (c n) -> p c n", c=2)
            xfb = scr[:, 8240:8240 + 2048].bitcast(BF16).rearrange("p (c n) -> p c n", c=2)
            yb = scr[:, 10288:10288 + 2048].bitcast(BF16).rearrange("p (c n) -> p c n", c=2)
            C = scr[:, 12336:12336 + 2048]
            Hd = [scr[:, 14384:14384 + 2048], scr[:, 16432:16432 + 2048]]
            m_ap = S('m', 0)
            hctr = [0]
            C = scr[:, 12336:12336 + 2048]
            def block(n):
                nin = ring.acquire(lambda slot, n=n: [
                    (slot[:, 0:4096].rearrange("p (k n) -> p k n", k=8)[:, :, 0:256],
                     lru_w_in[j, :, n * 256:(n + 1) * 256].rearrange("(k p) n -> p k n", p=128)),
                    (slot[:, 0:4096].rearrange("p (k n) -> p k n", k=8)[:, :, 256:512],
                     lru_w_in[j, :, D + n * 256:D + (n + 1) * 256].rearrange("(k p) n -> p k n", p=128))])
                nax = ring.acquire(lambda slot, n=n: [
                    (slot[:, 0:2048].rearrange("p (k m n) -> p k m n", k=2, m=4)[:, :, 2 * d + w, :],
                     (lru_w_a if w == 0 else lru_w_x)[j, d, n].rearrange("(k p) n -> p k n", p=128))
                    for d in range(2) for w in range(2)])
                nout = ring.acquire(lambda slot, n=n: [(
                    slot[:, 0:2048].rearrange("p (k n) -> p k n", k=2),
                    lru_w_out[j, n * 256:(n + 1) * 256, :].rearrange("(k p) n -> p k n", p=128))])
                if p.dry:
                    yield
                    yield
                    return
                win = ring.ap(nin).rearrange("p (k n) -> p k n", k=8)
                wax = ring.ap(nax)[:, 0:2048].rearrange("p (k m n) -> p k m n", k=2, m=4)
                wout = ring.ap(nout)[:, 0:2048].rearrange("p (k n) -> p k n", k=2)
                for cc in range(2):
                    for tb in range(NTB):
                        bank = nbank()
                        for k in range(NCH):
                            p.op('pe', lambda e, cc=cc, k=k, bank=bank, tb=tb, win=win: e.matmul(
                                ps[:, bank, :], win[:, k, 256 + cc * 128:256 + (cc + 1) * 128],
                                h[:, k, tb * TB:(tb + 1) * TB], start=(k == 0), stop=(k == NCH - 1)),
                                reads=[ring.key(nin), HK(k, tb)], writes=[('ps', bank)], inc=(k == NCH - 1))
                        p.op('act', lambda e, cc=cc, bank=bank, tb=tb: e.activation(
                            out=recp[:, cc, 2 * tb:2 * tb + 2, 1:257],
                            in_=ps[:, bank, :].rearrange("p (s w) -> p s w", s=2), func=AF.Copy),
                            reads=[('ps', bank)],
                            writes=[('recp', cc), ('A', 0), ('A', 1)] if cc == 0 else [('recp', cc), ('B', 0), ('B', 1)])
                    p.op('dve', lambda e, cc=cc: e.memset(recp[:, cc, 0, 0:1], 0.0), writes=[('recp', cc)])
                    p.op('dve', lambda e, cc=cc: e.memset(recp[:, cc, 7, 257:259], 0.0), writes=[('recp', cc)])
                    p.op('dve', lambda e, cc=cc: e.tensor_scalar(
                        out=recp[:, cc, 1:8, 0:1], in0=recp[:, cc, 0:7, 256:257], scalar1=m_ap, scalar2=None,
                        op0=ALU.mult), reads=['sm', ('recp', cc)], writes=[('recp', cc)])
                    p.op('dve', lambda e, cc=cc: e.tensor_scalar(
                        out=recp[:, cc, 0:7, 257:259], in0=recp[:, cc, 1:8, 1:3], scalar1=m_ap, scalar2=None,
                        op0=ALU.mult), reads=['sm', ('recp', cc)], writes=[('recp', cc)])
                    ch = 2 * n + cc
                    xfv = xf[:, cc, :].rearrange("p (s w) -> p s w", s=8)
                    ocw, _ = SM_OFF['lru_cw']
                    ocb, _ = SM_OFF['lru_cb']
                    wbase = ocw + (j * 8 + ch) * 4
                    p.op('dve', lambda e, cc=cc, xfv=xfv, wbase=wbase, ch=ch: e.tensor_scalar(
                        out=xfv, in0=recp[:, cc, :, 0:256], scalar1=sm[:, wbase:wbase + 1],
                        scalar2=sm[:, ocb + j * 8 + ch:ocb + j * 8 + ch + 1], op0=ALU.mult, op1=ALU.add),
                        reads=['sm', ('recp', cc)], writes=[('xf', cc)])
                    for k in range(1, 4):
                        p.op('dve', lambda e, cc=cc, xfv=xfv, wbase=wbase, k=k: e.scalar_tensor_tensor(
                            out=xfv, in0=recp[:, cc, :, k:k + 256], scalar=sm[:, wbase + k:wbase + k + 1], in1=xfv,
                            op0=ALU.mult, op1=ALU.add),
                            reads=['sm', ('recp', cc), ('xf', cc)], writes=[('xf', cc)])
                    p.op('act', lambda e, cc=cc: e.activation(out=xfb[:, cc, :], in_=xf[:, cc, :], func=AF.Copy),
                         reads=[('xf', cc)], writes=[('xfb', cc)])
                yield
                oba, _ = SM_OFF['lru_ba']
                obx, _ = SM_OFF['lru_bx']
                oh0, _ = SM_OFF['h0']
                HT = T // 2
                Aset = [scr[:, 0:1024], scr[:, 1024:2048]]
                Bset = [scr[:, 2072:3096], scr[:, 3096:4120]]
                Cset = [scr[:, 12336:13360], scr[:, 13360:14384]]
                for co in range(2):
                    ch = 2 * n + co
                    for d in range(2):
                        idx = (j * 2 + d) * 8 + ch
                        for hs in range(2):
                            Ah, Bh, Ch = Aset[hs], Bset[hs], Cset[hs]
                            AKs, BKs, CKs = ('A', hs), ('B', hs), ('C', hs)
                            if d == 0:
                                tbs = [2 * hs, 2 * hs + 1]
                            else:
                                tbs = [2 * (1 - hs) + 1, 2 * (1 - hs)]

                            def dst(buf, tb, d=d, hs=hs):
                                if d == 0:
                                    o_ = (tb - 2 * hs) * TB
                                    return buf[:, o_:o_ + TB]
                                hi = T - 1 - tb * TB - hs * HT
                                lo = hi - TB
                                return buf[:, hi:lo:-1] if lo >= 0 else buf[:, hi::-1]
                            for (w, buf, keys) in ((0, Ah, [AKs, ('recp', 0)]), (1, Bh, [BKs, ('recp', 1)])):
                                for tb in tbs:
                                    bank = nbank()
                                    for k in range(2):
                                        p.op('pe', lambda e, k=k, bank=bank, tb=tb, wax=wax, d=d, w=w, co=co: e.matmul(
                                            ps[:, bank, :], wax[:, k, 2 * d + w, co * 128:(co + 1) * 128],
                                            xfb[:, k, tb * TB:(tb + 1) * TB], start=(k == 0), stop=(k == 1)),
                                            reads=[ring.key(nax), ('xfb', k)], writes=[('ps', bank)], inc=(k == 1))
                                    p.op('act', lambda e, bank=bank, o=dst(buf, tb), w=w, idx=idx: e.activation(
                                        out=o, in_=ps[:, bank, :], func=AF.Tanh, bias=hbias[:, w, idx:idx + 1], scale=0.5),
                                        reads=[('ps', bank), 'hbias'], writes=keys)
                            p.op('act', lambda e, idx=idx, Ah=Ah, Ch=Ch: e.activation(
                                out=Ch, in_=Ah, func=AF.Exp, scale=cch[:, 1, idx:idx + 1], bias=cch[:, 1, idx:idx + 1]),
                                reads=[AKs, 'cch'], writes=[CKs])
                            p.op('act', lambda e, idx=idx, Ah=Ah: e.activation(
                                out=Ah, in_=Ah, func=AF.Exp, scale=cch[:, 0, idx:idx + 1], bias=cch[:, 0, idx:idx + 1]),
                                reads=[AKs, 'cch'], writes=[AKs])
                            if d == 0:
                                xsrc = xf[:, co, hs * HT:(hs + 1) * HT]
                            else:
                                hi = (2 - hs) * HT - 1
                                lo = (1 - hs) * HT - 1
                                xsrc = xf[:, co, hi:lo:-1] if lo >= 0 else xf[:, co, hi::-1]
                            p.op('dve', lambda e, xsrc=xsrc, Bh=Bh: e.scalar_tensor_tensor(
                                out=Bh, in0=Bh, scalar=1.0, in1=xsrc, op0=ALU.add, op1=ALU.mult),
                                reads=[BKs, ('xf', co)], writes=[BKs])
                        for hs in range(2):
                            Ch = Cset[hs]
                            p.op('act', lambda e, Ch=Ch: e.activation(out=Ch, in_=Ch, func=AF.Sqrt, bias=S('q25', 0),
                                                                      scale=-0.25),
                                 reads=[('C', hs), 'sm'], writes=[('C', hs)])
                        for hs in range(2):
                            Ah, Bh, Ch = Aset[hs], Bset[hs], Cset[hs]
                            AKs, BKs, CKs = ('A', hs), ('B', hs), ('C', hs)
                            p.op('dve', lambda e, Bh=Bh, Ch=Ch: e.tensor_tensor(out=Bh, in0=Bh, in1=Ch, op=ALU.mult),
                                 reads=[BKs, CKs], writes=[BKs])
                            p.op('dve', lambda e, Ah=Ah: e.tensor_scalar(
                                out=Ah[:, 0:HT:SEG], in0=Ah[:, 0:HT:SEG], scalar1=m_ap, scalar2=None, op0=ALU.mult),
                                reads=[AKs, 'sm'], writes=[AKs])
                            init = sm[:, oh0 + idx:oh0 + idx + 1] if hs == 0 else Hd[d][:, HT - 1:HT]
                            p.op('dve', lambda e, d=d, hs=hs, Ah=Ah, Bh=Bh, init=init: e.tensor_tensor_scan(
                                out=Hd[d][:, hs * HT:(hs + 1) * HT], data0=Ah, data1=Bh, initial=init,
                                op0=ALU.mult, op1=ALU.add), reads=[AKs, BKs, 'sm', ('H', d)], writes=[('H', d)])
                        so = ((j * 2 + d) * 8 + ch) * 8
                        p.op('dve', lambda e, d=d, so=so: e.tensor_copy(out=stt[:, so:so + 8],
                                                                        in_=Hd[d][:, SEG - 1:T:SEG]),
                             reads=[('H', d)], writes=['stt'])
                    for tb in range(NTB):
                        bank = nbank()
                        for k in range(NCH):
                            p.op('pe', lambda e, k=k, bank=bank, tb=tb, win=win, co=co: e.matmul(
                                ps[:, bank, :], win[:, k, co * 128:(co + 1) * 128],
                                h[:, k, tb * TB:(tb + 1) * TB], start=(k == 0), stop=(k == NCH - 1)),
                                reads=[ring.key(nin), HK(k, tb)], writes=[('ps', bank)], inc=(k == NCH - 1))
                        p.op('act', lambda e, bank=bank, tb=tb: e.activation(
                            out=C[:, tb * TB:(tb + 1) * TB], in_=ps[:, bank, :], func=AF.Gelu_apprx_tanh),
                            reads=[('ps', bank)], writes=[('C', 0), ('C', 1)])
                    p.op('dve', lambda e: e.tensor_tensor(out=Hd[0], in0=Hd[0], in1=Hd[1][:, ::-1], op=ALU.add),
                         reads=[('H', 0), ('H', 1)], writes=[('H', 0)])
                    p.op('dve', lambda e, co=co: e.tensor_tensor(out=yb[:, co, :], in0=Hd[0], in1=C, op=ALU.mult),
                         reads=[('H', 0), ('C', 0), ('C', 1)], writes=[('yb', co)])
                ring.release(nin)
                ring.release(nax)
                yield
                for tb in range(NTB):
                    for oc in range(NCH):
                        bank = nbank()
                        for k in range(2):
                            p.op('pe', lambda e, oc=oc, k=k, bank=bank, tb=tb, wout=wout: e.matmul(
                                ps[:, bank, :], wout[:, k, oc * 128:(oc + 1) * 128], yb[:, k, tb * TB:(tb + 1) * TB],
                                start=(k == 0), stop=(k == 1)),
                                reads=[ring.key(nout), ('yb', k)], writes=[('ps', bank)], inc=(k == 1))
                        p.op('dve', lambda e, oc=oc, bank=bank, tb=tb: e.scalar_tensor_tensor(
                            out=xT[:, oc, tb * TB:(tb + 1) * TB], in0=ps[:, bank, :],
                            scalar=mod[:, l, 16 + oc:17 + oc], in1=xT[:, oc, tb * TB:(tb + 1) * TB],
                            op0=ALU.mult, op1=ALU.add),
                            reads=[('ps', bank), XK(oc, tb), ('mod', l)], writes=[XK(oc, tb)])
                ring.release(nout)

            def finish(g):
                for _ in g:
                    pass
            gens = [block(n) for n in range(4)]
            next(gens[0])
            next(gens[0])
            for n in range(1, 4):
                next(gens[n])
                finish(gens[n - 1])
                next(gens[n])
            finish(gens[3])


        def conf_mixer(l):
            PW = 286
            zp = scr[:, 0:9152].bitcast(BF16).rearrange("p (c s w) -> p c s w", c=8, s=8)
            hflat = h[:].rearrange("p c n -> p (c n)")

            def zc(c):
                if c < 4:
                    return hflat[:, c * 4096:(c + 1) * 4096].bitcast(F32)
                return scr[:, 9152 + (c - 4) * 2048:9152 + (c - 3) * 2048]

            def zs(c):
                if c < 4:
                    return hflat[:, c * 4096:c * 4096 + 2048]
                return scr[:, 9152 + (c - 4) * 2048:9152 + (c - 4) * 2048 + 1024].bitcast(BF16)
            sg = [scr[:, 17344:17344 + 512], scr[:, 17856:17856 + 512]]
            m_ap = S('m', 0)
            ob1, _ = SM_OFF['cf_b1']
            it = 0
            for g in range(4):
                n1 = ring.acquire(lambda slot, g=g: [
                    (slot[:, 0:4096].rearrange("p (k n) -> p k n", k=8)[:, :, 0:256],
                     conf_w_pw1[0, :, g * 256:(g + 1) * 256].rearrange("(k p) n -> p k n", p=128)),
                    (slot[:, 0:4096].rearrange("p (k n) -> p k n", k=8)[:, :, 256:512],
                     conf_w_pw1[0, :, D + g * 256:D + (g + 1) * 256].rearrange("(k p) n -> p k n", p=128))])
                if p.dry:
                    ring.release(n1)
                    continue
                w1 = ring.ap(n1).rearrange("p (k n) -> p k n", k=8)
                for cc in range(2):
                    c = 2 * g + cc
                    for tb in range(NTB):
                        bv = nbank()
                        for k in range(NCH):
                            p.op('pe', lambda e, cc=cc, k=k, bv=bv, tb=tb, w1=w1: e.matmul(
                                ps[:, bv, :], w1[:, k, cc * 128:(cc + 1) * 128], h[:, k, tb * TB:(tb + 1) * TB],
                                start=(k == 0), stop=(k == NCH - 1)),
                                reads=[ring.key(n1), HK(k, tb)], writes=[('ps', bv)], inc=(k == NCH - 1))
                        bg = nbank()
                        for k in range(NCH):
                            p.op('pe', lambda e, cc=cc, k=k, bg=bg, tb=tb, w1=w1: e.matmul(
                                ps[:, bg, :], w1[:, k, 256 + cc * 128:256 + (cc + 1) * 128],
                                h[:, k, tb * TB:(tb + 1) * TB], start=(k == 0), stop=(k == NCH - 1)),
                                reads=[ring.key(n1), HK(k, tb)], writes=[('ps', bg)], inc=(k == NCH - 1))
                        sgt = sg[it % 2]
                        sgk = ('sg', it % 2)
                        it += 1
                        p.op('act', lambda e, bg=bg, sgt=sgt, c=c: e.activation(
                            out=sgt, in_=ps[:, bg, :], func=AF.Sigmoid, bias=sm[:, ob1 + 8 + c:ob1 + 9 + c]),
                            reads=[('ps', bg), 'sm'], writes=[sgk])
                        p.op('dve', lambda e, bv=bv, sgt=sgt, c=c, tb=tb: e.scalar_tensor_tensor(
                            out=zp[:, c, 2 * tb:2 * tb + 2, 15:271],
                            in0=ps[:, bv, :].rearrange("p (s w) -> p s w", s=2), scalar=sm[:, ob1 + c:ob1 + c + 1],
                            in1=sgt.rearrange("p (s w) -> p s w", s=2), op0=ALU.add, op1=ALU.mult),
                            reads=[('ps', bv), sgk, 'sm'], writes=[('zp', c)])
                    p.op('dve', lambda e, c=c: e.memset(zp[:, c, 0, 0:15], 0.0), writes=[('zp', c)])
                    p.op('dve', lambda e, c=c: e.memset(zp[:, c, 7, 271:286], 0.0), writes=[('zp', c)])
                    p.op('dve', lambda e, c=c: e.tensor_scalar(
                        out=zp[:, c, 1:8, 0:15], in0=zp[:, c, 0:7, 256:271], scalar1=m_ap, scalar2=None,
                        op0=ALU.mult), reads=['sm', ('zp', c)], writes=[('zp', c)])
                    p.op('dve', lambda e, c=c: e.tensor_scalar(
                        out=zp[:, c, 0:7, 271:286], in0=zp[:, c, 1:8, 15:30], scalar1=m_ap, scalar2=None,
                        op0=ALU.mult), reads=['sm', ('zp', c)], writes=[('zp', c)])
                ring.release(n1)
            p.barrier()
            odw, _ = SM_OFF['cf_dw']
            odb, _ = SM_OFF['cf_db']
            oid, _ = SM_OFF['ident']
            for c in range(NCH):
                nd = ring.acquire(lambda slot: [])
                if p.dry:
                    ring.release(nd)
                    continue
                dg = ring.ap(nd)[:, 0:31 * 128].rearrange("p (k n) -> p k n", k=31)
                for k in range(31):
                    p.op('dve', lambda e, k=k, dg=dg, c=c: e.tensor_scalar(
                        out=dg[:, k, :], in0=sm[:, oid:oid + 128], scalar1=sm[:, odw + c * 31 + k:odw + c * 31 + k + 1],
                        scalar2=None, op0=ALU.mult), reads=['sm'], writes=[ring.key(nd)])
                for tb in range(NTB):
                    bank = nbank()
                    for k in range(31):
                        p.op('pe', lambda e, k=k, dg=dg, c=c, tb=tb, bank=bank: e.matmul(
                            ps[:, bank, :], dg[:, k, :], zp[:, c, 2 * tb:2 * tb + 2, k:k + 256],
                            start=(k == 0), stop=(k == 30)),
                            reads=[ring.key(nd), ('zp', c)], writes=[('ps', bank)], inc=(k == 30))
                    p.op('act', lambda e, c=c, tb=tb, bank=bank: e.activation(
                        out=zc(c)[:, tb * TB:(tb + 1) * TB], in_=ps[:, bank, :], func=AF.Identity,
                        bias=sm[:, odb + c:odb + c + 1]), reads=[('ps', bank), 'sm'], writes=[('zc', c)])
                ring.release(nd)
            p.barrier()
            zcb = scr[:, 0:2048].bitcast(BF16).rearrange("p (c n) -> p c n", c=8)
            sqz = scr[:, 2048:4096].bitcast(BF16).rearrange("p (c n) -> p c n", c=8)
            mean = scr[:, 4096:4608]
            rstd = scr[:, 4608:5120]
            t1 = [scr[:, 5120:5632], scr[:, 5632:6144]]
            olg, _ = SM_OFF['cf_lg']
            olb, _ = SM_OFF['cf_lb']
            ob2, _ = SM_OFF['cf_b2']
            tmp2 = [scr[:, 6144:6656], scr[:, 6656:7168]]
            n2s = []
            for hf in range(2):
                n2s.append(ring.acquire(lambda slot, hf=hf: [(
                    slot[:, 0:4096].rearrange("p (k n) -> p k n", k=8),
                    conf_w_pw2[0, :, hf * 512:(hf + 1) * 512].rearrange("(k p) n -> p k n", p=128))]))
            it2 = [0]

            def pw2_block(tb):
                for hf in range(2):
                    n2 = n2s[hf]
                    w2 = ring.ap(n2).rearrange("p (k n) -> p k n", k=8)
                    for o4 in range(4):
                        oc = hf * 4 + o4
                        bank = nbank()
                        for k in range(NCH):
                            p.op('pe', lambda e, k=k, bank=bank, tb=tb, w2=w2, o4=o4: e.matmul(
                                ps[:, bank, :], w2[:, k, o4 * 128:(o4 + 1) * 128], zs(k)[:, tb * TB:(tb + 1) * TB],
                                start=(k == 0), stop=(k == NCH - 1)),
                                reads=[ring.key(n2), ('zc', k)], writes=[('ps', bank)], inc=(k == NCH - 1))
                        t = tmp2[it2[0] % 2]
                        tk = ('tmp2', it2[0] % 2)
                        it2[0] += 1
                        p.op('dve', lambda e, bank=bank, t=t, oc=oc: e.tensor_scalar(
                            out=t, in0=ps[:, bank, :], scalar1=sm[:, ob2 + oc:ob2 + oc + 1],
                            scalar2=mod[:, l, 16 + oc:17 + oc], op0=ALU.add, op1=ALU.mult),
                            reads=[('ps', bank), 'sm', ('mod', l)], writes=[tk])
                        p.op('dve', lambda e, t=t, oc=oc, tb=tb: e.tensor_tensor(
                            out=xT[:, oc, tb * TB:(tb + 1) * TB], in0=xT[:, oc, tb * TB:(tb + 1) * TB], in1=t,
                            op=ALU.add), reads=[tk, XK(oc, tb)], writes=[XK(oc, tb)])
            for tb in range(NTB):
                if not p.dry and tb >= 1:
                    pw2_block(tb - 1)
                b1 = nbank()
                b2 = nbank()
                for c in range(NCH):
                    p.op('act', lambda e, c=c, tb=tb: e.activation(
                        out=zcb[:, c, :], in_=zc(c)[:, tb * TB:(tb + 1) * TB], func=AF.Copy),
                        reads=[('zc', c)], writes=[('zcb', c)])
                    p.op('act', lambda e, c=c, tb=tb: e.activation(
                        out=sqz[:, c, :], in_=zc(c)[:, tb * TB:(tb + 1) * TB], func=AF.Square),
                        reads=[('zc', c)], writes=[('sqz', c)])
                for c in range(NCH):
                    p.op('pe', lambda e, c=c, b1=b1: e.matmul(ps[:, b1, :], ones[:], zcb[:, c, :],
                                                              start=(c == 0), stop=(c == NCH - 1)),
                         reads=['ones', ('zcb', c)], writes=[('ps', b1)], inc=(c == NCH - 1))
                for c in range(NCH):
                    p.op('pe', lambda e, c=c, b2=b2: e.matmul(ps[:, b2, :], ones[:], sqz[:, c, :],
                                                              start=(c == 0), stop=(c == NCH - 1)),
                         reads=['ones', ('sqz', c)], writes=[('ps', b2)], inc=(c == NCH - 1))
                p.op('dve', lambda e, b1=b1: e.tensor_scalar(out=mean, in0=ps[:, b1, :], scalar1=1.0 / D,
                                                             scalar2=None, op0=ALU.mult),
                     reads=[('ps', b1)], writes=['mean'])
                p.op('dve', lambda e: e.tensor_tensor(out=rstd, in0=mean, in1=mean, op=ALU.mult),
                     reads=['mean'], writes=['rstd'])
                p.op('dve', lambda e, b2=b2: e.scalar_tensor_tensor(
                    out=rstd, in0=ps[:, b2, :], scalar=1.0 / D, in1=rstd, op0=ALU.mult, op1=ALU.subtract),
                    reads=[('ps', b2), 'rstd'], writes=['rstd'])
                p.op('act', lambda e: e.activation(out=rstd, in_=rstd, func=AF.Sqrt, bias=S('eps', 0)),
                     reads=['rstd', 'sm'], writes=['rstd'])
                p.op('dve', lambda e: e.reciprocal(out=rstd, in_=rstd), reads=['rstd'], writes=['rstd'])
                for c in range(NCH):
                    t = t1[c % 2]
                    tk = ('t1', c % 2)
                    p.op('dve', lambda e, c=c, tb=tb, t=t: e.tensor_tensor(
                        out=t, in0=zc(c)[:, tb * TB:(tb + 1) * TB], in1=mean, op=ALU.subtract),
                        reads=[('zc', c), 'mean'], writes=[tk])
                    p.op('dve', lambda e, t=t: e.tensor_tensor(out=t, in0=t, in1=rstd, op=ALU.mult),
                         reads=[tk, 'rstd'], writes=[tk])
                    p.op('act', lambda e, c=c, tb=tb, t=t: e.activation(
                        out=zs(c)[:, tb * TB:(tb + 1) * TB], in_=t, func=AF.Silu,
                        bias=sm[:, olb + c:olb + c + 1], scale=sm[:, olg + c:olg + c + 1]),
                        reads=[tk, 'sm'], writes=[('zc', c)])
            if not p.dry:
                pw2_block(NTB - 1)
            for n2 in n2s:
                ring.release(n2)

        def att_chunks(i):
            if i == 0:
                ds = [0, 1, 2, 3]
            elif i == 1:
                ds = [-1, 0, 1, 2]
            elif i == 14:
                ds = [-2, -1, 0, 1]
            elif i == 15:
                ds = [-3, -2, -1, 0]
            else:
                return [(-2, 7), (-1, 2), (0, 3), (1, 4), (2, 8)]
            return [(d, d + 3) for d in ds]

        def na_mixer(l):
            oT = scr[:, 0:8192].bitcast(BF16).rearrange("p (c n) -> p c n", c=8)
            QTm = scr[:, 8192:10240].bitcast(BF16).rearrange("p (i h q) -> p i h q", i=16, h=2)
            KT = scr[:, 10240:11264].bitcast(BF16)
            Vaug = scr[:, 11264:12304].bitcast(BF16).rearrange("p (t h f) -> p t h f", t=16, h=2)
            Fb = scr[:, 12304:13456].bitcast(BF16).rearrange("p (h s q) -> p h s q", h=2, s=9)
            ctxK = scr[:, 13456:14480].bitcast(BF16).rearrange("p (c n) -> p c n", c=8)
            ctxVaug = scr[:, 14480:15520].bitcast(BF16).rearrange("p (t h f) -> p t h f", t=2, h=16)
            kst = [scr[:, 15520:16032], scr[:, 16032:16544]]
            ctxVtmp = scr[:, 15520:16544].bitcast(BF16).rearrange("p (t f) -> p t f", t=2)
            vst = [scr[:, 16544:16800].rearrange("p (t f) -> p t f", t=2),
                   scr[:, 16800:17056].rearrange("p (t f) -> p t f", t=2)]
            PT = [scr[:, 17056 + 128 * a:17056 + 128 * (a + 1)].bitcast(BF16).rearrange("p (h q) -> p h q", h=2)
                  for a in range(6)]
            tS = [scr[:, 17824 + 256 * a:17824 + 256 * (a + 1)].rearrange("p (h q) -> p h q", h=2) for a in range(2)]
            otok = [scr[:, 18336 + 64 * a:18336 + 64 * (a + 1)].bitcast(BF16) for a in range(2)]
            identb = scr[:, 18464:18528].bitcast(BF16)
            rinv = scr[:, 18528:18532]
            ocb, _ = SM_OFF['colb']
            oid, _ = SM_OFF['ident']
            bar = list(p.bar)
            p.dma('pool', [lambda e: e.dma_start(out=ctxK, in_=ctxk_d.rearrange("(c p) n -> p c n", p=128)),
                           lambda e: e.dma_start(out=ctxVtmp, in_=ctxv_d.rearrange("(t p) f -> p t f", p=128))],
                  'CTX', writes=['ctx', ('kst', 0), ('kst', 1)], extra=bar)
            p.op('dve', lambda e: e.tensor_copy(out=identb, in_=sm[:, oid:oid + 128]), reads=['sm'], writes=['identb'])
            p.op('dve', lambda e: e.memset(QTm[64:128, :, 0, :], 0.0), writes=['QT'])
            p.op('dve', lambda e: e.memset(QTm[0:64, :, 1, :], 0.0), writes=['QT'])
            p.op('dve', lambda e: e.memset(Vaug[:, :, :, 64:65], 1.0), writes=['Vc'])
            p.op('dve', lambda e: e.memset(ctxVaug[:, :, :, 64:65], 1.0), writes=['ctxV'])
            p.op('dve', lambda e: e.tensor_copy(
                out=ctxVaug[:, :, :, 0:64], in_=ctxVtmp.rearrange("p t (h f) -> p t h f", h=16)),
                reads=['ctx', ('kst', 0), ('kst', 1)], writes=['ctxV'])
            kv = k_out.rearrange("(c p) t -> p c t", p=128)
            vv = v_out.rearrange("(t p) f -> p t f", p=128)
            cnt = {'ss': 0, 'ks': 0, 'vs': 0}
            tiles = {}
            for c in range(NCH):
                cl = c % 2
                if cl == 0:
                    cp = c // 2
                    nqk = ring.acquire(lambda slot, cp=cp: [
                        (slot[:, 0:4096].rearrange("p (k n) -> p k n", k=8)[:, :, 0:256],
                         na_w_qkv[0, :, cp * 256:(cp + 1) * 256].rearrange("(k p) n -> p k n", p=128)),
                        (slot[:, 0:4096].rearrange("p (k n) -> p k n", k=8)[:, :, 256:512],
                         na_w_qkv[0, :, D + cp * 256:D + (cp + 1) * 256].rearrange("(k p) n -> p k n", p=128))])
                    nv = ring.acquire(lambda slot, cp=cp: [(
                        slot[:, 0:2048].rearrange("p (k n) -> p k n", k=8),
                        na_w_qkv[0, :, 2 * D + cp * 256:2 * D + (cp + 1) * 256].rearrange("(k p) n -> p k n", p=128))])
                    tiles['qk'], tiles['v'] = nqk, nv
                nqk, nv = tiles['qk'], tiles['v']
                if p.dry:
                    if cl == 1:
                        ring.release(nqk)
                        ring.release(nv)
                    continue
                wqk = ring.ap(nqk)[:, 0:4096].rearrange("p (k n) -> p k n", k=8)
                wvt = ring.ap(nv)[:, 0:2048].rearrange("p (k n) -> p k n", k=8)
                qo = cl * 128
                ko = 256 + cl * 128
                p.dma('pool', [lambda e, c=c: e.dma_start(
                    out=scr[:, 12304:13456].bitcast(BF16).rearrange("p (a n) -> p a n", a=2),
                    in_=fbias[c].rearrange("p (a n) -> p a n", a=2))], 'LDF', writes=['F'], extra=bar)
                for tb in range(NTB):
                    bank = tb
                    for k in range(NCH):
                        p.op('pe', lambda e, k=k, bank=bank, tb=tb, wqk=wqk, qo=qo: e.matmul(
                            ps[:, bank, :], wqk[:, k, qo:qo + 128], h[:, k, tb * TB:(tb + 1) * TB],
                            start=(k == 0), stop=(k == NCH - 1)),
                            reads=[ring.key(nqk), HK(k, tb)], writes=[('ps', bank)], inc=(k == NCH - 1))
                    for hh in range(2):
                        p.op('act', lambda e, bank=bank, tb=tb, hh=hh: e.activation(
                            out=QTm[64 * hh:64 * hh + 64, 4 * tb:4 * tb + 4, hh, :],
                            in_=ps[64 * hh:64 * hh + 64, bank, :].rearrange("p (i q) -> p i q", i=4),
                            func=AF.Identity, scale=0.125),
                            reads=[('ps', bank)], writes=['QT'])
                for tb in range(NTB):
                    bank = tb
                    for k in range(NCH):
                        p.op('pe', lambda e, k=k, bank=bank, tb=tb, wqk=wqk, ko=ko: e.matmul(
                            ps[:, bank, :], wqk[:, k, ko:ko + 128], h[:, k, tb * TB:(tb + 1) * TB],
                            start=(k == 0), stop=(k == NCH - 1)),
                            reads=[ring.key(nqk), HK(k, tb)], writes=[('ps', bank)], inc=(k == NCH - 1))
                    ks = cnt['ks'] % 2
                    cnt['ks'] += 1
                    p.op('act', lambda e, bank=bank, ks=ks: e.activation(out=kst[ks], in_=ps[:, bank, :], func=AF.Copy),
                         reads=[('ps', bank)], writes=[('kst', ks)])
                    p.op('dve', lambda e, ks=ks, tb=tb: e.tensor_copy(
                        out=KT[:, tb * TB:(tb + 1) * TB], in_=kst[ks]),
                        reads=[('kst', ks)], writes=['KT'])
                    p.dma('sp', [lambda e, c=c, tb=tb, ks=ks: e.dma_start(
                        out=kv[:, c, tb * TB:(tb + 1) * TB], in_=kst[ks])], 'STK%d' % ks, reads=[('kst', ks)])
                for t8 in range(8):
                    bank = t8 % 4
                    for tt in range(2):
                        tok = (t8 * 2 + tt) * 128
                        for k in range(NCH):
                            p.op('pe', lambda e, k=k, bank=bank, tt=tt, tok=tok, wvt=wvt, qo=qo: e.matmul(
                                ps[:, bank, tt * 128:(tt + 1) * 128], h[:, k, tok:tok + 128], wvt[:, k, qo:qo + 128],
                                start=(k == 0), stop=(k == NCH - 1)),
                                reads=[ring.key(nv), HK(k, tok // TB)], writes=[('ps', bank)], inc=(k == NCH - 1))
                    vs = cnt['vs'] % 2
                    cnt['vs'] += 1
                    p.op('act', lambda e, bank=bank, vs=vs: e.activation(
                        out=vst[vs], in_=ps[:, bank, 0:256].rearrange("p (t f) -> p t f", t=2), func=AF.Copy),
                        reads=[('ps', bank)], writes=[('vst', vs)])
                    p.op('dve', lambda e, vs=vs, t8=t8: e.tensor_copy(
                        out=Vaug[:, t8 * 2:(t8 + 1) * 2, :, 0:64],
                        in_=vst[vs].rearrange("p t (h f) -> p t h f", h=2)),
                        reads=[('vst', vs)], writes=['Vc'])
                    p.dma('sp', [lambda e, c=c, t8=t8, vs=vs, tt=tt: e.dma_start(
                        out=vv[:, t8 * 2 + tt, c * 128:(c + 1) * 128], in_=vst[vs][:, tt, :])
                        for tt in range(2)], 'STV%d' % vs, reads=[('vst', vs)])
                if cl == 1:
                    ring.release(nqk)
                    ring.release(nv)
                items = []
                for i in (range(16) if NA_DEBUG >= 2 else []):
                    chunks = [('loc', i + d, sl, d + 3) for (d, sl) in att_chunks(i)] + \
                             [('ctx', 0, None, 7), ('ctx', 1, None, 7)]
                    for ci, ch in enumerate(chunks):
                        items.append((i, ci, len(chunks), ch))
                LOOK = 5
                base = cnt['ss']
                cnt['ss'] += len(items)

                def emit_S(n, c=c):
                    i, ci, nchk, (kind, j, sl, cbi) = items[n]
                    gi = base + n
                    sb_ = gi % 5
                    psS = ps[:, sb_, 0:256].rearrange("p (h q) -> p h q", h=2)
                    skey = ('ps', sb_)
                    if kind == 'loc':
                        kl = KT[:, j * 128:(j + 1) * 128]
                        rd = ['KT', 'QT']
                    else:
                        kl = ctxK[:, c, j * 128:(j + 1) * 128]
                        rd = ['ctx', 'QT']
                    p.op('pe', lambda e: e.matmul(psS, kl, QTm[:, i, :, :],
                                                  start=True, stop=True), reads=rd, writes=[skey])
                    pt = gi % 6
                    cb = sm[:, ocb + i * 8 + cbi:ocb + i * 8 + cbi + 1]
                    if kind == 'loc':
                        ts = gi % 2
                        p.op('dve', lambda e: e.tensor_tensor(out=tS[ts], in0=psS, in1=Fb[:, :, sl, :], op=ALU.add),
                             reads=[skey, 'F'], writes=[('tS', ts)])
                        p.op('act', lambda e: e.activation(out=PT[pt], in_=tS[ts], func=AF.Exp, bias=cb),
                             reads=[('tS', ts), 'sm'], writes=[('PT', pt)])
                    else:
                        p.op('act', lambda e: e.activation(out=PT[pt], in_=psS, func=AF.Exp, bias=cb),
                             reads=[skey, 'sm'], writes=[('PT', pt)])

                def emit_PV(n, c=c):
                    i, ci, nchk, (kind, j, sl, cbi) = items[n]
                    gi = base + n
                    pt = gi % 6
                    for hh in range(2):
                        ob = 5 + hh
                        psO = ps[:, ob, 0:65]
                        if kind == 'loc':
                            vl = Vaug[:, j, hh, :]
                            rdv = 'Vc'
                        else:
                            vl = ctxVaug[:, j, 2 * c + hh, :]
                            rdv = 'ctxV'
                        p.op('pe', lambda e, psO=psO, vl=vl, hh=hh: e.matmul(
                            psO, PT[pt][:, hh, :], vl, start=(ci == 0), stop=(ci == nchk - 1)),
                            reads=[rdv, ('PT', pt)], writes=[('ps', ob)])
                    if ci == nchk - 1:
                        ot = otok[i % 2]
                        tbk = 7
                        psT = ps[:, tbk, 0:64].bitcast(BF16)

                        def tail1():
                            for hh in range(2):
                                ob = 5 + hh
                                psO = ps[:, ob, 0:65]
                                p.op('dve', lambda e, psO=psO, hh=hh: e.reciprocal(out=rinv[:, hh:hh + 1],
                                                                                   in_=psO[:, 64:65]),
                                     reads=[('ps', ob)], writes=[('rinv', hh)])
                                p.op('dve', lambda e, psO=psO, hh=hh: e.tensor_scalar(
                                    out=ot[:, hh * 64:(hh + 1) * 64], in0=psO[:, 0:64], scalar1=rinv[:, hh:hh + 1],
                                    scalar2=None, op0=ALU.mult),
                                    reads=[('ps', ob), ('rinv', hh)], writes=[('otok', i % 2)])

                        def tail2():
                            p.op('pe', lambda e: e.transpose(out=psT, in_=ot, identity=identb),
                                 reads=[('otok', i % 2), 'identb'], writes=[('ps', tbk)])

                        def tail3():
                            p.op('dve', lambda e: e.tensor_copy(out=oT[:, c, i * 128:(i + 1) * 128], in_=psT),
                                 reads=[('ps', tbk)], writes=[('oT', c)])
                        tail1()
                        deferred.append((n + LOOK + 2, tail2))
                        deferred.append((n + LOOK + 4, tail3))

                deferred = []
                nsteps = len(items) + LOOK + 6
                for n in range(nsteps):
                    if n < len(items):
                        emit_S(n)
                    if LOOK <= n < len(items) + LOOK:
                        emit_PV(n - LOOK)
                    while deferred and deferred[0][0] <= n:
                        deferred.pop(0)[1]()
                assert not deferred
            for hf in (range(2) if NA_DEBUG >= 3 else []):
                no = ring.acquire(lambda slot, hf=hf: [(
                    slot[:, 0:4096].rearrange("p (k n) -> p k n", k=8),
                    na_w_o[0, :, hf * 512:(hf + 1) * 512].rearrange("(k p) n -> p k n", p=128))])
                if p.dry:
                    ring.release(no)
                    continue
                wo = ring.ap(no).rearrange("p (k n) -> p k n", k=8)
                for tb in range(NTB):
                    for o4 in range(4):
                        oc = hf * 4 + o4
                        bank = (tb * 4 + o4) % 4
                        for k in range(NCH):
                            p.op('pe', lambda e, k=k, bank=bank, tb=tb, wo=wo, o4=o4: e.matmul(
                                ps[:, bank, :], wo[:, k, o4 * 128:(o4 + 1) * 128], oT[:, k, tb * TB:(tb + 1) * TB],
                                start=(k == 0), stop=(k == NCH - 1)),
                                reads=[ring.key(no), ('oT', k)], writes=[('ps', bank)], inc=(k == NCH - 1))
                        p.op('dve', lambda e, oc=oc, bank=bank, tb=tb: e.scalar_tensor_tensor(
                            out=xT[:, oc, tb * TB:(tb + 1) * TB], in0=ps[:, bank, :],
                            scalar=mod[:, l, 16 + oc:17 + oc], in1=xT[:, oc, tb * TB:(tb + 1) * TB],
                            op0=ALU.mult, op1=ALU.add),
                            reads=[('ps', bank), XK(oc, tb), ('mod', l)], writes=[XK(oc, tb)])
                ring.release(no)

        def state_out():
            p.dma('sp', [lambda e: e.dma_start(out=st_out[:, :], in_=stt[:])], 'STS', reads=['stt'])

        def final_out(gain=True):
            sq = scr_bf16(0, 8 * 512).rearrange("p (c n) -> p c n", c=8)
            rstd = scr_f32(2048, 512)
            tmps = [scr_f32(2560, 512), scr_f32(3072, 512)]
            stg = [scr_f32(4096, 512), scr_f32(4608, 512), scr_f32(5120, 512), scr_f32(5632, 512)]
            yv = y_out.rearrange("(c p) t -> p c t", p=128)
            it = 0
            for tb in range(NTB):
                if gain:
                    rms_stats(tb, sq, rstd)
                for c in range(NCH):
                    tmp = tmps[c % 2]
                    tk = ('tmp', c % 2)
                    st = stg[it % 4]
                    sk = ('stg', it % 4)
                    ssem = 'ST%d' % (it % 4)
                    it += 1
                    if gain:
                        p.op('dve', lambda e, c=c, tmp=tmp, tb=tb: e.tensor_tensor(
                            out=tmp, in0=xT[:, c, tb * TB:(tb + 1) * TB], in1=rstd, op=ALU.mult),
                            reads=[XK(c, tb), 'rstd'], writes=[tk])
                        p.op('act', lambda e, c=c, tmp=tmp, st=st: e.activation(
                            out=st, in_=tmp, func=AF.Identity, scale=S('fin_g', c)),
                            reads=[tk, 'sm'], writes=[sk])
                    else:
                        p.op('act', lambda e, c=c, st=st, tb=tb: e.activation(
                            out=st, in_=xT[:, c, tb * TB:(tb + 1) * TB], func=AF.Copy),
                            reads=[XK(c, tb)], writes=[sk])
                    p.dma('sp', [lambda e, c=c, st=st, tb=tb: e.dma_start(
                        out=yv[:, c, tb * TB:(tb + 1) * TB], in_=st)], ssem, reads=[sk])
            if not p.dry:
                fin = [(s_, v) for s_, v in p.cnt.items() if s_.startswith('ST')]
                p._waits('sp', fin)

        def plan():
            bank_ctr[0] = 0
            setup()
            lru_consts()
            if STOP_AFTER == ('setup',):
                final_out(gain=False)
                return
            for l in (range(4) if ONLY_LAYER is None else [ONLY_LAYER]):
                if l == 0 or ONLY_LAYER is not None:
                    modulation(l)
                if STOP_AFTER == ('mod', l):
                    final_out(gain=False)
                    return
                norm_mod(l, 0)
                if STOP_AFTER == ('norm', l):
                    final_out(gain=False)
                    return
                p.barrier()
                if l % 3 == 0:
                    lru_mixer(l, l // 3)
                elif l % 3 == 1:
                    conf_mixer(l)
                else:
                    na_mixer(l)
                p.barrier()
                if STOP_AFTER == ('mix', l):
                    final_out(gain=False)
                    return
                norm_mod(l, 24)
                p.barrier()
                ffn(l)
                p.barrier()
                if STOP_AFTER == ('ffn', l):
                    final_out(gain=False)
                    return
            state_out()
            final_out(gain=True)

        p.dry = True
        plan()
        p.dry = False
        p.reset()
        ring.start_real()
        plan()

        sem_names = sorted(p.cnt.keys())
        sems = {nm: es.enter_context(nc.semaphore(nm)) for nm in sem_names}
        block = es.enter_context(nc.Block())

        def replay(stream):
            def run(e):
                for it in stream:
                    if it[0] == 'wait':
                        e.wait_ge(sems[it[1]], it[2])
                    else:
                        ins = it[1](e)
                        if it[2] is not None:
                            ins.then_inc(sems[it[2]], it[3])
            return run

        block.tensor(replay(p.streams['pe']))
        block.scalar(replay(p.streams['act']))
        block.vector(replay(p.streams['dve']))
        block.gpsimd(replay(p.streams['pool']))
        block.sync(replay(p.streams['sp']))
    return nc


def make_smalls(cond, m, h0, inp, is_sample):
    a = np.zeros((128, SM_N), np.float32)

    def put(name, v):
        o, n = SM_OFF[name]
        v = np.asarray(v, np.float32).reshape(128, -1)
        assert v.shape[1] == n, (name, v.shape, n)
        a[:, o:o + n] = v
    put('cond', _fm(cond))
    put('bmod', np.stack([inp['b_mod'][l].reshape(48, 128).T for l in range(4)], axis=1))
    put('lru_cw', _fm(inp['lru_conv_w']).reshape(128, 2, 4, 8).transpose(0, 1, 3, 2))
    put('lru_cb', _fm(inp['lru_conv_b']))
    put('lru_ba', _fm(inp['lru_b_a']))
    put('lru_bx', _fm(inp['lru_b_x']))
    put('lru_lam', _fm(inp['lru_lambda']))
    put('h0', _fm(h0))
    put('cf_b1', inp['conf_b_pw1'][0].reshape(16, 128).T)
    put('cf_dw', _fm(inp['conf_dw_w'][0]).reshape(128, 31, 8).transpose(0, 2, 1))
    put('cf_db', _fm(inp['conf_dw_b'][0]))
    put('cf_lg', _fm(inp['conf_ln_g'][0]))
    put('cf_lb', _fm(inp['conf_ln_b'][0]))
    put('cf_b2', _fm(inp['conf_b_pw2'][0]))
    put('fin_g', _fm(inp['final_g']))
    put('m', np.full((128, 8), m, np.float32))
    put('colb', build_colb(is_sample))
    put('ident', np.eye(128, dtype=np.float32))
    put('eps', np.full((128, 8), EPS, np.float32))
    put('one', np.ones((128, 8), np.float32))
    put('q25', np.full((128, 8), 0.25, np.float32))
    return a


def build_fbias(rpb, is_sample):
    if not is_sample:
        return np.zeros((8, 128, 2304), np.float32)
    kr = np.arange(128) // 64
    kc = np.arange(128) % 64
    qr = np.arange(128) // 64
    qc = np.arange(128) % 64
    cs = np.clip(qc - 8, 0, 48)
    colmask = (kc[:, None] >= cs[None, :]) & (kc[:, None] < cs[None, :] + 16)
    dcol = np.clip(kc[:, None] - qc[None, :] + 15, 0, 30)
    out = np.full((16, 9, 128, 128), NEG, np.float32)
    for slot in range(9):
        delta = slot - 3 if slot < 7 else (-2 if slot == 7 else 2)
        dr = 2 * delta + kr[:, None] - qr[None, :]
        rowmask = (np.abs(dr) <= 7) if slot < 7 else ((dr >= -4) & (dr <= 3))
        mask = colmask & rowmask
        dri = np.clip(dr + 7, 0, 14)
        vals = rpb[:, dri, dcol]
        out[:, slot] = np.where(mask[None], vals, np.float32(NEG))
    out = out.reshape(8, 2, 9, 128, 128).transpose(0, 3, 1, 2, 4)
    return np.ascontiguousarray(out).reshape(8, 128, 2304)


def build_colb(is_sample):
    cb = np.zeros((16, 8), np.float32)
    if not is_sample:
        cb[:] = NEG
        for i in range(16):
            cb[i, 3] = 0.0
            if i % 2 == 0:
                cb[i, 4] = 0.0
            else:
                cb[i, 2] = 0.0
    return np.broadcast_to(cb.reshape(1, 128), (128, 128)).copy()


_NC_CACHE = {}


def kernel(**inp):
    inp = {k: np.asarray(v) for k, v in inp.items()}
    if 'nc' not in _NC_CACHE:
        _NC_CACHE['nc'] = build_program()
    nc = _NC_CACHE['nc']
    in_maps = []
    fb_s = build_fbias(inp['na_rpb'][0], True)
    fb_p = build_fbias(inp['na_rpb'][0], False)
    for core in range(8):
        if core < 4:
            xs = inp['x_prompt'][core * 8:(core + 1) * 8].reshape(T, D)
            cond = inp['c_ctx']
            m = 0.0
            h0 = np.zeros((2, 2, D), np.float32)
        else:
            b = core - 4
            xs = inp['x_sample'][b]
            cond = inp['c'][b]
            m = 1.0
            h0 = inp['state_lru'][b]
        in_maps.append({
            'xin': np.ascontiguousarray(xs.T),
            'smalls': make_smalls(cond, m, h0, inp, core >= 4),
            'na_w_qkv': inp['na_w_qkv'], 'na_w_o': inp['na_w_o'],
            'fbias': fb_s if core >= 4 else fb_p,
            'ctxk': (np.ascontiguousarray(inp['cache_k'][core - 4, 0].reshape(256, D).T) if core >= 4
                     else np.zeros((D, 256), np.float32)),
            'ctxv': (np.ascontiguousarray(inp['cache_v'][core - 4, 0].reshape(256, D)) if core >= 4
                     else np.zeros((256, D), np.float32)),
            'w_mod': inp['w_mod'], 'w_ff1': inp['w_ff1'], 'w_ff2': inp['w_ff2'],
            'lru_w_in': inp['lru_w_in'], 'lru_w_a': inp['lru_w_a'], 'lru_w_x': inp['lru_w_x'],
            'lru_w_out': inp['lru_w_out'],
            'conf_w_pw1': inp['conf_w_pw1'], 'conf_w_pw2': inp['conf_w_pw2'],
        })
    res = run_bass_kernel_spmd(nc, in_maps, core_ids=list(range(8)))
    outs = [r['y_out'] for r in res.results]
    y_prompt = np.stack([o.T for o in outs[:4]]).reshape(32, 256, D)
    y_sample = np.stack([o.T for o in outs[4:]])
    new_state = np.zeros((32, 2, 2, D), np.float32)
    for core in range(4):
        if 'st_out' not in res.results[core]:
            break
        st = res.results[core]['st_out'].reshape(128, 2, 2, 8, 8)
        st = np.concatenate([st[:, :, 0:1], st[:, :, 1:2, :, ::-1]], axis=2)
        new_state[core * 8:(core + 1) * 8] = st.transpose(4, 1, 2, 3, 0).reshape(8, 2, 2, D)
    new_k = np.zeros((32, 1, 256, 16, 64), np.float32)
    new_v = np.zeros((32, 1, 256, 16, 64), np.float32)
    for core in range(4):
        if 'k_out' not in res.results[core]:
            break
        new_k[core * 8:(core + 1) * 8, 0] = res.results[core]['k_out'].T.reshape(8, 256, 16, 64)
        new_v[core * 8:(core + 1) * 8, 0] = res.results[core]['v_out'].reshape(8, 256, 16, 64)
    return y_prompt, y_sample, new_state, new_k, new_v
```

```python
import numpy as np
from contextlib import ExitStack
import concourse.bass as bass
import concourse.mybir as mybir
from concourse.bass_utils import run_bass_kernel_spmd

F32 = mybir.dt.float32
BF16 = mybir.dt.bfloat16
ALU = mybir.AluOpType
AF = mybir.ActivationFunctionType

T = 2048
D = 1024
NCH = 8
TB = 512
NTB = 4
DFF = 4096
EPS = 1e-6
NSEG = 8
SEG = 256
NEG = -30000.0

STOP_AFTER = None
ONLY_LAYER = None
NA_DEBUG = 3
NA_SKIP = set()


class Planner:
    ENG = ('pe', 'act', 'dve', 'pool', 'sp')

    def __init__(self):
        self.dry = False
        self.reset()

    def reset(self):
        self.streams = {e: [] for e in self.ENG}
        self.cnt = {}
        self.seen = {e: {} for e in self.ENG}
        self.lastw = {}
        self.readers = {}
        self.bar = []

    def _deps(self, reads, writes):
        d = []
        for k in reads:
            if k in self.lastw:
                d.append(self.lastw[k])
        for k in writes:
            if k in self.lastw:
                d.append(self.lastw[k])
            r = self.readers.get(k)
            if r:
                d.extend(r.items())
        return d

    def _waits(self, eng, deps):
        need = {}
        seen = self.seen[eng]
        own = 'E_' + eng
        for (sem, v) in deps:
            if sem == own and (eng == 'pe' or v > self.cnt.get(sem, 0)):
                continue
            if seen.get(sem, 0) < v and need.get(sem, 0) < v:
                need[sem] = v
        for sem, v in need.items():
            seen[sem] = v
            self.streams[eng].append(('wait', sem, v))

    def _mark(self, reads, writes, sem, v):
        for k in writes:
            self.lastw[k] = (sem, v)
            self.readers[k] = {}
        for k in reads:
            r = self.readers.setdefault(k, {})
            if r.get(sem, 0) < v:
                r[sem] = v

    def op(self, eng, fn, reads=(), writes=(), inc=True):
        if self.dry:
            return
        self._waits(eng, self._deps(reads, writes))
        sem = 'E_' + eng
        v = self.cnt.get(sem, 0) + 1
        if inc:
            self.cnt[sem] = v
        self.streams[eng].append(('op', fn, sem if inc else None, 1))
        self._mark(reads, writes, sem, v)

    def dma(self, eng, fns, sem, reads=(), writes=(), extra=()):
        if self.dry:
            return
        self._waits(eng, self._deps(reads, writes) + list(extra))
        v = self.cnt.get(sem, 0) + 16 * len(fns)
        self.cnt[sem] = v
        for fn in fns:
            self.streams[eng].append(('op', fn, sem, 16))
        self._mark(reads, writes, sem, v)

    def barrier(self):
        if self.dry:
            return
        cur = [(s, v) for s, v in self.cnt.items() if s.startswith('E_') or s.startswith('ST')]
        for e in ('pe', 'act', 'dve'):
            self._waits(e, cur)
        self.bar = cur


class WRing:
    def __init__(self, p, slots):
        self.p = p
        self.slots = slots
        self.R = len(slots)
        self.reqs = []
        self.n_acq = 0
        self.n_dma = 0

    def start_real(self):
        self.n_acq = 0
        self.n_dma = 0

    def _emit(self):
        n = self.n_dma
        if n >= len(self.reqs):
            return
        self.n_dma += 1
        slot = n % self.R
        pairs = self.reqs[n](self.slots[slot])
        if not pairs:
            return
        fns = []
        for (o, i) in pairs:
            fns.append(lambda e, o=o, i=i: e.dma_start(out=o, in_=i))
        self.p.dma('pool', fns, 'W%d' % slot, reads=(), writes=[('W', slot)])

    def acquire(self, desc):
        if self.p.dry:
            self.reqs.append(desc)
            return len(self.reqs) - 1
        n = self.n_acq
        self.n_acq += 1
        if n == 0:
            for _ in range(self.R):
                self._emit()
        assert self.n_dma > n
        return n

    def key(self, n):
        return ('W', n % self.R)

    def ap(self, n):
        return self.slots[n % self.R]

    def release(self, n):
        if self.p.dry:
            return
        if n + self.R == self.n_dma:
            self._emit()


def _smalls_layout():
    off = {}
    cur = 0

    def add(name, n):
        nonlocal cur
        off[name] = (cur, n)
        cur += n
    add('cond', 8)
    add('bmod', 4 * 48)
    add('lru_cw', 2 * 8 * 4)
    add('lru_cb', 2 * 8)
    add('lru_ba', 2 * 2 * 8)
    add('lru_bx', 2 * 2 * 8)
    add('lru_lam', 2 * 2 * 8)
    add('h0', 2 * 2 * 8)
    add('cf_b1', 16)
    add('cf_dw', 8 * 31)
    add('cf_db', 8)
    add('cf_lg', 8)
    add('cf_lb', 8)
    add('cf_b2', 8)
    add('fin_g', 8)
    add('m', 8)
    add('colb', 16 * 8)
    add('ident', 128)
    add('eps', 8)
    add('one', 8)
    add('q25', 8)
    return off, cur


SM_OFF, SM_N = _smalls_layout()


def _fm(v):
    v = np.asarray(v, np.float32)
    lead = v.shape[:-1]
    a = v.reshape(lead + (8, 128))
    a = np.moveaxis(a, -1, 0)
    return np.ascontiguousarray(a).reshape(128, -1)


def build_program():
    nc = bass.Bass("TRN2", target_bir_lowering=False)
    dt_in = {}

    def din(name, shape):
        dt_in[name] = nc.dram_tensor(name, list(shape), F32, kind="ExternalInput").ap()
        return dt_in[name]

    def dout(name, shape):
        return nc.dram_tensor(name, list(shape), F32, kind="ExternalOutput").ap()

    xin = din('xin', [D, T])
    smalls_d = din('smalls', [128, SM_N])
    w_mod = din('w_mod', [4, D, 6 * D])
    w_ff1 = din('w_ff1', [4, D, DFF])
    w_ff2 = din('w_ff2', [4, DFF, D])
    lru_w_in = din('lru_w_in', [2, D, 2 * D])
    lru_w_a = din('lru_w_a', [2, 2, 4, 256, 256])
    lru_w_x = din('lru_w_x', [2, 2, 4, 256, 256])
    lru_w_out = din('lru_w_out', [2, D, D])
    st_out = dout('st_out', [128, 256])
    conf_w_pw1 = din('conf_w_pw1', [1, D, 2 * D])
    conf_w_pw2 = din('conf_w_pw2', [1, D, D])
    na_w_qkv = din('na_w_qkv', [1, D, 3 * D])
    na_w_o = din('na_w_o', [1, D, D])
    fbias = din('fbias', [8, 128, 2304])
    ctxk_d = din('ctxk', [D, 256])
    ctxv_d = din('ctxv', [256, D])
    k_out = dout('k_out', [D, T])
    v_out = dout('v_out', [T, D])
    y_out = dout('y_out', [D, T])

    p = Planner()
    es = ExitStack()
    with es:
        def sb(name, shape, dt):
            return es.enter_context(nc.sbuf_tensor(name, list(shape), dt))

        xT = sb('xT', [128, NCH, T], F32)
        h = sb('h', [128, NCH, T], BF16)
        wr = [sb('wr%d' % i, [128, 4096], BF16) for i in range(4)]
        sm = sb('smalls_sb', [128, SM_N], F32)
        mod = sb('mod', [128, 4, 48], F32)
        ones = sb('ones', [128, 128], BF16)
        scb = sb('scb', [128, 8], BF16)
        SCR = 18752
        stt = sb('stt', [128, 256], F32)
        cch = sb('cch', [128, 2, 32], F32)
        hbias = sb('hbias', [128, 2, 32], F32)
        scr = sb('scr', [128, SCR], F32)
        ps = es.enter_context(nc.psum_tensor('ps', [128, 8, 512], F32))

        ring = WRing(p, [w[:] for w in wr])

        def S(name, idx=None):
            o, n = SM_OFF[name]
            if idx is None:
                return sm[:, o:o + n]
            return sm[:, o + idx:o + idx + 1]

        bank_ctr = [0]

        def nbank():
            b = bank_ctr[0] % 8
            bank_ctr[0] += 1
            return b

        def XK(c, tb):
            return ('x', c, tb)

        def HK(c, tb):
            return ('h', c, tb)

        def setup():
            xv = xin.rearrange("(c p) t -> p c t", p=128)
            fns = []
            for c in range(NCH):
                fns.append(lambda e, c=c: e.dma_start(out=xT[:, c, :], in_=xv[:, c, :]))
            p.dma('sp', fns, 'LDX', writes=[XK(c, tb) for c in range(NCH) for tb in range(NTB)])
            p.dma('sp', [lambda e: e.dma_start(out=sm[:], in_=smalls_d[:, :])], 'LD0', writes=['sm'])
            p.op('dve', lambda e: e.memset(ones[:], 1.0), writes=['ones'])
            p.op('act', lambda e: e.activation(out=scb[:], in_=S('cond'), func=AF.Silu),
                 reads=['sm'], writes=['scb'])

        def mod_tile(l, cg, bank):
            n = ring.acquire(lambda slot, cg=cg: [(
                slot[:, 0:4096].rearrange("p (k n) -> p k n", k=8),
                w_mod[l, :, cg * 512:(cg + 1) * 512].rearrange("(k p) n -> p k n", p=128))])
            if not p.dry:
                wt = ring.ap(n).rearrange("p (k n) -> p k n", k=8)
                for j in range(4):
                    col = cg * 4 + j
                    for k in range(8):
                        p.op('pe', lambda e, wt=wt, j=j, k=k, col=col: e.matmul(
                            ps[:, bank, col:col + 1], wt[:, k, j * 128:(j + 1) * 128], scb[:, k:k + 1],
                            start=(k == 0), stop=(k == 7)),
                            reads=[ring.key(n), 'scb'], writes=[('ps', bank)], inc=(k == 7))
            ring.release(n)

        def mod_finish(l, bank):
            o, _ = SM_OFF['bmod']
            p.op('dve', lambda e: e.tensor_tensor(out=mod[:, l, :], in0=ps[:, bank, 0:48],
                                                   in1=sm[:, o + l * 48:o + (l + 1) * 48], op=ALU.add),
                 reads=[('ps', bank), 'sm'], writes=[('mod', l)])
            for a in (8, 32):
                p.op('dve', lambda e, a=a: e.tensor_scalar(out=mod[:, l, a:a + 8], in0=mod[:, l, a:a + 8],
                                                           scalar1=1.0, scalar2=None, op0=ALU.add),
                     reads=[('mod', l)], writes=[('mod', l)])

        def modulation(l):
            bank = nbank()
            for cg in range(12):
                mod_tile(l, cg, bank)
            mod_finish(l, bank)

        def scr_f32(off, n):
            return scr[:, off:off + n]

        def scr_bf16(off, n):
            return scr[:, off:off + n // 2].bitcast(BF16)

        def rms_stats(tb, sq, rstd):
            bank = nbank()
            for c in range(NCH):
                p.op('act', lambda e, c=c: e.activation(out=sq[:, c, :], in_=xT[:, c, tb * TB:(tb + 1) * TB],
                                                       func=AF.Square),
                     reads=[XK(c, tb)], writes=[('sq', c)])
            for c in range(NCH):
                p.op('pe', lambda e, c=c: e.matmul(ps[:, bank, :], ones[:], sq[:, c, :],
                                                  start=(c == 0), stop=(c == NCH - 1)),
                     reads=['ones', ('sq', c)], writes=[('ps', bank)], inc=(c == NCH - 1))
            p.op('act', lambda e: e.activation(out=rstd, in_=ps[:, bank, :], func=AF.Sqrt,
                                               bias=S('eps', 0), scale=1.0 / D),
                 reads=[('ps', bank), 'sm'], writes=['rstd'])
            p.op('dve', lambda e: e.reciprocal(out=rstd, in_=rstd), reads=['rstd'], writes=['rstd'])

        def norm_mod(l, a):
            sq = scr_bf16(0, 8 * 512).rearrange("p (c n) -> p c n", c=8)
            rstd = scr_f32(2048, 512)
            tmps = [scr_f32(2560, 512), scr_f32(3072, 512)]
            for tb in range(NTB):
                rms_stats(tb, sq, rstd)
                for c in range(NCH):
                    tmp = tmps[c % 2]
                    tk = ('tmp', c % 2)
                    p.op('dve', lambda e, c=c, tmp=tmp, tb=tb: e.tensor_tensor(
                        out=tmp, in0=xT[:, c, tb * TB:(tb + 1) * TB], in1=rstd, op=ALU.mult),
                        reads=[XK(c, tb), 'rstd'], writes=[tk])
                    p.op('act', lambda e, c=c, tmp=tmp, tb=tb: e.activation(
                        out=h[:, c, tb * TB:(tb + 1) * TB], in_=tmp, func=AF.Identity,
                        bias=mod[:, l, a + c:a + c + 1], scale=mod[:, l, a + 8 + c:a + 9 + c]),
                        reads=[tk, ('mod', l)], writes=[HK(c, tb)])

        def ffn(l):
            G = 4
            fctr = [0]

            def fbank():
                b = fctr[0] % 7
                fctr[0] += 1
                return b
            mod_todo = list(range(12)) if l + 1 < 4 else []
            hid = scr_bf16(0, G * T).rearrange("p (c n) -> p c n", c=G)
            rl = [scr_f32(4096, 512), scr_f32(4608, 512)]
            for g in range(DFF // (G * 128)):
                n1 = ring.acquire(lambda slot, g=g: [(
                    slot[:, 0:4096].rearrange("p (k n) -> p k n", k=8),
                    w_ff1[l, :, g * 512:(g + 1) * 512].rearrange("(k p) n -> p k n", p=128))])
                n2 = ring.acquire(lambda slot, g=g: [(
                    slot[:, 0:4096].rearrange("p (k n) -> p k n", k=4),
                    w_ff2[l, g * 512:(g + 1) * 512, :].rearrange("(k p) n -> p k n", p=128))])
                if not p.dry:
                    w1 = ring.ap(n1).rearrange("p (k n) -> p k n", k=8)
                    w2 = ring.ap(n2).rearrange("p (k n) -> p k n", k=4)
                    it = 0
                    for tb in range(NTB):
                        for c in range(G):
                            bank = fbank()
                            for k in range(NCH):
                                p.op('pe', lambda e, c=c, k=k, bank=bank, tb=tb, w1=w1: e.matmul(
                                    ps[:, bank, :], w1[:, k, c * 128:(c + 1) * 128], h[:, k, tb * TB:(tb + 1) * TB],
                                    start=(k == 0), stop=(k == NCH - 1)),
                                    reads=[ring.key(n1), HK(k, tb)], writes=[('ps', bank)], inc=(k == NCH - 1))
                            r = rl[it % 2]
                            rk = ('rl', it % 2)
                            it += 1
                            p.op('act', lambda e, r=r, bank=bank: e.activation(out=r, in_=ps[:, bank, :], func=AF.Relu),
                                 reads=[('ps', bank)], writes=[rk])
                            p.op('dve', lambda e, r=r, c=c, tb=tb: e.tensor_tensor(
                                out=hid[:, c, tb * TB:(tb + 1) * TB], in0=r, in1=r, op=ALU.mult),
                                reads=[rk], writes=[('hid', c, tb)])
                    ring.release(n1)
                    for tb in range(NTB):
                        for oc in range(NCH):
                            bank = fbank()
                            for k in range(G):
                                p.op('pe', lambda e, oc=oc, k=k, bank=bank, tb=tb, w2=w2: e.matmul(
                                    ps[:, bank, :], w2[:, k, oc * 128:(oc + 1) * 128], hid[:, k, tb * TB:(tb + 1) * TB],
                                    start=(k == 0), stop=(k == G - 1)),
                                    reads=[ring.key(n2), ('hid', k, tb)], writes=[('ps', bank)], inc=(k == G - 1))
                            p.op('dve', lambda e, oc=oc, bank=bank, tb=tb: e.scalar_tensor_tensor(
                                out=xT[:, oc, tb * TB:(tb + 1) * TB], in0=ps[:, bank, :],
                                scalar=mod[:, l, 40 + oc:41 + oc], in1=xT[:, oc, tb * TB:(tb + 1) * TB],
                                op0=ALU.mult, op1=ALU.add),
                                reads=[('ps', bank), XK(oc, tb), ('mod', l)], writes=[XK(oc, tb)])
                    ring.release(n2)
                else:
                    ring.release(n1)
                    ring.release(n2)
                for _ in range(2 if g % 2 == 0 else 1):
                    if mod_todo:
                        mod_tile(l + 1, mod_todo.pop(0), 7)
            if l + 1 < 4:
                mod_finish(l + 1, 7)


        def lru_consts():
            o, n = SM_OFF['lru_lam']
            p.op('act', lambda e: e.activation(out=cch[:, 0, :], in_=sm[:, o:o + n], func=AF.Exp, scale=-1.0),
                 reads=['sm'], writes=['cch'])
            p.op('act', lambda e: e.activation(out=cch[:, 0, :], in_=cch[:, 0, :], func=AF.Ln, bias=S('one', 0)),
                 reads=['cch', 'sm'], writes=['cch'])
            p.op('dve', lambda e: e.tensor_scalar(out=cch[:, 1, :], in0=cch[:, 0, :], scalar1=-8.0, scalar2=None,
                                                  op0=ALU.mult), reads=['cch'], writes=['cch'])
            p.op('dve', lambda e: e.tensor_scalar(out=cch[:, 0, :], in0=cch[:, 0, :], scalar1=-4.0, scalar2=None,
                                                  op0=ALU.mult), reads=['cch'], writes=['cch'])
            oa, na = SM_OFF['lru_ba']
            ox, nx = SM_OFF['lru_bx']
            p.op('dve', lambda e: e.tensor_scalar(out=hbias[:, 0, :], in0=sm[:, oa:oa + na], scalar1=0.5, scalar2=None,
                                                  op0=ALU.mult), reads=['sm'], writes=['hbias'])
            p.op('dve', lambda e: e.tensor_scalar(out=hbias[:, 1, :], in0=sm[:, ox:ox + nx], scalar1=0.5, scalar2=None,
                                                  op0=ALU.mult), reads=['sm'], writes=['hbias'])

        def lru_mixer(l, j):
            PADW = 259
            recp = scr[:, 0:2 * 8 * PADW].rearrange("p (c s w) -> p c s w", c=2, s=8)
            A = scr[:, 0:2048]
            B = scr[:, 2072:2072 + 2048]
            xf = scr[:, 4144:4144 + 4096].rearrange("p (c n) -> p c n", c=2)
            xfb = scr[:, 8240:8240 + 2048].bitcast(BF16).rearrange("p (c n) -> p c n", c=2)
            yb = scr[:, 10288:10288 + 2048].bitcast(BF16).rearrange("p (c n) -> p c n", c=2)
            C = scr[:, 12336:12336 + 2048]
            Hd = [scr[:, 14384:14384 + 2048], scr[:, 16432:16432 + 2048]]
            m_ap = S('m', 0)
            hctr = [0]
            C = scr[:, 12336:12336 + 2048]
            def block(n):
                nin = ring.acquire(lambda slot, n=n: [
                    (slot[:, 0:4096].rearrange("p (k n) -> p k n", k=8)[:, :, 0:256],
                     lru_w_in[j, :, n * 256:(n + 1) * 256].rearrange("(k p) n -> p k n", p=128)),
                    (slot[:, 0:4096].rearrange("p (k n) -> p k n", k=8)[:, :, 256:512],
                     lru_w_in[j, :, D + n * 256:D + (n + 1) * 256].rearrange("(k p) n -> p k n", p=128))])
                nax = ring.acquire(lambda slot, n=n: [
                    (slot[:, 0:2048].rearrange("p (k m n) -> p k m n", k=2, m=4)[:, :, 2 * d + w, :],
                     (lru_w_a if w == 0 else lru_w_x)[j, d, n].rearrange("(k p) n -> p k n", p=128))
                    for d in range(2) for w in range(2)])
                nout = ring.acquire(lambda slot, n=n: [(
                    slot[:, 0:2048].rearrange("p (k n) -> p k n", k=2),
                    lru_w_out[j, n * 256:(n + 1) * 256, :].rearrange("(k p) n -> p k n", p=128))])
                if p.dry:
                    yield
                    yield
                    return
                win = ring.ap(nin).rearrange("p (k n) -> p k n", k=8)
                wax = ring.ap(nax)[:, 0:2048].rearrange("p (k m n) -> p k m n", k=2, m=4)
                wout = ring.ap(nout)[:, 0:2048].rearrange("p (k n) -> p k n", k=2)
                for cc in range(2):
                    for tb in range(NTB):
                        bank = nbank()
                        for k in range(NCH):
                            p.op('pe', lambda e, cc=cc, k=k, bank=bank, tb=tb, win=win: e.matmul(
                                ps[:, bank, :], win[:, k, 256 + cc * 128:256 + (cc + 1) * 128],
                                h[:, k, tb * TB:(tb + 1) * TB], start=(k == 0), stop=(k == NCH - 1)),
                                reads=[ring.key(nin), HK(k, tb)], writes=[('ps', bank)], inc=(k == NCH - 1))
                        p.op('act', lambda e, cc=cc, bank=bank, tb=tb: e.activation(
                            out=recp[:, cc, 2 * tb:2 * tb + 2, 1:257],
                            in_=ps[:, bank, :].rearrange("p (s w) -> p s w", s=2), func=AF.Copy),
                            reads=[('ps', bank)],
                            writes=[('recp', cc), ('A', 0), ('A', 1)] if cc == 0 else [('recp', cc), ('B', 0), ('B', 1)])
                    p.op('dve', lambda e, cc=cc: e.memset(recp[:, cc, 0, 0:1], 0.0), writes=[('recp', cc)])
                    p.op('dve', lambda e, cc=cc: e.memset(recp[:, cc, 7, 257:259], 0.0), writes=[('recp', cc)])
                    p.op('dve', lambda e, cc=cc: e.tensor_scalar(
                        out=recp[:, cc, 1:8, 0:1], in0=recp[:, cc, 0:7, 256:257], scalar1=m_ap, scalar2=None,
                        op0=ALU.mult), reads=['sm', ('recp', cc)], writes=[('recp', cc)])
                    p.op('dve', lambda e, cc=cc: e.tensor_scalar(
                        out=recp[:, cc, 0:7, 257:259], in0=recp[:, cc, 1:8, 1:3], scalar1=m_ap, scalar2=None,
                        op0=ALU.mult), reads=['sm', ('recp', cc)], writes=[('recp', cc)])
                    ch = 2 * n + cc
                    xfv = xf[:, cc, :].rearrange("p (s w) -> p s w", s=8)
                    ocw, _ = SM_OFF['lru_cw']
                    ocb, _ = SM_OFF['lru_cb']
                    wbase = ocw + (j * 8 + ch) * 4
                    p.op('dve', lambda e, cc=cc, xfv=xfv, wbase=wbase, ch=ch: e.tensor_scalar(
                        out=xfv, in0=recp[:, cc, :, 0:256], scalar1=sm[:, wbase:wbase + 1],
                        scalar2=sm[:, ocb + j * 8 + ch:ocb + j * 8 + ch + 1], op0=ALU.mult, op1=ALU.add),
                        reads=['sm', ('recp', cc)], writes=[('xf', cc)])
                    for k in range(1, 4):
                        p.op('dve', lambda e, cc=cc, xfv=xfv, wbase=wbase, k=k: e.scalar_tensor_tensor(
                            out=xfv, in0=recp[:, cc, :, k:k + 256], scalar=sm[:, wbase + k:wbase + k + 1], in1=xfv,
                            op0=ALU.mult, op1=ALU.add),
                            reads=['sm', ('recp', cc), ('xf', cc)], writes=[('xf', cc)])
                    p.op('act', lambda e, cc=cc: e.activation(out=xfb[:, cc, :], in_=xf[:, cc, :], func=AF.Copy),
                         reads=[('xf', cc)], writes=[('xfb', cc)])
                yield
                oba, _ = SM_OFF['lru_ba']
                obx, _ = SM_OFF['lru_bx']
                oh0, _ = SM_OFF['h0']
                HT = T // 2
                Aset = [scr[:, 0:1024], scr[:, 1024:2048]]
                Bset = [scr[:, 2072:3096], scr[:, 3096:4120]]
                Cset = [scr[:, 12336:13360], scr[:, 13360:14384]]
                for co in range(2):
                    ch = 2 * n + co
                    for d in range(2):
                        idx = (j * 2 + d) * 8 + ch
                        for hs in range(2):
                            Ah, Bh, Ch = Aset[hs], Bset[hs], Cset[hs]
                            AKs, BKs, CKs = ('A', hs), ('B', hs), ('C', hs)
                            if d == 0:
                                tbs = [2 * hs, 2 * hs + 1]
                            else:
                                tbs = [2 * (1 - hs) + 1, 2 * (1 - hs)]

                            def dst(buf, tb, d=d, hs=hs):
                                if d == 0:
                                    o_ = (tb - 2 * hs) * TB
                                    return buf[:, o_:o_ + TB]
                                hi = T - 1 - tb * TB - hs * HT
                                lo = hi - TB
                                return buf[:, hi:lo:-1] if lo >= 0 else buf[:, hi::-1]
                            for (w, buf, keys) in ((0, Ah, [AKs, ('recp', 0)]), (1, Bh, [BKs, ('recp', 1)])):
                                for tb in tbs:
                                    bank = nbank()
                                    for k in range(2):
                                        p.op('pe', lambda e, k=k, bank=bank, tb=tb, wax=wax, d=d, w=w, co=co: e.matmul(
                                            ps[:, bank, :], wax[:, k, 2 * d + w, co * 128:(co + 1) * 128],
                                            xfb[:, k, tb * TB:(tb + 1) * TB], start=(k == 0), stop=(k == 1)),
                                            reads=[ring.key(nax), ('xfb', k)], writes=[('ps', bank)], inc=(k == 1))
                                    p.op('act', lambda e, bank=bank, o=dst(buf, tb), w=w, idx=idx: e.activation(
                                        out=o, in_=ps[:, bank, :], func=AF.Tanh, bias=hbias[:, w, idx:idx + 1], scale=0.5),
                                        reads=[('ps', bank), 'hbias'], writes=keys)
                            p.op('act', lambda e, idx=idx, Ah=Ah, Ch=Ch: e.activation(
                                out=Ch, in_=Ah, func=AF.Exp, scale=cch[:, 1, idx:idx + 1], bias=cch[:, 1, idx:idx + 1]),
                                reads=[AKs, 'cch'], writes=[CKs])
                            p.op('act', lambda e, idx=idx, Ah=Ah: e.activation(
                                out=Ah, in_=Ah, func=AF.Exp, scale=cch[:, 0, idx:idx + 1], bias=cch[:, 0, idx:idx + 1]),
                                reads=[AKs, 'cch'], writes=[AKs])
                            if d == 0:
                                xsrc = xf[:, co, hs * HT:(hs + 1) * HT]
                            else:
                                hi = (2 - hs) * HT - 1
                                lo = (1 - hs) * HT - 1
                                xsrc = xf[:, co, hi:lo:-1] if lo >= 0 else xf[:, co, hi::-1]
                            p.op('dve', lambda e, xsrc=xsrc, Bh=Bh: e.scalar_tensor_tensor(
                                out=Bh, in0=Bh, scalar=1.0, in1=xsrc, op0=ALU.add, op1=ALU.mult),
                                reads=[BKs, ('xf', co)], writes=[BKs])
                        for hs in range(2):
                            Ch = Cset[hs]
                            p.op('act', lambda e, Ch=Ch: e.activation(out=Ch, in_=Ch, func=AF.Sqrt, bias=S('q25', 0),
                                                                      scale=-0.25),
                                 reads=[('C', hs), 'sm'], writes=[('C', hs)])
                        for hs in range(2):
                            Ah, Bh, Ch = Aset[hs], Bset[hs], Cset[hs]
                            AKs, BKs, CKs = ('A', hs), ('B', hs), ('C', hs)
                            p.op('dve', lambda e, Bh=Bh, Ch=Ch: e.tensor_tensor(out=Bh, in0=Bh, in1=Ch, op=ALU.mult),
                                 reads=[BKs, CKs], writes=[BKs])
                            p.op('dve', lambda e, Ah=Ah: e.tensor_scalar(
                                out=Ah[:, 0:HT:SEG], in0=Ah[:, 0:HT:SEG], scalar1=m_ap, scalar2=None, op0=ALU.mult),
                                reads=[AKs, 'sm'], writes=[AKs])
                            init = sm[:, oh0 + idx:oh0 + idx + 1] if hs == 0 else Hd[d][:, HT - 1:HT]
                            p.op('dve', lambda e, d=d, hs=hs, Ah=Ah, Bh=Bh, init=init: e.tensor_tensor_scan(
                                out=Hd[d][:, hs * HT:(hs + 1) * HT], data0=Ah, data1=Bh, initial=init,
                                op0=ALU.mult, op1=ALU.add), reads=[AKs, BKs, 'sm', ('H', d)], writes=[('H', d)])
                        so = ((j * 2 + d) * 8 + ch) * 8
                        p.op('dve', lambda e, d=d, so=so: e.tensor_copy(out=stt[:, so:so + 8],
                                                                        in_=Hd[d][:, SEG - 1:T:SEG]),
                             reads=[('H', d)], writes=['stt'])
                    for tb in range(NTB):
                        bank = nbank()
                        for k in range(NCH):
                            p.op('pe', lambda e, k=k, bank=bank, tb=tb, win=win, co=co: e.matmul(
                                ps[:, bank, :], win[:, k, co * 128:(co + 1) * 128],
                                h[:, k, tb * TB:(tb + 1) * TB], start=(k == 0), stop=(k == NCH - 1)),
                                reads=[ring.key(nin), HK(k, tb)], writes=[('ps', bank)], inc=(k == NCH - 1))
                        p.op('act', lambda e, bank=bank, tb=tb: e.activation(
                            out=C[:, tb * TB:(tb + 1) * TB], in_=ps[:, bank, :], func=AF.Gelu_apprx_tanh),
                            reads=[('ps', bank)], writes=[('C', 0), ('C', 1)])
                    p.op('dve', lambda e: e.tensor_tensor(out=Hd[0], in0=Hd[0], in1=Hd[1][:, ::-1], op=ALU.add),
                         reads=[('H', 0), ('H', 1)], writes=[('H', 0)])
                    p.op('dve', lambda e, co=co: e.tensor_tensor(out=yb[:, co, :], in0=Hd[0], in1=C, op=ALU.mult),
                         reads=[('H', 0), ('C', 0), ('C', 1)], writes=[('yb', co)])
                ring.release(nin)
                ring.release(nax)
                yield
                for tb in range(NTB):
                    for oc in range(NCH):
                        bank = nbank()
                        for k in range(2):
                            p.op('pe', lambda e, oc=oc, k=k, bank=bank, tb=tb, wout=wout: e.matmul(
                                ps[:, bank, :], wout[:, k, oc * 128:(oc + 1) * 128], yb[:, k, tb * TB:(tb + 1) * TB],
                                start=(k == 0), stop=(k == 1)),
                                reads=[ring.key(nout), ('yb', k)], writes=[('ps', bank)], inc=(k == 1))
                        p.op('dve', lambda e, oc=oc, bank=bank, tb=tb: e.scalar_tensor_tensor(
                            out=xT[:, oc, tb * TB:(tb + 1) * TB], in0=ps[:, bank, :],
                            scalar=mod[:, l, 16 + oc:17 + oc], in1=xT[:, oc, tb * TB:(tb + 1) * TB],
                            op0=ALU.mult, op1=ALU.add),
                            reads=[('ps', bank), XK(oc, tb), ('mod', l)], writes=[XK(oc, tb)])
                ring.release(nout)

            def finish(g):
                for _ in g:
                    pass
            gens = [block(n) for n in range(4)]
            next(gens[0])
            next(gens[0])
            for n in range(1, 4):
                next(gens[n])
                finish(gens[n - 1])
                next(gens[n])
            finish(gens[3])


        def conf_mixer(l):
            PW = 286
            zp = scr[:, 0:9152].bitcast(BF16).rearrange("p (c s w) -> p c s w", c=8, s=8)
            hflat = h[:].rearrange("p c n -> p (c n)")

            def zc(c):
                if c < 4:
                    return hflat[:, c * 4096:(c + 1) * 4096].bitcast(F32)
                return scr[:, 9152 + (c - 4) * 2048:9152 + (c - 3) * 2048]

            def zs(c):
                if c < 4:
                    return hflat[:, c * 4096:c * 4096 + 2048]
                return scr[:, 9152 + (c - 4) * 2048:9152 + (c - 4) * 2048 + 1024].bitcast(BF16)
            sg = [scr[:, 17344:17344 + 512], scr[:, 17856:17856 + 512]]
            m_ap = S('m', 0)
            ob1, _ = SM_OFF['cf_b1']
            it = 0
            for g in range(4):
                n1 = ring.acquire(lambda slot, g=g: [
                    (slot[:, 0:4096].rearrange("p (k n) -> p k n", k=8)[:, :, 0:256],
                     conf_w_pw1[0, :, g * 256:(g + 1) * 256].rearrange("(k p) n -> p k n", p=128)),
                    (slot[:, 0:4096].rearrange("p (k n) -> p k n", k=8)[:, :, 256:512],
                     conf_w_pw1[0, :, D + g * 256:D + (g + 1) * 256].rearrange("(k p) n -> p k n", p=128))])
                if p.dry:
                    ring.release(n1)
                    continue
                w1 = ring.ap(n1).rearrange("p (k n) -> p k n", k=8)
                for cc in range(2):
                    c = 2 * g + cc
                    for tb in range(NTB):
                        bv = nbank()
                        for k in range(NCH):
                            p.op('pe', lambda e, cc=cc, k=k, bv=bv, tb=tb, w1=w1: e.matmul(
                                ps[:, bv, :], w1[:, k, cc * 128:(cc + 1) * 128], h[:, k, tb * TB:(tb + 1) * TB],
                                start=(k == 0), stop=(k == NCH - 1)),
                                reads=[ring.key(n1), HK(k, tb)], writes=[('ps', bv)], inc=(k == NCH - 1))
                        bg = nbank()
                        for k in range(NCH):
                            p.op('pe', lambda e, cc=cc, k=k, bg=bg, tb=tb, w1=w1: e.matmul(
                                ps[:, bg, :], w1[:, k, 256 + cc * 128:256 + (cc + 1) * 128],
                                h[:, k, tb * TB:(tb + 1) * TB], start=(k == 0), stop=(k == NCH - 1)),
                                reads=[ring.key(n1), HK(k, tb)], writes=[('ps', bg)], inc=(k == NCH - 1))
                        sgt = sg[it % 2]
                        sgk = ('sg', it % 2)
                        it += 1
                        p.op('act', lambda e, bg=bg, sgt=sgt, c=c: e.activation(
                            out=sgt, in_=ps[:, bg, :], func=AF.Sigmoid, bias=sm[:, ob1 + 8 + c:ob1 + 9 + c]),
                            reads=[('ps', bg), 'sm'], writes=[sgk])
                        p.op('dve', lambda e, bv=bv, sgt=sgt, c=c, tb=tb: e.scalar_tensor_tensor(
                            out=zp[:, c, 2 * tb:2 * tb + 2, 15:271],
                            in0=ps[:, bv, :].rearrange("p (s w) -> p s w", s=2), scalar=sm[:, ob1 + c:ob1 + c + 1],
                            in1=sgt.rearrange("p (s w) -> p s w", s=2), op0=ALU.add, op1=ALU.mult),
                            reads=[('ps', bv), sgk, 'sm'], writes=[('zp', c)])
                    p.op('dve', lambda e, c=c: e.memset(zp[:, c, 0, 0:15], 0.0), writes=[('zp', c)])
                    p.op('dve', lambda e, c=c: e.memset(zp[:, c, 7, 271:286], 0.0), writes=[('zp', c)])
                    p.op('dve', lambda e, c=c: e.tensor_scalar(
                        out=zp[:, c, 1:8, 0:15], in0=zp[:, c, 0:7, 256:271], scalar1=m_ap, scalar2=None,
                        op0=ALU.mult), reads=['sm', ('zp', c)], writes=[('zp', c)])
                    p.op('dve', lambda e, c=c: e.tensor_scalar(
                        out=zp[:, c, 0:7, 271:286], in0=zp[:, c, 1:8, 15:30], scalar1=m_ap, scalar2=None,
                        op0=ALU.mult), reads=['sm', ('zp', c)], writes=[('zp', c)])
                ring.release(n1)
            p.barrier()
            odw, _ = SM_OFF['cf_dw']
            odb, _ = SM_OFF['cf_db']
            oid, _ = SM_OFF['ident']
            for c in range(NCH):
                nd = ring.acquire(lambda slot: [])
                if p.dry:
                    ring.release(nd)
                    continue
                dg = ring.ap(nd)[:, 0:31 * 128].rearrange("p (k n) -> p k n", k=31)
                for k in range(31):
                    p.op('dve', lambda e, k=k, dg=dg, c=c: e.tensor_scalar(
                        out=dg[:, k, :], in0=sm[:, oid:oid + 128], scalar1=sm[:, odw + c * 31 + k:odw + c * 31 + k + 1],
                        scalar2=None, op0=ALU.mult), reads=['sm'], writes=[ring.key(nd)])
                for tb in range(NTB):
                    bank = nbank()
                    for k in range(31):
                        p.op('pe', lambda e, k=k, dg=dg, c=c, tb=tb, bank=bank: e.matmul(
                            ps[:, bank, :], dg[:, k, :], zp[:, c, 2 * tb:2 * tb + 2, k:k + 256],
                            start=(k == 0), stop=(k == 30)),
                            reads=[ring.key(nd), ('zp', c)], writes=[('ps', bank)], inc=(k == 30))
                    p.op('act', lambda e, c=c, tb=tb, bank=bank: e.activation(
                        out=zc(c)[:, tb * TB:(tb + 1) * TB], in_=ps[:, bank, :], func=AF.Identity,
                        bias=sm[:, odb + c:odb + c + 1]), reads=[('ps', bank), 'sm'], writes=[('zc', c)])
                ring.release(nd)
            p.barrier()
            zcb = scr[:, 0:2048].bitcast(BF16).rearrange("p (c n) -> p c n", c=8)
            sqz = scr[:, 2048:4096].bitcast(BF16).rearrange("p (c n) -> p c n", c=8)
            mean = scr[:, 4096:4608]
            rstd = scr[:, 4608:5120]
            t1 = [scr[:, 5120:5632], scr[:, 5632:6144]]
            olg, _ = SM_OFF['cf_lg']
            olb, _ = SM_OFF['cf_lb']
            ob2, _ = SM_OFF['cf_b2']
            tmp2 = [scr[:, 6144:6656], scr[:, 6656:7168]]
            n2s = []
            for hf in range(2):
                n2s.append(ring.acquire(lambda slot, hf=hf: [(
                    slot[:, 0:4096].rearrange("p (k n) -> p k n", k=8),
                    conf_w_pw2[0, :, hf * 512:(hf + 1) * 512].rearrange("(k p) n -> p k n", p=128))]))
            it2 = [0]

            def pw2_block(tb):
                for hf in range(2):
                    n2 = n2s[hf]
                    w2 = ring.ap(n2).rearrange("p (k n) -> p k n", k=8)
                    for o4 in range(4):
                        oc = hf * 4 + o4
                        bank = nbank()
                        for k in range(NCH):
                            p.op('pe', lambda e, k=k, bank=bank, tb=tb, w2=w2, o4=o4: e.matmul(
                                ps[:, bank, :], w2[:, k, o4 * 128:(o4 + 1) * 128], zs(k)[:, tb * TB:(tb + 1) * TB],
                                start=(k == 0), stop=(k == NCH - 1)),
                                reads=[ring.key(n2), ('zc', k)], writes=[('ps', bank)], inc=(k == NCH - 1))
                        t = tmp2[it2[0] % 2]
                        tk = ('tmp2', it2[0] % 2)
                        it2[0] += 1
                        p.op('dve', lambda e, bank=bank, t=t, oc=oc: e.tensor_scalar(
                            out=t, in0=ps[:, bank, :], scalar1=sm[:, ob2 + oc:ob2 + oc + 1],
                            scalar2=mod[:, l, 16 + oc:17 + oc], op0=ALU.add, op1=ALU.mult),
                            reads=[('ps', bank), 'sm', ('mod', l)], writes=[tk])
                        p.op('dve', lambda e, t=t, oc=oc, tb=tb: e.tensor_tensor(
                            out=xT[:, oc, tb * TB:(tb + 1) * TB], in0=xT[:, oc, tb * TB:(tb + 1) * TB], in1=t,
                            op=ALU.add), reads=[tk, XK(oc, tb)], writes=[XK(oc, tb)])
            for tb in range(NTB):
                if not p.dry and tb >= 1:
                    pw2_block(tb - 1)
                b1 = nbank()
                b2 = nbank()
                for c in range(NCH):
                    p.op('act', lambda e, c=c, tb=tb: e.activation(
                        out=zcb[:, c, :], in_=zc(c)[:, tb * TB:(tb + 1) * TB], func=AF.Copy),
                        reads=[('zc', c)], writes=[('zcb', c)])
                    p.op('act', lambda e, c=c, tb=tb: e.activation(
                        out=sqz[:, c, :], in_=zc(c)[:, tb * TB:(tb + 1) * TB], func=AF.Square),
                        reads=[('zc', c)], writes=[('sqz', c)])
                for c in range(NCH):
                    p.op('pe', lambda e, c=c, b1=b1: e.matmul(ps[:, b1, :], ones[:], zcb[:, c, :],
                                                              start=(c == 0), stop=(c == NCH - 1)),
                         reads=['ones', ('zcb', c)], writes=[('ps', b1)], inc=(c == NCH - 1))
                for c in range(NCH):
                    p.op('pe', lambda e, c=c, b2=b2: e.matmul(ps[:, b2, :], ones[:], sqz[:, c, :],
                                                              start=(c == 0), stop=(c == NCH - 1)),
                         reads=['ones', ('sqz', c)], writes=[('ps', b2)], inc=(c == NCH - 1))
                p.op('dve', lambda e, b1=b1: e.tensor_scalar(out=mean, in0=ps[:, b1, :], scalar1=1.0 / D,
                                                             scalar2=None, op0=ALU.mult),
                     reads=[('ps', b1)], writes=['mean'])
                p.op('dve', lambda e: e.tensor_tensor(out=rstd, in0=mean, in1=mean, op=ALU.mult),
                     reads=['mean'], writes=['rstd'])
                p.op('dve', lambda e, b2=b2: e.scalar_tensor_tensor(
                    out=rstd, in0=ps[:, b2, :], scalar=1.0 / D, in1=rstd, op0=ALU.mult, op1=ALU.subtract),
                    reads=[('ps', b2), 'rstd'], writes=['rstd'])
                p.op('act', lambda e: e.activation(out=rstd, in_=rstd, func=AF.Sqrt, bias=S('eps', 0)),
                     reads=['rstd', 'sm'], writes=['rstd'])
                p.op('dve', lambda e: e.reciprocal(out=rstd, in_=rstd), reads=['rstd'], writes=['rstd'])
                for c in range(NCH):
                    t = t1[c % 2]
                    tk = ('t1', c % 2)
                    p.op('dve', lambda e, c=c, tb=tb, t=t: e.tensor_tensor(
                        out=t, in0=zc(c)[:, tb * TB:(tb + 1) * TB], in1=mean, op=ALU.subtract),
                        reads=[('zc', c), 'mean'], writes=[tk])
                    p.op('dve', lambda e, t=t: e.tensor_tensor(out=t, in0=t, in1=rstd, op=ALU.mult),
                         reads=[tk, 'rstd'], writes=[tk])
                    p.op('act', lambda e, c=c, tb=tb, t=t: e.activation(
                        out=zs(c)[:, tb * TB:(tb + 1) * TB], in_=t, func=AF.Silu,
                        bias=sm[:, olb + c:olb + c + 1], scale=sm[:, olg + c:olg + c + 1]),
                        reads=[tk, 'sm'], writes=[('zc', c)])
            if not p.dry:
                pw2_block(NTB - 1)
            for n2 in n2s:
                ring.release(n2)

        def att_chunks(i):
            if i == 0:
                ds = [0, 1, 2, 3]
            elif i == 1:
                ds = [-1, 0, 1, 2]
            elif i == 14:
                ds = [-2, -1, 0, 1]
            elif i == 15:
                ds = [-3, -2, -1, 0]
            else:
                return [(-2, 7), (-1, 2), (0, 3), (1, 4), (2, 8)]
            return [(d, d + 3) for d in ds]

        def na_mixer(l):
            oT = scr[:, 0:8192].bitcast(BF16).rearrange("p (c n) -> p c n", c=8)
            QTm = scr[:, 8192:10240].bitcast(BF16).rearrange("p (i h q) -> p i h q", i=16, h=2)
            KT = scr[:, 10240:11264].bitcast(BF16)
            Vaug = scr[:, 11264:12304].bitcast(BF16).rearrange("p (t h f) -> p t h f", t=16, h=2)
            Fb = scr[:, 12304:13456].bitcast(BF16).rearrange("p (h s q) -> p h s q", h=2, s=9)
            ctxK = scr[:, 13456:14480].bitcast(BF16).rearrange("p (c n) -> p c n", c=8)
            ctxVaug = scr[:, 14480:15520].bitcast(BF16).rearrange("p (t h f) -> p t h f", t=2, h=16)
            kst = [scr[:, 15520:16032], scr[:, 16032:16544]]
            ctxVtmp = scr[:, 15520:16544].bitcast(BF16).rearrange("p (t f) -> p t f", t=2)
            vst = [scr[:, 16544:16800].rearrange("p (t f) -> p t f", t=2),
                   scr[:, 16800:17056].rearrange("p (t f) -> p t f", t=2)]
            NPT = 7
            PT = [scr[:, 17056 + 128 * a:17056 + 128 * (a + 1)].bitcast(BF16).rearrange("p (h q) -> p h q", h=2)
                  for a in range(NPT)]
            tS = [scr[:, 17952 + 256 * a:17952 + 256 * (a + 1)].rearrange("p (h q) -> p h q", h=2) for a in range(2)]
            otok = [scr[:, 18464 + 64 * a:18464 + 64 * (a + 1)].bitcast(BF16) for a in range(2)]
            identb = scr[:, 18592:18656].bitcast(BF16)
            rinv = scr[:, 18656:18660]
            ocb, _ = SM_OFF['colb']
            oid, _ = SM_OFF['ident']
            bar = list(p.bar)
            p.dma('pool', [lambda e: e.dma_start(out=ctxK, in_=ctxk_d.rearrange("(c p) n -> p c n", p=128)),
                           lambda e: e.dma_start(out=ctxVtmp, in_=ctxv_d.rearrange("(t p) f -> p t f", p=128))],
                  'CTX', writes=['ctx', ('kst', 0), ('kst', 1)], extra=bar)
            p.op('dve', lambda e: e.tensor_copy(out=identb, in_=sm[:, oid:oid + 128]), reads=['sm'], writes=['identb'])
            p.op('dve', lambda e: e.memset(QTm[64:128, :, 0, :], 0.0), writes=['QT'])
            p.op('dve', lambda e: e.memset(QTm[0:64, :, 1, :], 0.0), writes=['QT'])
            p.op('dve', lambda e: e.memset(Vaug[:, :, :, 64:65], 1.0), writes=['Vc'])
            p.op('dve', lambda e: e.memset(ctxVaug[:, :, :, 64:65], 1.0), writes=['ctxV'])
            p.op('dve', lambda e: e.tensor_copy(
                out=ctxVaug[:, :, :, 0:64], in_=ctxVtmp.rearrange("p t (h f) -> p t h f", h=16)),
                reads=['ctx', ('kst', 0), ('kst', 1)], writes=['ctxV'])
            kv = k_out.rearrange("(c p) t -> p c t", p=128)
            vv = v_out.rearrange("(t p) f -> p t f", p=128)
            cnt = {'ss': 0, 'ks': 0, 'vs': 0}
            tiles = {}
            for c in range(NCH):
                cl = c % 2
                if cl == 0:
                    cp = c // 2
                    nqk = ring.acquire(lambda slot, cp=cp: [
                        (slot[:, 0:4096].rearrange("p (k n) -> p k n", k=8)[:, :, 0:256],
                         na_w_qkv[0, :, cp * 256:(cp + 1) * 256].rearrange("(k p) n -> p k n", p=128)),
                        (slot[:, 0:4096].rearrange("p (k n) -> p k n", k=8)[:, :, 256:512],
                         na_w_qkv[0, :, D + cp * 256:D + (cp + 1) * 256].rearrange("(k p) n -> p k n", p=128))])
                    nv = ring.acquire(lambda slot, cp=cp: [(
                        slot[:, 0:2048].rearrange("p (k n) -> p k n", k=8),
                        na_w_qkv[0, :, 2 * D + cp * 256:2 * D + (cp + 1) * 256].rearrange("(k p) n -> p k n", p=128))])
                    tiles['qk'], tiles['v'] = nqk, nv
                nqk, nv = tiles['qk'], tiles['v']
                if p.dry:
                    if cl == 1:
                        ring.release(nqk)
                        ring.release(nv)
                    continue
                wqk = ring.ap(nqk)[:, 0:4096].rearrange("p (k n) -> p k n", k=8)
                wvt = ring.ap(nv)[:, 0:2048].rearrange("p (k n) -> p k n", k=8)
                qo = cl * 128
                ko = 256 + cl * 128
                p.dma('pool', [lambda e, c=c: e.dma_start(
                    out=scr[:, 12304:13456].bitcast(BF16).rearrange("p (a n) -> p a n", a=2),
                    in_=fbias[c].rearrange("p (a n) -> p a n", a=2))], 'LDF', writes=['F'], extra=bar)
                for tb in range(NTB):
                    bank = tb
                    for k in range(NCH):
                        p.op('pe', lambda e, k=k, bank=bank, tb=tb, wqk=wqk, qo=qo: e.matmul(
                            ps[:, bank, :], wqk[:, k, qo:qo + 128], h[:, k, tb * TB:(tb + 1) * TB],
                            start=(k == 0), stop=(k == NCH - 1)),
                            reads=[ring.key(nqk), HK(k, tb)], writes=[('ps', bank)], inc=(k == NCH - 1))
                    for hh in range(2):
                        p.op('act', lambda e, bank=bank, tb=tb, hh=hh: e.activation(
                            out=QTm[64 * hh:64 * hh + 64, 4 * tb:4 * tb + 4, hh, :],
                            in_=ps[64 * hh:64 * hh + 64, bank, :].rearrange("p (i q) -> p i q", i=4),
                            func=AF.Identity, scale=0.125),
                            reads=[('ps', bank)], writes=['QT'])
                for tb in range(NTB):
                    bank = tb
                    for k in range(NCH):
                        p.op('pe', lambda e, k=k, bank=bank, tb=tb, wqk=wqk, ko=ko: e.matmul(
                            ps[:, bank, :], wqk[:, k, ko:ko + 128], h[:, k, tb * TB:(tb + 1) * TB],
                            start=(k == 0), stop=(k == NCH - 1)),
                            reads=[ring.key(nqk), HK(k, tb)], writes=[('ps', bank)], inc=(k == NCH - 1))
                    ks = cnt['ks'] % 2
                    cnt['ks'] += 1
                    p.op('act', lambda e, bank=bank, ks=ks: e.activation(out=kst[ks], in_=ps[:, bank, :], func=AF.Copy),
                         reads=[('ps', bank)], writes=[('kst', ks)])
                    p.op('dve', lambda e, ks=ks, tb=tb: e.tensor_copy(
                        out=KT[:, tb * TB:(tb + 1) * TB], in_=kst[ks]),
                        reads=[('kst', ks)], writes=['KT'])
                    p.dma('sp', [lambda e, c=c, tb=tb, ks=ks: e.dma_start(
                        out=kv[:, c, tb * TB:(tb + 1) * TB], in_=kst[ks])], 'STK%d' % ks, reads=[('kst', ks)])
                for t8 in range(8):
                    bank = t8 % 4
                    for tt in range(2):
                        tok = (t8 * 2 + tt) * 128
                        for k in range(NCH):
                            p.op('pe', lambda e, k=k, bank=bank, tt=tt, tok=tok, wvt=wvt, qo=qo: e.matmul(
                                ps[:, bank, tt * 128:(tt + 1) * 128], h[:, k, tok:tok + 128], wvt[:, k, qo:qo + 128],
                                start=(k == 0), stop=(k == NCH - 1)),
                                reads=[ring.key(nv), HK(k, tok // TB)], writes=[('ps', bank)], inc=(k == NCH - 1))
                    vs = cnt['vs'] % 2
                    cnt['vs'] += 1
                    p.op('act', lambda e, bank=bank, vs=vs: e.activation(
                        out=vst[vs], in_=ps[:, bank, 0:256].rearrange("p (t f) -> p t f", t=2), func=AF.Copy),
                        reads=[('ps', bank)], writes=[('vst', vs)])
                    p.op('dve', lambda e, vs=vs, t8=t8: e.tensor_copy(
                        out=Vaug[:, t8 * 2:(t8 + 1) * 2, :, 0:64],
                        in_=vst[vs].rearrange("p t (h f) -> p t h f", h=2)),
                        reads=[('vst', vs)], writes=['Vc'])
                    p.dma('sp', [lambda e, c=c, t8=t8, vs=vs, tt=tt: e.dma_start(
                        out=vv[:, t8 * 2 + tt, c * 128:(c + 1) * 128], in_=vst[vs][:, tt, :])
                        for tt in range(2)], 'STV%d' % vs, reads=[('vst', vs)])
                if cl == 1:
                    ring.release(nqk)
                    ring.release(nv)
                items = []
                for i in (range(16) if NA_DEBUG >= 2 else []):
                    chunks = [('loc', i + d, sl, d + 3) for (d, sl) in att_chunks(i)] + \
                             [('ctx', 0, None, 7), ('ctx', 1, None, 7)]
                    for ci, ch in enumerate(chunks):
                        items.append((i, ci, len(chunks), ch))
                LOOK = 6
                base = cnt['ss']
                cnt['ss'] += len(items)

                def emit_S(n, c=c):
                    i, ci, nchk, (kind, j, sl, cbi) = items[n]
                    gi = base + n
                    sb_ = gi % 5
                    psS = ps[:, sb_, 0:256].rearrange("p (h q) -> p h q", h=2)
                    skey = ('ps', sb_)
                    if kind == 'loc':
                        kl = KT[:, j * 128:(j + 1) * 128]
                        rd = ['KT', 'QT']
                    else:
                        kl = ctxK[:, c, j * 128:(j + 1) * 128]
                        rd = ['ctx', 'QT']
                    p.op('pe', lambda e: e.matmul(psS, kl, QTm[:, i, :, :],
                                                  start=True, stop=True), reads=rd, writes=[skey])
                    pt = gi % NPT
                    cb = sm[:, ocb + i * 8 + cbi:ocb + i * 8 + cbi + 1]
                    if kind == 'loc':
                        ts = gi % 2
                        p.op('dve', lambda e: e.tensor_tensor(out=tS[ts], in0=psS, in1=Fb[:, :, sl, :], op=ALU.add),
                             reads=[skey, 'F'], writes=[('tS', ts)])
                        p.op('act', lambda e: e.activation(out=PT[pt], in_=tS[ts], func=AF.Exp, bias=cb),
                             reads=[('tS', ts), 'sm'], writes=[('PT', pt)])
                    else:
                        p.op('act', lambda e: e.activation(out=PT[pt], in_=psS, func=AF.Exp, bias=cb),
                             reads=[skey, 'sm'], writes=[('PT', pt)])

                def emit_PV(n, c=c):
                    i, ci, nchk, (kind, j, sl, cbi) = items[n]
                    gi = base + n
                    pt = gi % NPT
                    for hh in range(2):
                        ob = 5 + hh
                        psO = ps[:, ob, 0:65]
                        if kind == 'loc':
                            vl = Vaug[:, j, hh, :]
                            rdv = 'Vc'
                        else:
                            vl = ctxVaug[:, j, 2 * c + hh, :]
                            rdv = 'ctxV'
                        p.op('pe', lambda e, psO=psO, vl=vl, hh=hh: e.matmul(
                            psO, PT[pt][:, hh, :], vl, start=(ci == 0), stop=(ci == nchk - 1)),
                            reads=[rdv, ('PT', pt)], writes=[('ps', ob)])
                    if ci == nchk - 1:
                        ot = otok[i % 2]
                        tbk = 7
                        psT = ps[:, tbk, 0:64].bitcast(BF16)

                        def tail1():
                            for hh in range(2):
                                ob = 5 + hh
                                psO = ps[:, ob, 0:65]
                                p.op('dve', lambda e, psO=psO, hh=hh: e.reciprocal(out=rinv[:, hh:hh + 1],
                                                                                   in_=psO[:, 64:65]),
                                     reads=[('ps', ob)], writes=[('rinv', hh)])
                                p.op('dve', lambda e, psO=psO, hh=hh: e.tensor_scalar(
                                    out=ot[:, hh * 64:(hh + 1) * 64], in0=psO[:, 0:64], scalar1=rinv[:, hh:hh + 1],
                                    scalar2=None, op0=ALU.mult),
                                    reads=[('ps', ob), ('rinv', hh)], writes=[('otok', i % 2)])

                        def tail2():
                            p.op('pe', lambda e: e.transpose(out=psT, in_=ot, identity=identb),
                                 reads=[('otok', i % 2), 'identb'], writes=[('ps', tbk)])

                        def tail3():
                            p.op('dve', lambda e: e.tensor_copy(out=oT[:, c, i * 128:(i + 1) * 128], in_=psT),
                                 reads=[('ps', tbk)], writes=[('oT', c)])
                        tail1()
                        deferred.append((n + LOOK + 2, tail2))
                        deferred.append((n + LOOK + 4, tail3))

                deferred = []
                nsteps = len(items) + LOOK + 6
                for n in range(nsteps):
                    if n < len(items):
                        emit_S(n)
                    if LOOK <= n < len(items) + LOOK:
                        emit_PV(n - LOOK)
                    while deferred and deferred[0][0] <= n:
                        deferred.pop(0)[1]()
                assert not deferred
            for hf in (range(2) if NA_DEBUG >= 3 else []):
                no = ring.acquire(lambda slot, hf=hf: [(
                    slot[:, 0:4096].rearrange("p (k n) -> p k n", k=8),
                    na_w_o[0, :, hf * 512:(hf + 1) * 512].rearrange("(k p) n -> p k n", p=128))])
                if p.dry:
                    ring.release(no)
                    continue
                wo = ring.ap(no).rearrange("p (k n) -> p k n", k=8)
                for tb in range(NTB):
                    for o4 in range(4):
                        oc = hf * 4 + o4
                        bank = (tb * 4 + o4) % 4
                        for k in range(NCH):
                            p.op('pe', lambda e, k=k, bank=bank, tb=tb, wo=wo, o4=o4: e.matmul(
                                ps[:, bank, :], wo[:, k, o4 * 128:(o4 + 1) * 128], oT[:, k, tb * TB:(tb + 1) * TB],
                                start=(k == 0), stop=(k == NCH - 1)),
                                reads=[ring.key(no), ('oT', k)], writes=[('ps', bank)], inc=(k == NCH - 1))
                        p.op('dve', lambda e, oc=oc, bank=bank, tb=tb: e.scalar_tensor_tensor(
                            out=xT[:, oc, tb * TB:(tb + 1) * TB], in0=ps[:, bank, :],
                            scalar=mod[:, l, 16 + oc:17 + oc], in1=xT[:, oc, tb * TB:(tb + 1) * TB],
                            op0=ALU.mult, op1=ALU.add),
                            reads=[('ps', bank), XK(oc, tb), ('mod', l)], writes=[XK(oc, tb)])
                ring.release(no)

        def state_out():
            p.dma('sp', [lambda e: e.dma_start(out=st_out[:, :], in_=stt[:])], 'STS', reads=['stt'])

        def final_out(gain=True):
            sq = scr_bf16(0, 8 * 512).rearrange("p (c n) -> p c n", c=8)
            rstd = scr_f32(2048, 512)
            tmps = [scr_f32(2560, 512), scr_f32(3072, 512)]
            stg = [scr_f32(4096, 512), scr_f32(4608, 512), scr_f32(5120, 512), scr_f32(5632, 512)]
            yv = y_out.rearrange("(c p) t -> p c t", p=128)
            it = 0
            for tb in range(NTB):
                if gain:
                    rms_stats(tb, sq, rstd)
                for c in range(NCH):
                    tmp = tmps[c % 2]
                    tk = ('tmp', c % 2)
                    st = stg[it % 4]
                    sk = ('stg', it % 4)
                    ssem = 'ST%d' % (it % 4)
                    it += 1
                    if gain:
                        p.op('dve', lambda e, c=c, tmp=tmp, tb=tb: e.tensor_tensor(
                            out=tmp, in0=xT[:, c, tb * TB:(tb + 1) * TB], in1=rstd, op=ALU.mult),
                            reads=[XK(c, tb), 'rstd'], writes=[tk])
                        p.op('act', lambda e, c=c, tmp=tmp, st=st: e.activation(
                            out=st, in_=tmp, func=AF.Identity, scale=S('fin_g', c)),
                            reads=[tk, 'sm'], writes=[sk])
                    else:
                        p.op('act', lambda e, c=c, st=st, tb=tb: e.activation(
                            out=st, in_=xT[:, c, tb * TB:(tb + 1) * TB], func=AF.Copy),
                            reads=[XK(c, tb)], writes=[sk])
                    p.dma('sp', [lambda e, c=c, st=st, tb=tb: e.dma_start(
                        out=yv[:, c, tb * TB:(tb + 1) * TB], in_=st)], ssem, reads=[sk])
            if not p.dry:
                fin = [(s_, v) for s_, v in p.cnt.items() if s_.startswith('ST')]
                p._waits('sp', fin)

        def plan():
            bank_ctr[0] = 0
            setup()
            lru_consts()
            if STOP_AFTER == ('setup',):
                final_out(gain=False)
                return
            for l in (range(4) if ONLY_LAYER is None else [ONLY_LAYER]):
                if l == 0 or ONLY_LAYER is not None:
                    modulation(l)
                if STOP_AFTER == ('mod', l):
                    final_out(gain=False)
                    return
                norm_mod(l, 0)
                if STOP_AFTER == ('norm', l):
                    final_out(gain=False)
                    return
                p.barrier()
                if l % 3 == 0:
                    lru_mixer(l, l // 3)
                elif l % 3 == 1:
                    conf_mixer(l)
                else:
                    na_mixer(l)
                p.barrier()
                if STOP_AFTER == ('mix', l):
                    final_out(gain=False)
                    return
                norm_mod(l, 24)
                p.barrier()
                ffn(l)
                p.barrier()
                if STOP_AFTER == ('ffn', l):
                    final_out(gain=False)
                    return
            state_out()
            final_out(gain=True)

        p.dry = True
        plan()
        p.dry = False
        p.reset()
        ring.start_real()
        plan()

        sem_names = sorted(p.cnt.keys())
        sems = {nm: es.enter_context(nc.semaphore(nm)) for nm in sem_names}
        block = es.enter_context(nc.Block())

        def replay(stream):
            def run(e):
                for it in stream:
                    if it[0] == 'wait':
                        e.wait_ge(sems[it[1]], it[2])
                    else:
                        ins = it[1](e)
                        if it[2] is not None:
                            ins.then_inc(sems[it[2]], it[3])
            return run

        block.tensor(replay(p.streams['pe']))
        block.scalar(replay(p.streams['act']))
        block.vector(replay(p.streams['dve']))
        block.gpsimd(replay(p.streams['pool']))
        block.sync(replay(p.streams['sp']))
    return nc


def make_smalls(cond, m, h0, inp, is_sample):
    a = np.zeros((128, SM_N), np.float32)

    def put(name, v):
        o, n = SM_OFF[name]
        v = np.asarray(v, np.float32).reshape(128, -1)
        assert v.shape[1] == n, (name, v.shape, n)
        a[:, o:o + n] = v
    put('cond', _fm(cond))
    put('bmod', np.stack([inp['b_mod'][l].reshape(48, 128).T for l in range(4)], axis=1))
    put('lru_cw', _fm(inp['lru_conv_w']).reshape(128, 2, 4, 8).transpose(0, 1, 3, 2))
    put('lru_cb', _fm(inp['lru_conv_b']))
    put('lru_ba', _fm(inp['lru_b_a']))
    put('lru_bx', _fm(inp['lru_b_x']))
    put('lru_lam', _fm(inp['lru_lambda']))
    put('h0', _fm(h0))
    put('cf_b1', inp['conf_b_pw1'][0].reshape(16, 128).T)
    put('cf_dw', _fm(inp['conf_dw_w'][0]).reshape(128, 31, 8).transpose(0, 2, 1))
    put('cf_db', _fm(inp['conf_dw_b'][0]))
    put('cf_lg', _fm(inp['conf_ln_g'][0]))
    put('cf_lb', _fm(inp['conf_ln_b'][0]))
    put('cf_b2', _fm(inp['conf_b_pw2'][0]))
    put('fin_g', _fm(inp['final_g']))
    put('m', np.full((128, 8), m, np.float32))
    put('colb', build_colb(is_sample))
    put('ident', np.eye(128, dtype=np.float32))
    put('eps', np.full((128, 8), EPS, np.float32))
    put('one', np.ones((128, 8), np.float32))
    put('q25', np.full((128, 8), 0.25, np.float32))
    return a


def build_fbias(rpb, is_sample):
    if not is_sample:
        return np.zeros((8, 128, 2304), np.float32)
    kr = np.arange(128) // 64
    kc = np.arange(128) % 64
    qr = np.arange(128) // 64
    qc = np.arange(128) % 64
    cs = np.clip(qc - 8, 0, 48)
    colmask = (kc[:, None] >= cs[None, :]) & (kc[:, None] < cs[None, :] + 16)
    dcol = np.clip(kc[:, None] - qc[None, :] + 15, 0, 30)
    out = np.full((16, 9, 128, 128), NEG, np.float32)
    for slot in range(9):
        delta = slot - 3 if slot < 7 else (-2 if slot == 7 else 2)
        dr = 2 * delta + kr[:, None] - qr[None, :]
        rowmask = (np.abs(dr) <= 7) if slot < 7 else ((dr >= -4) & (dr <= 3))
        mask = colmask & rowmask
        dri = np.clip(dr + 7, 0, 14)
        vals = rpb[:, dri, dcol]
        out[:, slot] = np.where(mask[None], vals, np.float32(NEG))
    out = out.reshape(8, 2, 9, 128, 128).transpose(0, 3, 1, 2, 4)
    return np.ascontiguousarray(out).reshape(8, 128, 2304)


def build_colb(is_sample):
    cb = np.zeros((16, 8), np.float32)
    if not is_sample:
        cb[:] = NEG
        for i in range(16):
            cb[i, 3] = 0.0
            if i % 2 == 0:
                cb[i, 4] = 0.0
            else:
                cb[i, 2] = 0.0
    return np.broadcast_to(cb.reshape(1, 128), (128, 128)).copy()


_NC_CACHE = {}


def kernel(**inp):
    inp = {k: np.asarray(v) for k, v in inp.items()}
    if 'nc' not in _NC_CACHE:
        _NC_CACHE['nc'] = build_program()
    nc = _NC_CACHE['nc']
    in_maps = []
    fb_s = build_fbias(inp['na_rpb'][0], True)
    fb_p = build_fbias(inp['na_rpb'][0], False)
    for core in range(8):
        if core < 4:
            xs = inp['x_prompt'][core * 8:(core + 1) * 8].reshape(T, D)
            cond = inp['c_ctx']
            m = 0.0
            h0 = np.zeros((2, 2, D), np.float32)
        else:
            b = core - 4
            xs = inp['x_sample'][b]
            cond = inp['c'][b]
            m = 1.0
            h0 = inp['state_lru'][b]
        in_maps.append({
            'xin': np.ascontiguousarray(xs.T),
            'smalls': make_smalls(cond, m, h0, inp, core >= 4),
            'na_w_qkv': inp['na_w_qkv'], 'na_w_o': inp['na_w_o'],
            'fbias': fb_s if core >= 4 else fb_p,
            'ctxk': (np.ascontiguousarray(inp['cache_k'][core - 4, 0].reshape(256, D).T) if core >= 4
                     else np.zeros((D, 256), np.float32)),
            'ctxv': (np.ascontiguousarray(inp['cache_v'][core - 4, 0].reshape(256, D)) if core >= 4
                     else np.zeros((256, D), np.float32)),
            'w_mod': inp['w_mod'], 'w_ff1': inp['w_ff1'], 'w_ff2': inp['w_ff2'],
            'lru_w_in': inp['lru_w_in'], 'lru_w_a': inp['lru_w_a'], 'lru_w_x': inp['lru_w_x'],
            'lru_w_out': inp['lru_w_out'],
            'conf_w_pw1': inp['conf_w_pw1'], 'conf_w_pw2': inp['conf_w_pw2'],
        })
    res = run_bass_kernel_spmd(nc, in_maps, core_ids=list(range(8)))
    outs = [r['y_out'] for r in res.results]
    y_prompt = np.stack([o.T for o in outs[:4]]).reshape(32, 256, D)
    y_sample = np.stack([o.T for o in outs[4:]])
    new_state = np.zeros((32, 2, 2, D), np.float32)
    for core in range(4):
        if 'st_out' not in res.results[core]:
            break
        st = res.results[core]['st_out'].reshape(128, 2, 2, 8, 8)
        st = np.concatenate([st[:, :, 0:1], st[:, :, 1:2, :, ::-1]], axis=2)
        new_state[core * 8:(core + 1) * 8] = st.transpose(4, 1, 2, 3, 0).reshape(8, 2, 2, D)
    new_k = np.zeros((32, 1, 256, 16, 64), np.float32)
    new_v = np.zeros((32, 1, 256, 16, 64), np.float32)
    for core in range(4):
        if 'k_out' not in res.results[core]:
            break
        new_k[core * 8:(core + 1) * 8, 0] = res.results[core]['k_out'].T.reshape(8, 256, 16, 64)
        new_v[core * 8:(core + 1) * 8, 0] = res.results[core]['v_out'].reshape(8, 256, 16, 64)
    return y_prompt, y_sample, new_state, new_k, new_v
```

```python
import numpy as np
from contextlib import ExitStack
import concourse.bass as bass
import concourse.mybir as mybir
from concourse.bass_utils import run_bass_kernel_spmd

F32 = mybir.dt.float32
BF16 = mybir.dt.bfloat16
ALU = mybir.AluOpType
AF = mybir.ActivationFunctionType

T = 2048
D = 1024
NCH = 8
TB = 512
NTB = 4
DFF = 4096
EPS = 1e-6
NSEG = 8
SEG = 256
NEG = -30000.0

STOP_AFTER = None
ONLY_LAYER = None
NA_DEBUG = 3
NA_SKIP = set()


class Planner:
    ENG = ('pe', 'act', 'dve', 'pool', 'sp')

    def __init__(self):
        self.dry = False
        self.reset()

    def reset(self):
        self.streams = {e: [] for e in self.ENG}
        self.cnt = {}
        self.seen = {e: {} for e in self.ENG}
        self.lastw = {}
        self.readers = {}
        self.bar = []

    def _deps(self, reads, writes):
        d = []
        for k in reads:
            if k in self.lastw:
                d.append(self.lastw[k])
        for k in writes:
            if k in self.lastw:
                d.append(self.lastw[k])
            r = self.readers.get(k)
            if r:
                d.extend(r.items())
        return d

    def _waits(self, eng, deps):
        need = {}
        seen = self.seen[eng]
        own = 'E_' + eng
        for (sem, v) in deps:
            if sem == own and (eng == 'pe' or v > self.cnt.get(sem, 0)):
                continue
            if seen.get(sem, 0) < v and need.get(sem, 0) < v:
                need[sem] = v
        for sem, v in need.items():
            seen[sem] = v
            self.streams[eng].append(('wait', sem, v))

    def _mark(self, reads, writes, sem, v):
        for k in writes:
            self.lastw[k] = (sem, v)
            self.readers[k] = {}
        for k in reads:
            r = self.readers.setdefault(k, {})
            if r.get(sem, 0) < v:
                r[sem] = v

    def op(self, eng, fn, reads=(), writes=(), inc=True):
        if self.dry:
            return
        self._waits(eng, self._deps(reads, writes))
        sem = 'E_' + eng
        v = self.cnt.get(sem, 0) + 1
        if inc:
            self.cnt[sem] = v
        self.streams[eng].append(('op', fn, sem if inc else None, 1))
        self._mark(reads, writes, sem, v)

    def dma(self, eng, fns, sem, reads=(), writes=(), extra=()):
        if self.dry:
            return
        self._waits(eng, self._deps(reads, writes) + list(extra))
        v = self.cnt.get(sem, 0) + 16 * len(fns)
        self.cnt[sem] = v
        for fn in fns:
            self.streams[eng].append(('op', fn, sem, 16))
        self._mark(reads, writes, sem, v)

    def barrier(self):
        if self.dry:
            return
        cur = [(s, v) for s, v in self.cnt.items() if s.startswith('E_') or s.startswith('ST')]
        for e in ('pe', 'act', 'dve'):
            self._waits(e, cur)
        self.bar = cur


class WRing:
    def __init__(self, p, slots):
        self.p = p
        self.slots = slots
        self.R = len(slots)
        self.reqs = []
        self.n_acq = 0
        self.n_dma = 0

    def start_real(self):
        self.n_acq = 0
        self.n_dma = 0

    def _emit(self):
        n = self.n_dma
        if n >= len(self.reqs):
            return
        self.n_dma += 1
        slot = n % self.R
        pairs = self.reqs[n](self.slots[slot])
        if not pairs:
            return
        fns = []
        for (o, i) in pairs:
            fns.append(lambda e, o=o, i=i: e.dma_start(out=o, in_=i))
        self.p.dma('pool', fns, 'W%d' % slot, reads=(), writes=[('W', slot)])

    def acquire(self, desc):
        if self.p.dry:
            self.reqs.append(desc)
            return len(self.reqs) - 1
        n = self.n_acq
        self.n_acq += 1
        if n == 0:
            for _ in range(self.R):
                self._emit()
        assert self.n_dma > n
        return n

    def key(self, n):
        return ('W', n % self.R)

    def ap(self, n):
        return self.slots[n % self.R]

    def release(self, n):
        if self.p.dry:
            return
        if n + self.R == self.n_dma:
            self._emit()


def _smalls_layout():
    off = {}
    cur = 0

    def add(name, n):
        nonlocal cur
        off[name] = (cur, n)
        cur += n
    add('cond', 8)
    add('bmod', 4 * 48)
    add('lru_cw', 2 * 8 * 4)
    add('lru_cb', 2 * 8)
    add('lru_ba', 2 * 2 * 8)
    add('lru_bx', 2 * 2 * 8)
    add('lru_lam', 2 * 2 * 8)
    add('h0', 2 * 2 * 8)
    add('cf_b1', 16)
    add('cf_dw', 8 * 31)
    add('cf_db', 8)
    add('cf_lg', 8)
    add('cf_lb', 8)
    add('cf_b2', 8)
    add('fin_g', 8)
    add('m', 8)
    add('colb', 16 * 8)
    add('ident', 128)
    add('eps', 8)
    add('one', 8)
    add('q25', 8)
    return off, cur


SM_OFF, SM_N = _smalls_layout()


def _fm(v):
    v = np.asarray(v, np.float32)
    lead = v.shape[:-1]
    a = v.reshape(lead + (8, 128))
    a = np.moveaxis(a, -1, 0)
    return np.ascontiguousarray(a).reshape(128, -1)


def build_program():
    nc = bass.Bass("TRN2", target_bir_lowering=False)
    dt_in = {}

    def din(name, shape):
        dt_in[name] = nc.dram_tensor(name, list(shape), F32, kind="ExternalInput").ap()
        return dt_in[name]

    def dout(name, shape):
        return nc.dram_tensor(name, list(shape), F32, kind="ExternalOutput").ap()

    xin = din('xin', [D, T])
    smalls_d = din('smalls', [128, SM_N])
    w_mod = din('w_mod', [4, D, 6 * D])
    w_ff1 = din('w_ff1', [4, D, DFF])
    w_ff2 = din('w_ff2', [4, DFF, D])
    lru_w_in = din('lru_w_in', [2, D, 2 * D])
    lru_w_a = din('lru_w_a', [2, 2, 4, 256, 256])
    lru_w_x = din('lru_w_x', [2, 2, 4, 256, 256])
    lru_w_out = din('lru_w_out', [2, D, D])
    st_out = dout('st_out', [128, 256])
    conf_w_pw1 = din('conf_w_pw1', [1, D, 2 * D])
    conf_w_pw2 = din('conf_w_pw2', [1, D, D])
    na_w_qkv = din('na_w_qkv', [1, D, 3 * D])
    na_w_o = din('na_w_o', [1, D, D])
    fbias = din('fbias', [8, 128, 2304])
    ctxk_d = din('ctxk', [D, 256])
    ctxv_d = din('ctxv', [256, D])
    k_out = dout('k_out', [D, T])
    v_out = dout('v_out', [T, D])
    y_out = dout('y_out', [D, T])

    p = Planner()
    es = ExitStack()
    with es:
        def sb(name, shape, dt):
            return es.enter_context(nc.sbuf_tensor(name, list(shape), dt))

        xT = sb('xT', [128, NCH, T], F32)
        h = sb('h', [128, NCH, T], BF16)
        wr = [sb('wr%d' % i, [128, 4096], BF16) for i in range(4)]
        sm = sb('smalls_sb', [128, SM_N], F32)
        mod = sb('mod', [128, 4, 48], F32)
        ones = sb('ones', [128, 128], BF16)
        scb = sb('scb', [128, 8], BF16)
        SCR = 18752
        stt = sb('stt', [128, 256], F32)
        cch = sb('cch', [128, 2, 32], F32)
        hbias = sb('hbias', [128, 2, 32], F32)
        scr = sb('scr', [128, SCR], F32)
        ps = es.enter_context(nc.psum_tensor('ps', [128, 8, 512], F32))

        ring = WRing(p, [w[:] for w in wr])

        def S(name, idx=None):
            o, n = SM_OFF[name]
            if idx is None:
                return sm[:, o:o + n]
            return sm[:, o + idx:o + idx + 1]

        bank_ctr = [0]

        def nbank():
            b = bank_ctr[0] % 8
            bank_ctr[0] += 1
            return b

        def XK(c, tb):
            return ('x', c, tb)

        def HK(c, tb):
            return ('h', c, tb)

        def setup():
            xv = xin.rearrange("(c p) t -> p c t", p=128)
            fns = []
            for c in range(NCH):
                fns.append(lambda e, c=c: e.dma_start(out=xT[:, c, :], in_=xv[:, c, :]))
            p.dma('sp', fns, 'LDX', writes=[XK(c, tb) for c in range(NCH) for tb in range(NTB)])
            p.dma('sp', [lambda e: e.dma_start(out=sm[:], in_=smalls_d[:, :])], 'LD0', writes=['sm'])
            p.op('dve', lambda e: e.memset(ones[:], 1.0), writes=['ones'])
            p.op('act', lambda e: e.activation(out=scb[:], in_=S('cond'), func=AF.Silu),
                 reads=['sm'], writes=['scb'])

        def mod_tile(l, cg, bank):
            n = ring.acquire(lambda slot, cg=cg: [(
                slot[:, 0:4096].rearrange("p (k n) -> p k n", k=8),
                w_mod[l, :, cg * 512:(cg + 1) * 512].rearrange("(k p) n -> p k n", p=128))])
            if not p.dry:
                wt = ring.ap(n).rearrange("p (k n) -> p k n", k=8)
                for j in range(4):
                    col = cg * 4 + j
                    for k in range(8):
                        p.op('pe', lambda e, wt=wt, j=j, k=k, col=col: e.matmul(
                            ps[:, bank, col:col + 1], wt[:, k, j * 128:(j + 1) * 128], scb[:, k:k + 1],
                            start=(k == 0), stop=(k == 7)),
                            reads=[ring.key(n), 'scb'], writes=[('ps', bank)], inc=(k == 7))
            ring.release(n)

        def mod_finish(l, bank):
            o, _ = SM_OFF['bmod']
            p.op('dve', lambda e: e.tensor_tensor(out=mod[:, l, :], in0=ps[:, bank, 0:48],
                                                   in1=sm[:, o + l * 48:o + (l + 1) * 48], op=ALU.add),
                 reads=[('ps', bank), 'sm'], writes=[('mod', l)])
            for a in (8, 32):
                p.op('dve', lambda e, a=a: e.tensor_scalar(out=mod[:, l, a:a + 8], in0=mod[:, l, a:a + 8],
                                                           scalar1=1.0, scalar2=None, op0=ALU.add),
                     reads=[('mod', l)], writes=[('mod', l)])

        def modulation(l):
            bank = nbank()
            for cg in range(12):
                mod_tile(l, cg, bank)
            mod_finish(l, bank)

        def scr_f32(off, n):
            return scr[:, off:off + n]

        def scr_bf16(off, n):
            return scr[:, off:off + n // 2].bitcast(BF16)

        def rms_stats(tb, sq, rstd):
            bank = nbank()
            for c in range(NCH):
                p.op('act', lambda e, c=c: e.activation(out=sq[:, c, :], in_=xT[:, c, tb * TB:(tb + 1) * TB],
                                                       func=AF.Square),
                     reads=[XK(c, tb)], writes=[('sq', c)])
            for c in range(NCH):
                p.op('pe', lambda e, c=c: e.matmul(ps[:, bank, :], ones[:], sq[:, c, :],
                                                  start=(c == 0), stop=(c == NCH - 1)),
                     reads=['ones', ('sq', c)], writes=[('ps', bank)], inc=(c == NCH - 1))
            p.op('act', lambda e: e.activation(out=rstd, in_=ps[:, bank, :], func=AF.Sqrt,
                                               bias=S('eps', 0), scale=1.0 / D),
                 reads=[('ps', bank), 'sm'], writes=['rstd'])
            p.op('dve', lambda e: e.reciprocal(out=rstd, in_=rstd), reads=['rstd'], writes=['rstd'])

        def norm_mod(l, a):
            sq = scr_bf16(0, 8 * 512).rearrange("p (c n) -> p c n", c=8)
            rstd = scr_f32(2048, 512)
            tmps = [scr_f32(2560, 512), scr_f32(3072, 512)]
            for tb in range(NTB):
                rms_stats(tb, sq, rstd)
                for c in range(NCH):
                    tmp = tmps[c % 2]
                    tk = ('tmp', c % 2)
                    p.op('dve', lambda e, c=c, tmp=tmp, tb=tb: e.tensor_tensor(
                        out=tmp, in0=xT[:, c, tb * TB:(tb + 1) * TB], in1=rstd, op=ALU.mult),
                        reads=[XK(c, tb), 'rstd'], writes=[tk])
                    p.op('act', lambda e, c=c, tmp=tmp, tb=tb: e.activation(
                        out=h[:, c, tb * TB:(tb + 1) * TB], in_=tmp, func=AF.Identity,
                        bias=mod[:, l, a + c:a + c + 1], scale=mod[:, l, a + 8 + c:a + 9 + c]),
                        reads=[tk, ('mod', l)], writes=[HK(c, tb)])

        def ffn(l):
            G = 4
            fctr = [0]

            def fbank():
                b = fctr[0] % 7
                fctr[0] += 1
                return b
            mod_todo = list(range(12)) if l + 1 < 4 else []
            hid = scr_bf16(0, G * T).rearrange("p (c n) -> p c n", c=G)
            rl = [scr_f32(4096, 512), scr_f32(4608, 512)]
            for g in range(DFF // (G * 128)):
                n1 = ring.acquire(lambda slot, g=g: [(
                    slot[:, 0:4096].rearrange("p (k n) -> p k n", k=8),
                    w_ff1[l, :, g * 512:(g + 1) * 512].rearrange("(k p) n -> p k n", p=128))])
                n2 = ring.acquire(lambda slot, g=g: [(
                    slot[:, 0:4096].rearrange("p (k n) -> p k n", k=4),
                    w_ff2[l, g * 512:(g + 1) * 512, :].rearrange("(k p) n -> p k n", p=128))])
                if not p.dry:
                    w1 = ring.ap(n1).rearrange("p (k n) -> p k n", k=8)
                    w2 = ring.ap(n2).rearrange("p (k n) -> p k n", k=4)
                    it = 0
                    for tb in range(NTB):
                        for c in range(G):
                            bank = fbank()
                            for k in range(NCH):
                                p.op('pe', lambda e, c=c, k=k, bank=bank, tb=tb, w1=w1: e.matmul(
                                    ps[:, bank, :], w1[:, k, c * 128:(c + 1) * 128], h[:, k, tb * TB:(tb + 1) * TB],
                                    start=(k == 0), stop=(k == NCH - 1)),
                                    reads=[ring.key(n1), HK(k, tb)], writes=[('ps', bank)], inc=(k == NCH - 1))
                            r = rl[it % 2]
                            rk = ('rl', it % 2)
                            it += 1
                            p.op('act', lambda e, r=r, bank=bank: e.activation(out=r, in_=ps[:, bank, :], func=AF.Relu),
                                 reads=[('ps', bank)], writes=[rk])
                            p.op('dve', lambda e, r=r, c=c, tb=tb: e.tensor_tensor(
                                out=hid[:, c, tb * TB:(tb + 1) * TB], in0=r, in1=r, op=ALU.mult),
                                reads=[rk], writes=[('hid', c, tb)])
                    ring.release(n1)
                    for tb in range(NTB):
                        for oc in range(NCH):
                            bank = fbank()
                            for k in range(G):
                                p.op('pe', lambda e, oc=oc, k=k, bank=bank, tb=tb, w2=w2: e.matmul(
                                    ps[:, bank, :], w2[:, k, oc * 128:(oc + 1) * 128], hid[:, k, tb * TB:(tb + 1) * TB],
                                    start=(k == 0), stop=(k == G - 1)),
                                    reads=[ring.key(n2), ('hid', k, tb)], writes=[('ps', bank)], inc=(k == G - 1))
                            p.op('dve', lambda e, oc=oc, bank=bank, tb=tb: e.scalar_tensor_tensor(
                                out=xT[:, oc, tb * TB:(tb + 1) * TB], in0=ps[:, bank, :],
                                scalar=mod[:, l, 40 + oc:41 + oc], in1=xT[:, oc, tb * TB:(tb + 1) * TB],
                                op0=ALU.mult, op1=ALU.add),
                                reads=[('ps', bank), XK(oc, tb), ('mod', l)], writes=[XK(oc, tb)])
                    ring.release(n2)
                else:
                    ring.release(n1)
                    ring.release(n2)
                for _ in range(2 if g % 2 == 0 else 1):
                    if mod_todo:
                        mod_tile(l + 1, mod_todo.pop(0), 7)
            if l + 1 < 4:
                mod_finish(l + 1, 7)


        def lru_consts():
            o, n = SM_OFF['lru_lam']
            p.op('act', lambda e: e.activation(out=cch[:, 0, :], in_=sm[:, o:o + n], func=AF.Exp, scale=-1.0),
                 reads=['sm'], writes=['cch'])
            p.op('act', lambda e: e.activation(out=cch[:, 0, :], in_=cch[:, 0, :], func=AF.Ln, bias=S('one', 0)),
                 reads=['cch', 'sm'], writes=['cch'])
            p.op('dve', lambda e: e.tensor_scalar(out=cch[:, 1, :], in0=cch[:, 0, :], scalar1=-8.0, scalar2=None,
                                                  op0=ALU.mult), reads=['cch'], writes=['cch'])
            p.op('dve', lambda e: e.tensor_scalar(out=cch[:, 0, :], in0=cch[:, 0, :], scalar1=-4.0, scalar2=None,
                                                  op0=ALU.mult), reads=['cch'], writes=['cch'])
            oa, na = SM_OFF['lru_ba']
            ox, nx = SM_OFF['lru_bx']
            p.op('dve', lambda e: e.tensor_scalar(out=hbias[:, 0, :], in0=sm[:, oa:oa + na], scalar1=0.5, scalar2=None,
                                                  op0=ALU.mult), reads=['sm'], writes=['hbias'])
            p.op('dve', lambda e: e.tensor_scalar(out=hbias[:, 1, :], in0=sm[:, ox:ox + nx], scalar1=0.5, scalar2=None,
                                                  op0=ALU.mult), reads=['sm'], writes=['hbias'])

        def lru_mixer(l, j):
            PADW = 259
            recp = scr[:, 0:2 * 8 * PADW].rearrange("p (c s w) -> p c s w", c=2, s=8)
            A = scr[:, 0:2048]
            B = scr[:, 2072:2072 + 2048]
            xf = scr[:, 4144:4144 + 4096].rearrange("p (c n) -> p c n", c=2)
            xfb = scr[:, 8240:8240 + 2048].bitcast(BF16).rearrange("p (c n) -> p c n", c=2)
            yb = scr[:, 10288:10288 + 2048].bitcast(BF16).rearrange("p (c n) -> p c n", c=2)
            C = scr[:, 12336:12336 + 2048]
            Hd = [scr[:, 14384:14384 + 2048], scr[:, 16432:16432 + 2048]]
            m_ap = S('m', 0)
            hctr = [0]
            C = scr[:, 12336:12336 + 2048]
            def block(n):
                nin = ring.acquire(lambda slot, n=n: [
                    (slot[:, 0:4096].rearrange("p (k n) -> p k n", k=8)[:, :, 0:256],
                     lru_w_in[j, :, n * 256:(n + 1) * 256].rearrange("(k p) n -> p k n", p=128)),
                    (slot[:, 0:4096].rearrange("p (k n) -> p k n", k=8)[:, :, 256:512],
                     lru_w_in[j, :, D + n * 256:D + (n + 1) * 256].rearrange("(k p) n -> p k n", p=128))])
                nax = ring.acquire(lambda slot, n=n: [
                    (slot[:, 0:2048].rearrange("p (k m n) -> p k m n", k=2, m=4)[:, :, 2 * d + w, :],
                     (lru_w_a if w == 0 else lru_w_x)[j, d, n].rearrange("(k p) n -> p k n", p=128))
                    for d in range(2) for w in range(2)])
                nout = ring.acquire(lambda slot, n=n: [(
                    slot[:, 0:2048].rearrange("p (k n) -> p k n", k=2),
                    lru_w_out[j, n * 256:(n + 1) * 256, :].rearrange("(k p) n -> p k n", p=128))])
                if p.dry:
                    yield
                    yield
                    return
                win = ring.ap(nin).rearrange("p (k n) -> p k n", k=8)
                wax = ring.ap(nax)[:, 0:2048].rearrange("p (k m n) -> p k m n", k=2, m=4)
                wout = ring.ap(nout)[:, 0:2048].rearrange("p (k n) -> p k n", k=2)
                for cc in range(2):
                    for tb in range(NTB):
                        bank = nbank()
                        for k in range(NCH):
                            p.op('pe', lambda e, cc=cc, k=k, bank=bank, tb=tb, win=win: e.matmul(
                                ps[:, bank, :], win[:, k, 256 + cc * 128:256 + (cc + 1) * 128],
                                h[:, k, tb * TB:(tb + 1) * TB], start=(k == 0), stop=(k == NCH - 1)),
                                reads=[ring.key(nin), HK(k, tb)], writes=[('ps', bank)], inc=(k == NCH - 1))
                        p.op('act', lambda e, cc=cc, bank=bank, tb=tb: e.activation(
                            out=recp[:, cc, 2 * tb:2 * tb + 2, 1:257],
                            in_=ps[:, bank, :].rearrange("p (s w) -> p s w", s=2), func=AF.Copy),
                            reads=[('ps', bank)],
                            writes=[('recp', cc), ('A', 0), ('A', 1)] if cc == 0 else [('recp', cc), ('B', 0), ('B', 1)])
                    p.op('dve', lambda e, cc=cc: e.memset(recp[:, cc, 0, 0:1], 0.0), writes=[('recp', cc)])
                    p.op('dve', lambda e, cc=cc: e.memset(recp[:, cc, 7, 257:259], 0.0), writes=[('recp', cc)])
                    p.op('dve', lambda e, cc=cc: e.tensor_scalar(
                        out=recp[:, cc, 1:8, 0:1], in0=recp[:, cc, 0:7, 256:257], scalar1=m_ap, scalar2=None,
                        op0=ALU.mult), reads=['sm', ('recp', cc)], writes=[('recp', cc)])
                    p.op('dve', lambda e, cc=cc: e.tensor_scalar(
                        out=recp[:, cc, 0:7, 257:259], in0=recp[:, cc, 1:8, 1:3], scalar1=m_ap, scalar2=None,
                        op0=ALU.mult), reads=['sm', ('recp', cc)], writes=[('recp', cc)])
                    ch = 2 * n + cc
                    xfv = xf[:, cc, :].rearrange("p (s w) -> p s w", s=8)
                    ocw, _ = SM_OFF['lru_cw']
                    ocb, _ = SM_OFF['lru_cb']
                    wbase = ocw + (j * 8 + ch) * 4
                    p.op('dve', lambda e, cc=cc, xfv=xfv, wbase=wbase, ch=ch: e.tensor_scalar(
                        out=xfv, in0=recp[:, cc, :, 0:256], scalar1=sm[:, wbase:wbase + 1],
                        scalar2=sm[:, ocb + j * 8 + ch:ocb + j * 8 + ch + 1], op0=ALU.mult, op1=ALU.add),
                        reads=['sm', ('recp', cc)], writes=[('xf', cc)])
                    for k in range(1, 4):
                        p.op('dve', lambda e, cc=cc, xfv=xfv, wbase=wbase, k=k: e.scalar_tensor_tensor(
                            out=xfv, in0=recp[:, cc, :, k:k + 256], scalar=sm[:, wbase + k:wbase + k + 1], in1=xfv,
                            op0=ALU.mult, op1=ALU.add),
                            reads=['sm', ('recp', cc), ('xf', cc)], writes=[('xf', cc)])
                    p.op('act', lambda e, cc=cc: e.activation(out=xfb[:, cc, :], in_=xf[:, cc, :], func=AF.Copy),
                         reads=[('xf', cc)], writes=[('xfb', cc)])
                yield
                oba, _ = SM_OFF['lru_ba']
                obx, _ = SM_OFF['lru_bx']
                oh0, _ = SM_OFF['h0']
                HT = T // 2
                Aset = [scr[:, 0:1024], scr[:, 1024:2048]]
                Bset = [scr[:, 2072:3096], scr[:, 3096:4120]]
                Cset = [scr[:, 12336:13360], scr[:, 13360:14384]]
                for co in range(2):
                    ch = 2 * n + co
                    for d in range(2):
                        idx = (j * 2 + d) * 8 + ch
                        for hs in range(2):
                            Ah, Bh, Ch = Aset[hs], Bset[hs], Cset[hs]
                            AKs, BKs, CKs = ('A', hs), ('B', hs), ('C', hs)
                            if d == 0:
                                tbs = [2 * hs, 2 * hs + 1]
                            else:
                                tbs = [2 * (1 - hs) + 1, 2 * (1 - hs)]

                            def dst(buf, tb, d=d, hs=hs):
                                if d == 0:
                                    o_ = (tb - 2 * hs) * TB
                                    return buf[:, o_:o_ + TB]
                                hi = T - 1 - tb * TB - hs * HT
                                lo = hi - TB
                                return buf[:, hi:lo:-1] if lo >= 0 else buf[:, hi::-1]
                            for (w, buf, keys) in ((0, Ah, [AKs, ('recp', 0)]), (1, Bh, [BKs, ('recp', 1)])):
                                for tb in tbs:
                                    bank = nbank()
                                    for k in range(2):
                                        p.op('pe', lambda e, k=k, bank=bank, tb=tb, wax=wax, d=d, w=w, co=co: e.matmul(
                                            ps[:, bank, :], wax[:, k, 2 * d + w, co * 128:(co + 1) * 128],
                                            xfb[:, k, tb * TB:(tb + 1) * TB], start=(k == 0), stop=(k == 1)),
                                            reads=[ring.key(nax), ('xfb', k)], writes=[('ps', bank)], inc=(k == 1))
                                    p.op('act', lambda e, bank=bank, o=dst(buf, tb), w=w, idx=idx: e.activation(
                                        out=o, in_=ps[:, bank, :], func=AF.Tanh, bias=hbias[:, w, idx:idx + 1], scale=0.5),
                                        reads=[('ps', bank), 'hbias'], writes=keys)
                            p.op('act', lambda e, idx=idx, Ah=Ah, Ch=Ch: e.activation(
                                out=Ch, in_=Ah, func=AF.Exp, scale=cch[:, 1, idx:idx + 1], bias=cch[:, 1, idx:idx + 1]),
                                reads=[AKs, 'cch'], writes=[CKs])
                            p.op('act', lambda e, idx=idx, Ah=Ah: e.activation(
                                out=Ah, in_=Ah, func=AF.Exp, scale=cch[:, 0, idx:idx + 1], bias=cch[:, 0, idx:idx + 1]),
                                reads=[AKs, 'cch'], writes=[AKs])
                            if d == 0:
                                xsrc = xf[:, co, hs * HT:(hs + 1) * HT]
                            else:
                                hi = (2 - hs) * HT - 1
                                lo = (1 - hs) * HT - 1
                                xsrc = xf[:, co, hi:lo:-1] if lo >= 0 else xf[:, co, hi::-1]
                            p.op('dve', lambda e, xsrc=xsrc, Bh=Bh: e.scalar_tensor_tensor(
                                out=Bh, in0=Bh, scalar=1.0, in1=xsrc, op0=ALU.add, op1=ALU.mult),
                                reads=[BKs, ('xf', co)], writes=[BKs])
                        for hs in range(2):
                            Ch = Cset[hs]
                            p.op('act', lambda e, Ch=Ch: e.activation(out=Ch, in_=Ch, func=AF.Sqrt, bias=S('q25', 0),
                                                                      scale=-0.25),
                                 reads=[('C', hs), 'sm'], writes=[('C', hs)])
                        for hs in range(2):
                            Ah, Bh, Ch = Aset[hs], Bset[hs], Cset[hs]
                            AKs, BKs, CKs = ('A', hs), ('B', hs), ('C', hs)
                            p.op('dve', lambda e, Bh=Bh, Ch=Ch: e.tensor_tensor(out=Bh, in0=Bh, in1=Ch, op=ALU.mult),
                                 reads=[BKs, CKs], writes=[BKs])
                            p.op('dve', lambda e, Ah=Ah: e.tensor_scalar(
                                out=Ah[:, 0:HT:SEG], in0=Ah[:, 0:HT:SEG], scalar1=m_ap, scalar2=None, op0=ALU.mult),
                                reads=[AKs, 'sm'], writes=[AKs])
                            init = sm[:, oh0 + idx:oh0 + idx + 1] if hs == 0 else Hd[d][:, HT - 1:HT]
                            p.op('dve', lambda e, d=d, hs=hs, Ah=Ah, Bh=Bh, init=init: e.tensor_tensor_scan(
                                out=Hd[d][:, hs * HT:(hs + 1) * HT], data0=Ah, data1=Bh, initial=init,
                                op0=ALU.mult, op1=ALU.add), reads=[AKs, BKs, 'sm', ('H', d)], writes=[('H', d)])
                        so = ((j * 2 + d) * 8 + ch) * 8
                        p.op('dve', lambda e, d=d, so=so: e.tensor_copy(out=stt[:, so:so + 8],
                                                                        in_=Hd[d][:, SEG - 1:T:SEG]),
                             reads=[('H', d)], writes=['stt'])
                    for tb in range(NTB):
                        bank = nbank()
                        for k in range(NCH):
                            p.op('pe', lambda e, k=k, bank=bank, tb=tb, win=win, co=co: e.matmul(
                                ps[:, bank, :], win[:, k, co * 128:(co + 1) * 128],
                                h[:, k, tb * TB:(tb + 1) * TB], start=(k == 0), stop=(k == NCH - 1)),
                                reads=[ring.key(nin), HK(k, tb)], writes=[('ps', bank)], inc=(k == NCH - 1))
                        p.op('act', lambda e, bank=bank, tb=tb: e.activation(
                            out=C[:, tb * TB:(tb + 1) * TB], in_=ps[:, bank, :], func=AF.Gelu_apprx_tanh),
                            reads=[('ps', bank)], writes=[('C', 0), ('C', 1)])
                    p.op('dve', lambda e: e.tensor_tensor(out=Hd[0], in0=Hd[0], in1=Hd[1][:, ::-1], op=ALU.add),
                         reads=[('H', 0), ('H', 1)], writes=[('H', 0)])
                    p.op('dve', lambda e, co=co: e.tensor_tensor(out=yb[:, co, :], in0=Hd[0], in1=C, op=ALU.mult),
                         reads=[('H', 0), ('C', 0), ('C', 1)], writes=[('yb', co)])
                ring.release(nin)
                ring.release(nax)
                yield
                for tb in range(NTB):
                    for oc in range(NCH):
                        bank = nbank()
                        for k in range(2):
                            p.op('pe', lambda e, oc=oc, k=k, bank=bank, tb=tb, wout=wout: e.matmul(
                                ps[:, bank, :], wout[:, k, oc * 128:(oc + 1) * 128], yb[:, k, tb * TB:(tb + 1) * TB],
                                start=(k == 0), stop=(k == 1)),
                                reads=[ring.key(nout), ('yb', k)], writes=[('ps', bank)], inc=(k == 1))
                        p.op('dve', lambda e, oc=oc, bank=bank, tb=tb: e.scalar_tensor_tensor(
                            out=xT[:, oc, tb * TB:(tb + 1) * TB], in0=ps[:, bank, :],
                            scalar=mod[:, l, 16 + oc:17 + oc], in1=xT[:, oc, tb * TB:(tb + 1) * TB],
                            op0=ALU.mult, op1=ALU.add),
                            reads=[('ps', bank), XK(oc, tb), ('mod', l)], writes=[XK(oc, tb)])
                ring.release(nout)

            def finish(g):
                for _ in g:
                    pass
            gens = [block(n) for n in range(4)]
            next(gens[0])
            next(gens[0])
            for n in range(1, 4):
                next(gens[n])
                finish(gens[n - 1])
                next(gens[n])
            finish(gens[3])


        def conf_mixer(l):
            PW = 286
            zp = scr[:, 0:9152].bitcast(BF16).rearrange("p (c s w) -> p c s w", c=8, s=8)
            hflat = h[:].rearrange("p c n -> p (c n)")

            def zc(c):
                if c < 4:
                    return hflat[:, c * 4096:(c + 1) * 4096].bitcast(F32)
                return scr[:, 9152 + (c - 4) * 2048:9152 + (c - 3) * 2048]

            def zs(c):
                if c < 4:
                    return hflat[:, c * 4096:c * 4096 + 2048]
                return scr[:, 9152 + (c - 4) * 2048:9152 + (c - 4) * 2048 + 1024].bitcast(BF16)
            sg = [scr[:, 17344:17344 + 512], scr[:, 17856:17856 + 512]]
            m_ap = S('m', 0)
            ob1, _ = SM_OFF['cf_b1']
            it = 0
            for g in range(4):
                n1 = ring.acquire(lambda slot, g=g: [
                    (slot[:, 0:4096].rearrange("p (k n) -> p k n", k=8)[:, :, 0:256],
                     conf_w_pw1[0, :, g * 256:(g + 1) * 256].rearrange("(k p) n -> p k n", p=128)),
                    (slot[:, 0:4096].rearrange("p (k n) -> p k n", k=8)[:, :, 256:512],
                     conf_w_pw1[0, :, D + g * 256:D + (g + 1) * 256].rearrange("(k p) n -> p k n", p=128))])
                if p.dry:
                    ring.release(n1)
                    continue
                w1 = ring.ap(n1).rearrange("p (k n) -> p k n", k=8)
                for cc in range(2):
                    c = 2 * g + cc
                    for tb in range(NTB):
                        bv = nbank()
                        for k in range(NCH):
                            p.op('pe', lambda e, cc=cc, k=k, bv=bv, tb=tb, w1=w1: e.matmul(
                                ps[:, bv, :], w1[:, k, cc * 128:(cc + 1) * 128], h[:, k, tb * TB:(tb + 1) * TB],
                                start=(k == 0), stop=(k == NCH - 1)),
                                reads=[ring.key(n1), HK(k, tb)], writes=[('ps', bv)], inc=(k == NCH - 1))
                        bg = nbank()
                        for k in range(NCH):
                            p.op('pe', lambda e, cc=cc, k=k, bg=bg, tb=tb, w1=w1: e.matmul(
                                ps[:, bg, :], w1[:, k, 256 + cc * 128:256 + (cc + 1) * 128],
                                h[:, k, tb * TB:(tb + 1) * TB], start=(k == 0), stop=(k == NCH - 1)),
                                reads=[ring.key(n1), HK(k, tb)], writes=[('ps', bg)], inc=(k == NCH - 1))
                        sgt = sg[it % 2]
                        sgk = ('sg', it % 2)
                        it += 1
                        p.op('act', lambda e, bg=bg, sgt=sgt, c=c: e.activation(
                            out=sgt, in_=ps[:, bg, :], func=AF.Sigmoid, bias=sm[:, ob1 + 8 + c:ob1 + 9 + c]),
                            reads=[('ps', bg), 'sm'], writes=[sgk])
                        p.op('dve', lambda e, bv=bv, sgt=sgt, c=c, tb=tb: e.scalar_tensor_tensor(
                            out=zp[:, c, 2 * tb:2 * tb + 2, 15:271],
                            in0=ps[:, bv, :].rearrange("p (s w) -> p s w", s=2), scalar=sm[:, ob1 + c:ob1 + c + 1],
                            in1=sgt.rearrange("p (s w) -> p s w", s=2), op0=ALU.add, op1=ALU.mult),
                            reads=[('ps', bv), sgk, 'sm'], writes=[('zp', c)])
                    p.op('dve', lambda e, c=c: e.memset(zp[:, c, 0, 0:15], 0.0), writes=[('zp', c)])
                    p.op('dve', lambda e, c=c: e.memset(zp[:, c, 7, 271:286], 0.0), writes=[('zp', c)])
                    p.op('dve', lambda e, c=c: e.tensor_scalar(
                        out=zp[:, c, 1:8, 0:15], in0=zp[:, c, 0:7, 256:271], scalar1=m_ap, scalar2=None,
                        op0=ALU.mult), reads=['sm', ('zp', c)], writes=[('zp', c)])
                    p.op('dve', lambda e, c=c: e.tensor_scalar(
                        out=zp[:, c, 0:7, 271:286], in0=zp[:, c, 1:8, 15:30], scalar1=m_ap, scalar2=None,
                        op0=ALU.mult), reads=['sm', ('zp', c)], writes=[('zp', c)])
                ring.release(n1)
            p.barrier()
            odw, _ = SM_OFF['cf_dw']
            odb, _ = SM_OFF['cf_db']
            oid, _ = SM_OFF['ident']
            for c in range(NCH):
                nd = ring.acquire(lambda slot: [])
                if p.dry:
                    ring.release(nd)
                    continue
                dg = ring.ap(nd)[:, 0:31 * 128].rearrange("p (k n) -> p k n", k=31)
                for k in range(31):
                    p.op('dve', lambda e, k=k, dg=dg, c=c: e.tensor_scalar(
                        out=dg[:, k, :], in0=sm[:, oid:oid + 128], scalar1=sm[:, odw + c * 31 + k:odw + c * 31 + k + 1],
                        scalar2=None, op0=ALU.mult), reads=['sm'], writes=[ring.key(nd)])
                for tb in range(NTB):
                    bank = nbank()
                    for k in range(31):
                        p.op('pe', lambda e, k=k, dg=dg, c=c, tb=tb, bank=bank: e.matmul(
                            ps[:, bank, :], dg[:, k, :], zp[:, c, 2 * tb:2 * tb + 2, k:k + 256],
                            start=(k == 0), stop=(k == 30)),
                            reads=[ring.key(nd), ('zp', c)], writes=[('ps', bank)], inc=(k == 30))
                    p.op('act', lambda e, c=c, tb=tb, bank=bank: e.activation(
                        out=zc(c)[:, tb * TB:(tb + 1) * TB], in_=ps[:, bank, :], func=AF.Identity,
                        bias=sm[:, odb + c:odb + c + 1]), reads=[('ps', bank), 'sm'], writes=[('zc', c)])
                ring.release(nd)
            p.barrier()
            zcb = scr[:, 0:2048].bitcast(BF16).rearrange("p (c n) -> p c n", c=8)
            sqz = scr[:, 2048:4096].bitcast(BF16).rearrange("p (c n) -> p c n", c=8)
            mean = scr[:, 4096:4608]
            rstd = scr[:, 4608:5120]
            t1 = [scr[:, 5120:5632], scr[:, 5632:6144]]
            olg, _ = SM_OFF['cf_lg']
            olb, _ = SM_OFF['cf_lb']
            ob2, _ = SM_OFF['cf_b2']
            tmp2 = [scr[:, 6144:6656], scr[:, 6656:7168]]
            n2s = []
            for hf in range(2):
                n2s.append(ring.acquire(lambda slot, hf=hf: [(
                    slot[:, 0:4096].rearrange("p (k n) -> p k n", k=8),
                    conf_w_pw2[0, :, hf * 512:(hf + 1) * 512].rearrange("(k p) n -> p k n", p=128))]))
            it2 = [0]

            def pw2_block(tb):
                for hf in range(2):
                    n2 = n2s[hf]
                    w2 = ring.ap(n2).rearrange("p (k n) -> p k n", k=8)
                    for o4 in range(4):
                        oc = hf * 4 + o4
                        bank = nbank()
                        for k in range(NCH):
                            p.op('pe', lambda e, k=k, bank=bank, tb=tb, w2=w2, o4=o4: e.matmul(
                                ps[:, bank, :], w2[:, k, o4 * 128:(o4 + 1) * 128], zs(k)[:, tb * TB:(tb + 1) * TB],
                                start=(k == 0), stop=(k == NCH - 1)),
                                reads=[ring.key(n2), ('zc', k)], writes=[('ps', bank)], inc=(k == NCH - 1))
                        t = tmp2[it2[0] % 2]
                        tk = ('tmp2', it2[0] % 2)
                        it2[0] += 1
                        p.op('dve', lambda e, bank=bank, t=t, oc=oc: e.tensor_scalar(
                            out=t, in0=ps[:, bank, :], scalar1=sm[:, ob2 + oc:ob2 + oc + 1],
                            scalar2=mod[:, l, 16 + oc:17 + oc], op0=ALU.add, op1=ALU.mult),
                            reads=[('ps', bank), 'sm', ('mod', l)], writes=[tk])
                        p.op('dve', lambda e, t=t, oc=oc, tb=tb: e.tensor_tensor(
                            out=xT[:, oc, tb * TB:(tb + 1) * TB], in0=xT[:, oc, tb * TB:(tb + 1) * TB], in1=t,
                            op=ALU.add), reads=[tk, XK(oc, tb)], writes=[XK(oc, tb)])
            for tb in range(NTB):
                if not p.dry and tb >= 1:
                    pw2_block(tb - 1)
                b1 = nbank()
                b2 = nbank()
                for c in range(NCH):
                    p.op('act', lambda e, c=c, tb=tb: e.activation(
                        out=zcb[:, c, :], in_=zc(c)[:, tb * TB:(tb + 1) * TB], func=AF.Copy),
                        reads=[('zc', c)], writes=[('zcb', c)])
                    p.op('act', lambda e, c=c, tb=tb: e.activation(
                        out=sqz[:, c, :], in_=zc(c)[:, tb * TB:(tb + 1) * TB], func=AF.Square),
                        reads=[('zc', c)], writes=[('sqz', c)])
                for c in range(NCH):
                    p.op('pe', lambda e, c=c, b1=b1: e.matmul(ps[:, b1, :], ones[:], zcb[:, c, :],
                                                              start=(c == 0), stop=(c == NCH - 1)),
                         reads=['ones', ('zcb', c)], writes=[('ps', b1)], inc=(c == NCH - 1))
                for c in range(NCH):
                    p.op('pe', lambda e, c=c, b2=b2: e.matmul(ps[:, b2, :], ones[:], sqz[:, c, :],
                                                              start=(c == 0), stop=(c == NCH - 1)),
                         reads=['ones', ('sqz', c)], writes=[('ps', b2)], inc=(c == NCH - 1))
                p.op('dve', lambda e, b1=b1: e.tensor_scalar(out=mean, in0=ps[:, b1, :], scalar1=1.0 / D,
                                                             scalar2=None, op0=ALU.mult),
                     reads=[('ps', b1)], writes=['mean'])
                p.op('dve', lambda e: e.tensor_tensor(out=rstd, in0=mean, in1=mean, op=ALU.mult),
                     reads=['mean'], writes=['rstd'])
                p.op('dve', lambda e, b2=b2: e.scalar_tensor_tensor(
                    out=rstd, in0=ps[:, b2, :], scalar=1.0 / D, in1=rstd, op0=ALU.mult, op1=ALU.subtract),
                    reads=[('ps', b2), 'rstd'], writes=['rstd'])
                p.op('act', lambda e: e.activation(out=rstd, in_=rstd, func=AF.Sqrt, bias=S('eps', 0)),
                     reads=['rstd', 'sm'], writes=['rstd'])
                p.op('dve', lambda e: e.reciprocal(out=rstd, in_=rstd), reads=['rstd'], writes=['rstd'])
                for c in range(NCH):
                    t = t1[c % 2]
                    tk = ('t1', c % 2)
                    p.op('dve', lambda e, c=c, tb=tb, t=t: e.tensor_tensor(
                        out=t, in0=zc(c)[:, tb * TB:(tb + 1) * TB], in1=mean, op=ALU.subtract),
                        reads=[('zc', c), 'mean'], writes=[tk])
                    p.op('dve', lambda e, t=t: e.tensor_tensor(out=t, in0=t, in1=rstd, op=ALU.mult),
                         reads=[tk, 'rstd'], writes=[tk])
                    p.op('act', lambda e, c=c, tb=tb, t=t: e.activation(
                        out=zs(c)[:, tb * TB:(tb + 1) * TB], in_=t, func=AF.Silu,
                        bias=sm[:, olb + c:olb + c + 1], scale=sm[:, olg + c:olg + c + 1]),
                        reads=[tk, 'sm'], writes=[('zc', c)])
            if not p.dry:
                pw2_block(NTB - 1)
            for n2 in n2s:
                ring.release(n2)

        def att_chunks(i):
            if i == 0:
                ds = [0, 1, 2, 3]
            elif i == 1:
                ds = [-1, 0, 1, 2]
            elif i == 14:
                ds = [-2, -1, 0, 1]
            elif i == 15:
                ds = [-3, -2, -1, 0]
            else:
                return [(-2, 7), (-1, 2), (0, 3), (1, 4), (2, 8)]
            return [(d, d + 3) for d in ds]

        def na_mixer(l):
            oT = scr[:, 0:8192].bitcast(BF16).rearrange("p (c n) -> p c n", c=8)
            QTm = scr[:, 8192:10240].bitcast(BF16).rearrange("p (i h q) -> p i h q", i=16, h=2)
            KT = scr[:, 10240:11264].bitcast(BF16)
            Vaug = scr[:, 11264:12304].bitcast(BF16).rearrange("p (t h f) -> p t h f", t=16, h=2)
            Fb = scr[:, 12304:13456].bitcast(BF16).rearrange("p (h s q) -> p h s q", h=2, s=9)
            ctxK = scr[:, 13456:14480].bitcast(BF16).rearrange("p (c n) -> p c n", c=8)
            ctxVaug = scr[:, 14480:15520].bitcast(BF16).rearrange("p (t h f) -> p t h f", t=2, h=16)
            kst = [scr[:, 15520:16032], scr[:, 16032:16544]]
            ctxVtmp = scr[:, 15520:16544].bitcast(BF16).rearrange("p (t f) -> p t f", t=2)
            vst = [scr[:, 16544:16800].rearrange("p (t f) -> p t f", t=2),
                   scr[:, 16800:17056].rearrange("p (t f) -> p t f", t=2)]
            NPT = 8
            PT = [scr[:, 17056 + 128 * a:17056 + 128 * (a + 1)].bitcast(BF16).rearrange("p (h q) -> p h q", h=2)
                  for a in range(NPT)]
            tS = [scr[:, 18080 + 256 * a:18080 + 256 * (a + 1)].rearrange("p (h q) -> p h q", h=2) for a in range(2)]
            otok = [scr[:, 18592:18656].bitcast(BF16)] * 2
            identb = scr[:, 18656:18720].bitcast(BF16)
            rinv = scr[:, 18720:18724]
            ocb, _ = SM_OFF['colb']
            oid, _ = SM_OFF['ident']
            bar = list(p.bar)
            p.dma('pool', [lambda e: e.dma_start(out=ctxK, in_=ctxk_d.rearrange("(c p) n -> p c n", p=128)),
                           lambda e: e.dma_start(out=ctxVtmp, in_=ctxv_d.rearrange("(t p) f -> p t f", p=128))],
                  'CTX', writes=['ctx', ('kst', 0), ('kst', 1)], extra=bar)
            p.op('dve', lambda e: e.tensor_copy(out=identb, in_=sm[:, oid:oid + 128]), reads=['sm'], writes=['identb'])
            p.op('dve', lambda e: e.memset(QTm[64:128, :, 0, :], 0.0), writes=['QT'])
            p.op('dve', lambda e: e.memset(QTm[0:64, :, 1, :], 0.0), writes=['QT'])
            p.op('dve', lambda e: e.memset(Vaug[:, :, :, 64:65], 1.0), writes=['Vc'])
            p.op('dve', lambda e: e.memset(ctxVaug[:, :, :, 64:65], 1.0), writes=['ctxV'])
            p.op('dve', lambda e: e.tensor_copy(
                out=ctxVaug[:, :, :, 0:64], in_=ctxVtmp.rearrange("p t (h f) -> p t h f", h=16)),
                reads=['ctx', ('kst', 0), ('kst', 1)], writes=['ctxV'])
            kv = k_out.rearrange("(c p) t -> p c t", p=128)
            vv = v_out.rearrange("(t p) f -> p t f", p=128)
            cnt = {'ss': 0, 'ks': 0, 'vs': 0}
            tiles = {}
            for c in range(NCH):
                cl = c % 2
                if cl == 0:
                    cp = c // 2
                    nqk = ring.acquire(lambda slot, cp=cp: [
                        (slot[:, 0:4096].rearrange("p (k n) -> p k n", k=8)[:, :, 0:256],
                         na_w_qkv[0, :, cp * 256:(cp + 1) * 256].rearrange("(k p) n -> p k n", p=128)),
                        (slot[:, 0:4096].rearrange("p (k n) -> p k n", k=8)[:, :, 256:512],
                         na_w_qkv[0, :, D + cp * 256:D + (cp + 1) * 256].rearrange("(k p) n -> p k n", p=128))])
                    nv = ring.acquire(lambda slot, cp=cp: [(
                        slot[:, 0:2048].rearrange("p (k n) -> p k n", k=8),
                        na_w_qkv[0, :, 2 * D + cp * 256:2 * D + (cp + 1) * 256].rearrange("(k p) n -> p k n", p=128))])
                    tiles['qk'], tiles['v'] = nqk, nv
                nqk, nv = tiles['qk'], tiles['v']
                if p.dry:
                    if cl == 1:
                        ring.release(nqk)
                        ring.release(nv)
                    continue
                wqk = ring.ap(nqk)[:, 0:4096].rearrange("p (k n) -> p k n", k=8)
                wvt = ring.ap(nv)[:, 0:2048].rearrange("p (k n) -> p k n", k=8)
                qo = cl * 128
                ko = 256 + cl * 128
                p.dma('pool', [lambda e, c=c: e.dma_start(
                    out=scr[:, 12304:13456].bitcast(BF16).rearrange("p (a n) -> p a n", a=2),
                    in_=fbias[c].rearrange("p (a n) -> p a n", a=2))], 'LDF', writes=['F'], extra=bar)
                for tb in range(NTB):
                    bank = tb
                    for k in range(NCH):
                        p.op('pe', lambda e, k=k, bank=bank, tb=tb, wqk=wqk, qo=qo: e.matmul(
                            ps[:, bank, :], wqk[:, k, qo:qo + 128], h[:, k, tb * TB:(tb + 1) * TB],
                            start=(k == 0), stop=(k == NCH - 1)),
                            reads=[ring.key(nqk), HK(k, tb)], writes=[('ps', bank)], inc=(k == NCH - 1))
                    for hh in range(2):
                        p.op('act', lambda e, bank=bank, tb=tb, hh=hh: e.activation(
                            out=QTm[64 * hh:64 * hh + 64, 4 * tb:4 * tb + 4, hh, :],
                            in_=ps[64 * hh:64 * hh + 64, bank, :].rearrange("p (i q) -> p i q", i=4),
                            func=AF.Identity, scale=0.125),
                            reads=[('ps', bank)], writes=['QT'])
                for tb in range(NTB):
                    bank = tb
                    for k in range(NCH):
                        p.op('pe', lambda e, k=k, bank=bank, tb=tb, wqk=wqk, ko=ko: e.matmul(
                            ps[:, bank, :], wqk[:, k, ko:ko + 128], h[:, k, tb * TB:(tb + 1) * TB],
                            start=(k == 0), stop=(k == NCH - 1)),
                            reads=[ring.key(nqk), HK(k, tb)], writes=[('ps', bank)], inc=(k == NCH - 1))
                    ks = cnt['ks'] % 2
                    cnt['ks'] += 1
                    p.op('act', lambda e, bank=bank, ks=ks: e.activation(out=kst[ks], in_=ps[:, bank, :], func=AF.Copy),
                         reads=[('ps', bank)], writes=[('kst', ks)])
                    p.op('dve', lambda e, ks=ks, tb=tb: e.tensor_copy(
                        out=KT[:, tb * TB:(tb + 1) * TB], in_=kst[ks]),
                        reads=[('kst', ks)], writes=['KT'])
                    p.dma('sp', [lambda e, c=c, tb=tb, ks=ks: e.dma_start(
                        out=kv[:, c, tb * TB:(tb + 1) * TB], in_=kst[ks])], 'STK%d' % ks, reads=[('kst', ks)])
                for t8 in range(8):
                    bank = t8 % 4
                    for tt in range(2):
                        tok = (t8 * 2 + tt) * 128
                        for k in range(NCH):
                            p.op('pe', lambda e, k=k, bank=bank, tt=tt, tok=tok, wvt=wvt, qo=qo: e.matmul(
                                ps[:, bank, tt * 128:(tt + 1) * 128], h[:, k, tok:tok + 128], wvt[:, k, qo:qo + 128],
                                start=(k == 0), stop=(k == NCH - 1)),
                                reads=[ring.key(nv), HK(k, tok // TB)], writes=[('ps', bank)], inc=(k == NCH - 1))
                    vs = cnt['vs'] % 2
                    cnt['vs'] += 1
                    p.op('act', lambda e, bank=bank, vs=vs: e.activation(
                        out=vst[vs], in_=ps[:, bank, 0:256].rearrange("p (t f) -> p t f", t=2), func=AF.Copy),
                        reads=[('ps', bank)], writes=[('vst', vs)])
                    p.op('dve', lambda e, vs=vs, t8=t8: e.tensor_copy(
                        out=Vaug[:, t8 * 2:(t8 + 1) * 2, :, 0:64],
                        in_=vst[vs].rearrange("p t (h f) -> p t h f", h=2)),
                        reads=[('vst', vs)], writes=['Vc'])
                    p.dma('sp', [lambda e, c=c, t8=t8, vs=vs, tt=tt: e.dma_start(
                        out=vv[:, t8 * 2 + tt, c * 128:(c + 1) * 128], in_=vst[vs][:, tt, :])
                        for tt in range(2)], 'STV%d' % vs, reads=[('vst', vs)])
                if cl == 1:
                    ring.release(nqk)
                    ring.release(nv)
                items = []
                for i in (range(16) if NA_DEBUG >= 2 else []):
                    chunks = [('loc', i + d, sl, d + 3) for (d, sl) in att_chunks(i)] + \
                             [('ctx', 0, None, 7), ('ctx', 1, None, 7)]
                    for ci, ch in enumerate(chunks):
                        items.append((i, ci, len(chunks), ch))
                LOOK = 7
                base = cnt['ss']
                cnt['ss'] += len(items)

                def emit_S(n, c=c):
                    i, ci, nchk, (kind, j, sl, cbi) = items[n]
                    gi = base + n
                    sb_ = gi % 5
                    psS = ps[:, sb_, 0:256].rearrange("p (h q) -> p h q", h=2)
                    skey = ('ps', sb_)
                    if kind == 'loc':
                        kl = KT[:, j * 128:(j + 1) * 128]
                        rd = ['KT', 'QT']
                    else:
                        kl = ctxK[:, c, j * 128:(j + 1) * 128]
                        rd = ['ctx', 'QT']
                    p.op('pe', lambda e: e.matmul(psS, kl, QTm[:, i, :, :],
                                                  start=True, stop=True), reads=rd, writes=[skey])
                    pt = gi % NPT
                    cb = sm[:, ocb + i * 8 + cbi:ocb + i * 8 + cbi + 1]
                    if kind == 'loc':
                        ts = gi % 2
                        p.op('dve', lambda e: e.tensor_tensor(out=tS[ts], in0=psS, in1=Fb[:, :, sl, :], op=ALU.add),
                             reads=[skey, 'F'], writes=[('tS', ts)])
                        p.op('act', lambda e: e.activation(out=PT[pt], in_=tS[ts], func=AF.Exp, bias=cb),
                             reads=[('tS', ts), 'sm'], writes=[('PT', pt)])
                    else:
                        p.op('act', lambda e: e.activation(out=PT[pt], in_=psS, func=AF.Exp, bias=cb),
                             reads=[skey, 'sm'], writes=[('PT', pt)])

                def emit_PV(n, c=c):
                    i, ci, nchk, (kind, j, sl, cbi) = items[n]
                    gi = base + n
                    pt = gi % NPT
                    for hh in range(2):
                        ob = 5 + hh
                        psO = ps[:, ob, 0:65]
                        if kind == 'loc':
                            vl = Vaug[:, j, hh, :]
                            rdv = 'Vc'
                        else:
                            vl = ctxVaug[:, j, 2 * c + hh, :]
                            rdv = 'ctxV'
                        p.op('pe', lambda e, psO=psO, vl=vl, hh=hh: e.matmul(
                            psO, PT[pt][:, hh, :], vl, start=(ci == 0), stop=(ci == nchk - 1)),
                            reads=[rdv, ('PT', pt)], writes=[('ps', ob)])
                    if ci == nchk - 1:
                        ot = otok[i % 2]
                        tbk = 7
                        psT = ps[:, tbk, 0:64].bitcast(BF16)

                        def tail1():
                            for hh in range(2):
                                ob = 5 + hh
                                psO = ps[:, ob, 0:65]
                                p.op('dve', lambda e, psO=psO, hh=hh: e.reciprocal(out=rinv[:, hh:hh + 1],
                                                                                   in_=psO[:, 64:65]),
                                     reads=[('ps', ob)], writes=[('rinv', hh)])
                                p.op('dve', lambda e, psO=psO, hh=hh: e.tensor_scalar(
                                    out=ot[:, hh * 64:(hh + 1) * 64], in0=psO[:, 0:64], scalar1=rinv[:, hh:hh + 1],
                                    scalar2=None, op0=ALU.mult),
                                    reads=[('ps', ob), ('rinv', hh)], writes=[('otok', 0)])

                        def tail2():
                            p.op('pe', lambda e: e.transpose(out=psT, in_=ot, identity=identb),
                                 reads=[('otok', 0), 'identb'], writes=[('ps', tbk)])

                        def tail3():
                            p.op('dve', lambda e: e.tensor_copy(out=oT[:, c, i * 128:(i + 1) * 128], in_=psT),
                                 reads=[('ps', tbk)], writes=[('oT', c)])
                        tail1()
                        deferred.append((n + LOOK + 2, tail2))
                        deferred.append((n + LOOK + 4, tail3))

                deferred = []
                nsteps = len(items) + LOOK + 6
                for n in range(nsteps):
                    if n < len(items):
                        emit_S(n)
                    if LOOK <= n < len(items) + LOOK:
                        emit_PV(n - LOOK)
                    while deferred and deferred[0][0] <= n:
                        deferred.pop(0)[1]()
                assert not deferred
            for hf in (range(2) if NA_DEBUG >= 3 else []):
                no = ring.acquire(lambda slot, hf=hf: [(
                    slot[:, 0:4096].rearrange("p (k n) -> p k n", k=8),
                    na_w_o[0, :, hf * 512:(hf + 1) * 512].rearrange("(k p) n -> p k n", p=128))])
                if p.dry:
                    ring.release(no)
                    continue
                wo = ring.ap(no).rearrange("p (k n) -> p k n", k=8)
                for tb in range(NTB):
                    for o4 in range(4):
                        oc = hf * 4 + o4
                        bank = (tb * 4 + o4) % 4
                        for k in range(NCH):
                            p.op('pe', lambda e, k=k, bank=bank, tb=tb, wo=wo, o4=o4: e.matmul(
                                ps[:, bank, :], wo[:, k, o4 * 128:(o4 + 1) * 128], oT[:, k, tb * TB:(tb + 1) * TB],
                                start=(k == 0), stop=(k == NCH - 1)),
                                reads=[ring.key(no), ('oT', k)], writes=[('ps', bank)], inc=(k == NCH - 1))
                        p.op('dve', lambda e, oc=oc, bank=bank, tb=tb: e.scalar_tensor_tensor(
                            out=xT[:, oc, tb * TB:(tb + 1) * TB], in0=ps[:, bank, :],
                            scalar=mod[:, l, 16 + oc:17 + oc], in1=xT[:, oc, tb * TB:(tb + 1) * TB],
                            op0=ALU.mult, op1=ALU.add),
                            reads=[('ps', bank), XK(oc, tb), ('mod', l)], writes=[XK(oc, tb)])
                ring.release(no)

        def state_out():
            p.dma('sp', [lambda e: e.dma_start(out=st_out[:, :], in_=stt[:])], 'STS', reads=['stt'])

        def final_out(gain=True):
            sq = scr_bf16(0, 8 * 512).rearrange("p (c n) -> p c n", c=8)
            rstd = scr_f32(2048, 512)
            tmps = [scr_f32(2560, 512), scr_f32(3072, 512)]
            stg = [scr_f32(4096, 512), scr_f32(4608, 512), scr_f32(5120, 512), scr_f32(5632, 512)]
            yv = y_out.rearrange("(c p) t -> p c t", p=128)
            it = 0
            for tb in range(NTB):
                if gain:
                    rms_stats(tb, sq, rstd)
                for c in range(NCH):
                    tmp = tmps[c % 2]
                    tk = ('tmp', c % 2)
                    st = stg[it % 4]
                    sk = ('stg', it % 4)
                    ssem = 'ST%d' % (it % 4)
                    it += 1
                    if gain:
                        p.op('dve', lambda e, c=c, tmp=tmp, tb=tb: e.tensor_tensor(
                            out=tmp, in0=xT[:, c, tb * TB:(tb + 1) * TB], in1=rstd, op=ALU.mult),
                            reads=[XK(c, tb), 'rstd'], writes=[tk])
                        p.op('act', lambda e, c=c, tmp=tmp, st=st: e.activation(
                            out=st, in_=tmp, func=AF.Identity, scale=S('fin_g', c)),
                            reads=[tk, 'sm'], writes=[sk])
                    else:
                        p.op('act', lambda e, c=c, st=st, tb=tb: e.activation(
                            out=st, in_=xT[:, c, tb * TB:(tb + 1) * TB], func=AF.Copy),
                            reads=[XK(c, tb)], writes=[sk])
                    p.dma('sp', [lambda e, c=c, st=st, tb=tb: e.dma_start(
                        out=yv[:, c, tb * TB:(tb + 1) * TB], in_=st)], ssem, reads=[sk])
            if not p.dry:
                fin = [(s_, v) for s_, v in p.cnt.items() if s_.startswith('ST')]
                p._waits('sp', fin)

        def plan():
            bank_ctr[0] = 0
            setup()
            lru_consts()
            if STOP_AFTER == ('setup',):
                final_out(gain=False)
                return
            for l in (range(4) if ONLY_LAYER is None else [ONLY_LAYER]):
                if l == 0 or ONLY_LAYER is not None:
                    modulation(l)
                if STOP_AFTER == ('mod', l):
                    final_out(gain=False)
                    return
                norm_mod(l, 0)
                if STOP_AFTER == ('norm', l):
                    final_out(gain=False)
                    return
                p.barrier()
                if l % 3 == 0:
                    lru_mixer(l, l // 3)
                elif l % 3 == 1:
                    conf_mixer(l)
                else:
                    na_mixer(l)
                p.barrier()
                if STOP_AFTER == ('mix', l):
                    final_out(gain=False)
                    return
                norm_mod(l, 24)
                p.barrier()
                ffn(l)
                p.barrier()
                if STOP_AFTER == ('ffn', l):
                    final_out(gain=False)
                    return
            state_out()
            final_out(gain=True)

        p.dry = True
        plan()
        p.dry = False
        p.reset()
        ring.start_real()
        plan()

        sem_names = sorted(p.cnt.keys())
        sems = {nm: es.enter_context(nc.semaphore(nm)) for nm in sem_names}
        block = es.enter_context(nc.Block())

        def replay(stream):
            def run(e):
                for it in stream:
                    if it[0] == 'wait':
                        e.wait_ge(sems[it[1]], it[2])
                    else:
                        ins = it[1](e)
                        if it[2] is not None:
                            ins.then_inc(sems[it[2]], it[3])
            return run

        block.tensor(replay(p.streams['pe']))
        block.scalar(replay(p.streams['act']))
        block.vector(replay(p.streams['dve']))
        block.gpsimd(replay(p.streams['pool']))
        block.sync(replay(p.streams['sp']))
    return nc


def make_smalls(cond, m, h0, inp, is_sample):
    a = np.zeros((128, SM_N), np.float32)

    def put(name, v):
        o, n = SM_OFF[name]
        v = np.asarray(v, np.float32).reshape(128, -1)
        assert v.shape[1] == n, (name, v.shape, n)
        a[:, o:o + n] = v
    put('cond', _fm(cond))
    put('bmod', np.stack([inp['b_mod'][l].reshape(48, 128).T for l in range(4)], axis=1))
    put('lru_cw', _fm(inp['lru_conv_w']).reshape(128, 2, 4, 8).transpose(0, 1, 3, 2))
    put('lru_cb', _fm(inp['lru_conv_b']))
    put('lru_ba', _fm(inp['lru_b_a']))
    put('lru_bx', _fm(inp['lru_b_x']))
    put('lru_lam', _fm(inp['lru_lambda']))
    put('h0', _fm(h0))
    put('cf_b1', inp['conf_b_pw1'][0].reshape(16, 128).T)
    put('cf_dw', _fm(inp['conf_dw_w'][0]).reshape(128, 31, 8).transpose(0, 2, 1))
    put('cf_db', _fm(inp['conf_dw_b'][0]))
    put('cf_lg', _fm(inp['conf_ln_g'][0]))
    put('cf_lb', _fm(inp['conf_ln_b'][0]))
    put('cf_b2', _fm(inp['conf_b_pw2'][0]))
    put('fin_g', _fm(inp['final_g']))
    put('m', np.full((128, 8), m, np.float32))
    put('colb', build_colb(is_sample))
    put('ident', np.eye(128, dtype=np.float32))
    put('eps', np.full((128, 8), EPS, np.float32))
    put('one', np.ones((128, 8), np.float32))
    put('q25', np.full((128, 8), 0.25, np.float32))
    return a


def build_fbias(rpb, is_sample):
    if not is_sample:
        return np.zeros((8, 128, 2304), np.float32)
    kr = np.arange(128) // 64
    kc = np.arange(128) % 64
    qr = np.arange(128) // 64
    qc = np.arange(128) % 64
    cs = np.clip(qc - 8, 0, 48)
    colmask = (kc[:, None] >= cs[None, :]) & (kc[:, None] < cs[None, :] + 16)
    dcol = np.clip(kc[:, None] - qc[None, :] + 15, 0, 30)
    out = np.full((16, 9, 128, 128), NEG, np.float32)
    for slot in range(9):
        delta = slot - 3 if slot < 7 else (-2 if slot == 7 else 2)
        dr = 2 * delta + kr[:, None] - qr[None, :]
        rowmask = (np.abs(dr) <= 7) if slot < 7 else ((dr >= -4) & (dr <= 3))
        mask = colmask & rowmask
        dri = np.clip(dr + 7, 0, 14)
        vals = rpb[:, dri, dcol]
        out[:, slot] = np.where(mask[None], vals, np.float32(NEG))
    out = out.reshape(8, 2, 9, 128, 128).transpose(0, 3, 1, 2, 4)
    return np.ascontiguousarray(out).reshape(8, 128, 2304)


def build_colb(is_sample):
    cb = np.zeros((16, 8), np.float32)
    if not is_sample:
        cb[:] = NEG
        for i in range(16):
            cb[i, 3] = 0.0
            if i % 2 == 0:
                cb[i, 4] = 0.0
            else:
                cb[i, 2] = 0.0
    return np.broadcast_to(cb.reshape(1, 128), (128, 128)).copy()


_NC_CACHE = {}


def kernel(**inp):
    inp = {k: np.asarray(v) for k, v in inp.items()}
    if 'nc' not in _NC_CACHE:
        _NC_CACHE['nc'] = build_program()
    nc = _NC_CACHE['nc']
    in_maps = []
    fb_s = build_fbias(inp['na_rpb'][0], True)
    fb_p = build_fbias(inp['na_rpb'][0], False)
    for core in range(8):
        if core < 4:
            xs = inp['x_prompt'][core * 8:(core + 1) * 8].reshape(T, D)
            cond = inp['c_ctx']
            m = 0.0
            h0 = np.zeros((2, 2, D), np.float32)
        else:
            b = core - 4
            xs = inp['x_sample'][b]
            cond = inp['c'][b]
            m = 1.0
            h0 = inp['state_lru'][b]
        in_maps.append({
            'xin': np.ascontiguousarray(xs.T),
            'smalls': make_smalls(cond, m, h0, inp, core >= 4),
            'na_w_qkv': inp['na_w_qkv'], 'na_w_o': inp['na_w_o'],
            'fbias': fb_s if core >= 4 else fb_p,
            'ctxk': (np.ascontiguousarray(inp['cache_k'][core - 4, 0].reshape(256, D).T) if core >= 4
                     else np.zeros((D, 256), np.float32)),
            'ctxv': (np.ascontiguousarray(inp['cache_v'][core - 4, 0].reshape(256, D)) if core >= 4
                     else np.zeros((256, D), np.float32)),
            'w_mod': inp['w_mod'], 'w_ff1': inp['w_ff1'], 'w_ff2': inp['w_ff2'],
            'lru_w_in': inp['lru_w_in'], 'lru_w_a': inp['lru_w_a'], 'lru_w_x': inp['lru_w_x'],
            'lru_w_out': inp['lru_w_out'],
            'conf_w_pw1': inp['conf_w_pw1'], 'conf_w_pw2': inp['conf_w_pw2'],
        })
    res = run_bass_kernel_spmd(nc, in_maps, core_ids=list(range(8)))
    outs = [r['y_out'] for r in res.results]
    y_prompt = np.stack([o.T for o in outs[:4]]).reshape(32, 256, D)
    y_sample = np.stack([o.T for o in outs[4:]])
    new_state = np.zeros((32, 2, 2, D), np.float32)
    for core in range(4):
        if 'st_out' not in res.results[core]:
            break
        st = res.results[core]['st_out'].reshape(128, 2, 2, 8, 8)
        st = np.concatenate([st[:, :, 0:1], st[:, :, 1:2, :, ::-1]], axis=2)
        new_state[core * 8:(core + 1) * 8] = st.transpose(4, 1, 2, 3, 0).reshape(8, 2, 2, D)
    new_k = np.zeros((32, 1, 256, 16, 64), np.float32)
    new_v = np.zeros((32, 1, 256, 16, 64), np.float32)
    for core in range(4):
        if 'k_out' not in res.results[core]:
            break
        new_k[core * 8:(core + 1) * 8, 0] = res.results[core]['k_out'].T.reshape(8, 256, 16, 64)
        new_v[core * 8:(core + 1) * 8, 0] = res.results[core]['v_out'].reshape(8, 256, 16, 64)
    return y_prompt, y_sample, new_state, new_k, new_v
```
